# Optimizing a Trainium2 kernel written in Bass

```python
import math
import numpy as np
import jax, jax.numpy as jnp
from jax import lax

D_MODEL = 4096
BATCH = 4
SEQ = 2048
DEPTH = 2

GRID_W = 64
HEAD_DIM = 128
Q_BLOCK = 128
D_MIX = D_MODEL
GROUP_W = D_MIX // 4
NA_HEADS = GROUP_W // HEAD_DIM
NA_WIN_ROWS_MAX = 8
NA_WIN_COLS = 16
AX_HEADS = GROUP_W // HEAD_DIM
AX_KV_HEADS = AX_HEADS // 4
ROPE_DIM_PER_AXIS = HEAD_DIM // 2
ROPE_THETA = 10000.0
SW_HEADS = GROUP_W // HEAD_DIM
SW_KV_HEADS = SW_HEADS // 4
SW_WINDOW = 128
DF_V_DIM = 2 * HEAD_DIM
DF_HEADS = GROUP_W // DF_V_DIM
T5_BUCKETS = 32
T5_MAX_DIST = 128
T5_HEADS = SW_HEADS + DF_HEADS
D_FF = 4 * D_MODEL
EPS = 1e-6
IN_WIDTHS = (
    NA_HEADS * HEAD_DIM, NA_HEADS * HEAD_DIM, NA_HEADS * HEAD_DIM,
    AX_HEADS * HEAD_DIM, AX_KV_HEADS * HEAD_DIM, AX_KV_HEADS * HEAD_DIM,
    SW_HEADS * HEAD_DIM, SW_KV_HEADS * HEAD_DIM, SW_KV_HEADS * HEAD_DIM,
    2 * DF_HEADS * HEAD_DIM, 2 * DF_HEADS * HEAD_DIM, DF_HEADS * DF_V_DIM,
)
D_IN = sum(IN_WIDTHS)

kernel_name = "hybrid_parallel_head_group_encoder"

F32 = jnp.float32


def split_points():
    pts, acc = [], 0
    for w in IN_WIDTHS[:-1]:
        acc += w
        pts.append(acc)
    return pts


def rms_norm(x, g):
    xf = x.astype(F32)
    y = xf * lax.rsqrt(jnp.mean(xf * xf, axis=-1, keepdims=True) + EPS)
    return (y * g.astype(F32)).astype(x.dtype)


def t5_bucket(rel):
    nb = T5_BUCKETS // 2
    max_exact = nb // 2
    base = jnp.where(rel > 0, nb, 0)
    n = jnp.abs(rel)
    n_f = jnp.maximum(n, 1).astype(F32)
    large = max_exact + (jnp.log(n_f / max_exact) / math.log(T5_MAX_DIST / max_exact)
                         * (nb - max_exact)).astype(jnp.int32)
    large = jnp.minimum(large, nb - 1)
    return base + jnp.where(n < max_exact, n, large)


def neighbourhood_attention(q, k, v, rpb):
    B, S, H, Dh = q.shape
    rows = S // GRID_W
    kr = min(NA_WIN_ROWS_MAX, rows)
    kc = NA_WIN_COLS
    nk = kr * kc
    r = np.arange(rows)
    c = np.arange(GRID_W)
    rs = np.clip(r - kr // 2, 0, rows - kr)
    cs = np.clip(c - kc // 2, 0, GRID_W - kc)
    key_r = rs[:, None] + np.arange(kr)
    key_c = cs[:, None] + np.arange(kc)
    idx = (key_r[:, None, :, None] * GRID_W + key_c[None, :, None, :]).reshape(rows, GRID_W, nk)
    dr = np.broadcast_to((key_r - r[:, None])[:, None, :, None], (rows, GRID_W, kr, kc)) + NA_WIN_ROWS_MAX - 1
    dc = np.broadcast_to((key_c - c[:, None])[None, :, None, :], (rows, GRID_W, kr, kc)) + NA_WIN_COLS - 1
    bias = rpb[:, dr, dc].astype(F32).reshape(H, rows, GRID_W, nk).transpose(1, 0, 2, 3)
    q_rows = q.reshape(B, rows, GRID_W, H, Dh).transpose(1, 0, 3, 2, 4)
    scale = Dh ** -0.5

    def row_block(args):
        q_r, idx_r, bias_r = args
        k_g = k[:, idx_r]
        v_g = v[:, idx_r]
        s = jnp.einsum('bhwd,bwnhd->bhwn', q_r, k_g, preferred_element_type=F32) * scale + bias_r
        p = jax.nn.softmax(s, axis=-1)
        return jnp.einsum('bhwn,bwnhd->bwhd', p.astype(v.dtype), v_g)

    o = lax.map(row_block, (q_rows, jnp.asarray(idx, dtype=jnp.int32), bias))
    return o.transpose(1, 0, 2, 3, 4).reshape(B, S, H * Dh)


def axial_rope(S):
    pos = jnp.arange(S, dtype=jnp.int32)
    row = (pos // GRID_W).astype(F32)
    col = (pos % GRID_W).astype(F32)
    n_pairs = ROPE_DIM_PER_AXIS // 2
    inv = ROPE_THETA ** (-jnp.arange(n_pairs, dtype=F32) / n_pairs)
    ang = jnp.concatenate([row[:, None] * inv, col[:, None] * inv], axis=-1)
    return jnp.cos(ang), jnp.sin(ang)


def apply_rope(x, cos, sin):
    xf = x.astype(F32)
    x1, x2 = xf[..., 0::2], xf[..., 1::2]
    c, s = cos[None, :, None, :], sin[None, :, None, :]
    out = jnp.stack([x1 * c - x2 * s, x1 * s + x2 * c], axis=-1).reshape(x.shape)
    return out.astype(x.dtype)


def blocked_gqa(q, k, v):
    B, S, H, Dh = q.shape
    Hkv = k.shape[2]
    G = H // Hkv
    nb = S // Q_BLOCK
    scale = Dh ** -0.5
    qb = q.reshape(B, nb, Q_BLOCK, Hkv, G, Dh).transpose(1, 0, 2, 3, 4, 5)

    def step(qi):
        s = jnp.einsum('bqkgd,bskd->bkgqs', qi, k, preferred_element_type=F32) * scale
        p = jax.nn.softmax(s, axis=-1)
        return jnp.einsum('bkgqs,bskd->bqkgd', p.astype(v.dtype), v)

    o = lax.map(step, qb)
    return o.transpose(1, 0, 2, 3, 4, 5).reshape(B, S, H * Dh)


def sliding_window_gqa(q, k, v, sink, t5_tab):
    B, S, H, Dh = q.shape
    Hkv = k.shape[2]
    G = H // Hkv
    nb = S // Q_BLOCK
    n_side = SW_WINDOW // Q_BLOCK
    span = (2 * n_side + 1) * Q_BLOCK
    pad = n_side * Q_BLOCK
    scale = Dh ** -0.5

    def band(t):
        tp = jnp.pad(t, ((0, 0), (pad, pad), (0, 0), (0, 0)))
        tp = tp.reshape(B, nb + 2 * n_side, Q_BLOCK, Hkv, t.shape[-1])
        return jnp.concatenate([tp[:, j:j + nb] for j in range(2 * n_side + 1)], axis=2)

    kb, vb = band(k), band(v)
    qb = q.reshape(B, nb, Q_BLOCK, Hkv, G, Dh)
    j = jnp.arange(span, dtype=jnp.int32)
    t = jnp.arange(Q_BLOCK, dtype=jnp.int32)
    rel = j[None, :] - pad - t[:, None]
    kpos = jnp.arange(nb, dtype=jnp.int32)[:, None] * Q_BLOCK - pad + j[None, :]
    allowed = (jnp.abs(rel) <= SW_WINDOW)[None] & ((kpos >= 0) & (kpos < S))[:, None, :]
    bias = t5_tab[t5_bucket(rel)].astype(F32).transpose(2, 0, 1).reshape(Hkv, G, 1, Q_BLOCK, span)
    s = jnp.einsum('bnqkgd,bnjkd->bkgnqj', qb, kb, preferred_element_type=F32) * scale + bias
    s = jnp.where(allowed, s, -jnp.inf)
    sink_l = sink.astype(F32).reshape(Hkv, G, 1, 1, 1)
    m = jnp.maximum(jnp.max(s, axis=-1, keepdims=True), sink_l)
    e = jnp.exp(s - m)
    p = e / (jnp.sum(e, axis=-1, keepdims=True) + jnp.exp(sink_l - m))
    o = jnp.einsum('bkgnqj,bnjkd->bnqkgd', p.astype(v.dtype), vb)
    return o.reshape(B, S, H * Dh)


def differential_attention(q1, q2, k1, k2, v, lam, t5_tab):
    B, S, H, Dh = q1.shape
    nb = S // Q_BLOCK
    scale = Dh ** -0.5
    qb = jnp.stack([q1, q2], axis=0).reshape(2, B, nb, Q_BLOCK, H, Dh).transpose(2, 0, 1, 3, 4, 5)
    starts = jnp.arange(nb, dtype=jnp.int32) * Q_BLOCK
    kpos = jnp.arange(S, dtype=jnp.int32)

    def step(args):
        qi, q0 = args
        rel = kpos[None, :] - (q0 + jnp.arange(Q_BLOCK, dtype=jnp.int32))[:, None]
        bias = t5_tab[t5_bucket(rel)].astype(F32).transpose(2, 0, 1)
        s1 = jnp.einsum('bqhd,bshd->bhqs', qi[0], k1, preferred_element_type=F32) * scale + bias
        s2 = jnp.einsum('bqhd,bshd->bhqs', qi[1], k2, preferred_element_type=F32) * scale + bias
        p = jax.nn.softmax(s1, axis=-1) - lam * jax.nn.softmax(s2, axis=-1)
        return jnp.einsum('bhqs,bshe->bqhe', p.astype(v.dtype), v)

    o = lax.map(step, (qb, starts))
    return o.transpose(1, 0, 2, 3, 4).reshape(B, S, H, v.shape[-1])


def setup_inputs(seed: int = 0) -> dict:
    key = jax.random.key(seed)
    ks = jax.random.split(key, 18)

    def nrm(k, shape, scale):
        return jax.random.normal(k, shape, F32) * scale

    def gain(k, shape):
        return 1.0 + 0.05 * jax.random.normal(k, shape, F32)

    return {
        "x": nrm(ks[0], (BATCH, SEQ, D_MODEL), 1.0),
        "ln_attn_pre": gain(ks[1], (DEPTH, D_MODEL)),
        "ln_attn_post": gain(ks[2], (DEPTH, D_MODEL)),
        "ln_mlp_pre": gain(ks[3], (DEPTH, D_MODEL)),
        "ln_mlp_post": gain(ks[4], (DEPTH, D_MODEL)),
        "w_in": nrm(ks[5], (DEPTH, D_MODEL, D_IN), D_MODEL ** -0.5),
        "w_out": nrm(ks[6], (DEPTH, D_MIX, D_MODEL), D_MIX ** -0.5),
        "na_rpb": nrm(ks[7], (DEPTH, NA_HEADS, 2 * NA_WIN_ROWS_MAX - 1, 2 * NA_WIN_COLS - 1), 0.5),
        "ax_q_norm": gain(ks[8], (DEPTH, HEAD_DIM)),
        "ax_k_norm": gain(ks[9], (DEPTH, HEAD_DIM)),
        "sw_sink": nrm(ks[10], (DEPTH, SW_HEADS), 0.5),
        "df_lambda": nrm(ks[11], (DEPTH, 4, HEAD_DIM), 0.1),
        "df_subln": gain(ks[12], (DEPTH, DF_V_DIM)),
        "t5_table": nrm(ks[13], (T5_BUCKETS, T5_HEADS), 0.5),
        "w_mlp_in": nrm(ks[14], (DEPTH, D_MODEL, D_FF), D_MODEL ** -0.5),
        "w_mlp_out": nrm(ks[15], (DEPTH, D_FF, D_MODEL), D_FF ** -0.5),
    }


def reference(x, ln_attn_pre, ln_attn_post, ln_mlp_pre, ln_mlp_post, w_in, w_out, na_rpb,
              ax_q_norm, ax_k_norm, sw_sink, df_lambda, df_subln, t5_table, w_mlp_in, w_mlp_out):
    B, S, _ = x.shape
    cos, sin = axial_rope(S)
    pts = split_points()
    t5_sw = t5_table[:, :SW_HEADS]
    t5_df = t5_table[:, SW_HEADS:]
    for l in range(DEPTH):
        h = rms_norm(x, ln_attn_pre[l])
        proj = jnp.einsum('bsd,de->bse', h, w_in[l])
        aq, ak, av, bq, bk, bv, cq, ck, cv, dq, dk, dv = jnp.split(proj, pts, axis=-1)

        ya = neighbourhood_attention(aq.reshape(B, S, NA_HEADS, HEAD_DIM),
                                     ak.reshape(B, S, NA_HEADS, HEAD_DIM),
                                     av.reshape(B, S, NA_HEADS, HEAD_DIM), na_rpb[l])

        qb_ = apply_rope(rms_norm(bq.reshape(B, S, AX_HEADS, HEAD_DIM), ax_q_norm[l]), cos, sin)
        kb_ = apply_rope(rms_norm(bk.reshape(B, S, AX_KV_HEADS, HEAD_DIM), ax_k_norm[l]), cos, sin)
        yb = blocked_gqa(qb_, kb_, bv.reshape(B, S, AX_KV_HEADS, HEAD_DIM))

        yc = sliding_window_gqa(cq.reshape(B, S, SW_HEADS, HEAD_DIM),
                                ck.reshape(B, S, SW_KV_HEADS, HEAD_DIM),
                                cv.reshape(B, S, SW_KV_HEADS, HEAD_DIM), sw_sink[l], t5_sw)

        lambda_init = 0.8 - 0.6 * math.exp(-0.3 * l)
        lp = df_lambda[l].astype(F32)
        lam = jnp.exp(jnp.sum(lp[0] * lp[1])) - jnp.exp(jnp.sum(lp[2] * lp[3])) + lambda_init
        dq4 = dq.reshape(B, S, 2, DF_HEADS, HEAD_DIM)
        dk4 = dk.reshape(B, S, 2, DF_HEADS, HEAD_DIM)
        od = differential_attention(dq4[:, :, 0], dq4[:, :, 1], dk4[:, :, 0], dk4[:, :, 1],
                                    dv.reshape(B, S, DF_HEADS, DF_V_DIM), lam, t5_df)
        yd = (rms_norm(od, df_subln[l]) * (1.0 - lambda_init)).reshape(B, S, DF_HEADS * DF_V_DIM)

        mix = jnp.concatenate([ya, yb, yc, yd], axis=-1)
        x = x + rms_norm(jnp.einsum('bse,ed->bsd', mix, w_out[l]), ln_attn_post[l])

        h = rms_norm(x, ln_mlp_pre[l])
        u = jnp.square(jax.nn.relu(jnp.einsum('bsd,df->bsf', h, w_mlp_in[l])))
        x = x + rms_norm(jnp.einsum('bsf,fd->bsd', u, w_mlp_out[l]), ln_mlp_post[l])
    return x
```

```python
import math
import os
from contextlib import ExitStack
import numpy as np
import ml_dtypes
import concourse.bass as bass
import concourse.mybir as mybir
from concourse.bass_utils import run_bass_kernel_spmd

F32 = mybir.dt.float32
BF16 = mybir.dt.bfloat16
ALU = mybir.AluOpType
ACTF = mybir.ActivationFunctionType
AX = mybir.AxisListType

D = 4096
KC = 32
NTOK = 1024
NT = 8
S = 2048
KT = 16
DIN = 9216
DFF = 16384
EPS = 1e-6
SCALE = 128 ** -0.5
NEG = -30000.0
PAIRS = [[0, 1], [2, 3], [4, 5], [6, 7]]


class Sched:
    ENG_ATTR = [('pe', 'tensor'), ('act', 'scalar'), ('dve', 'vector'), ('pool', 'gpsimd'), ('sp', 'sync')]

    def __init__(self, nc, es):
        self.nc = nc
        self.engs = [e for e, _ in self.ENG_ATTR]
        self.sem = {e: es.enter_context(nc.semaphore("s_" + e)) for e in self.engs}
        self.cnt = {e: 0 for e in self.engs}
        self.NDS = 40
        self.dsem = [es.enter_context(nc.semaphore("d%d" % i)) for i in range(self.NDS)]
        self.dcnt = [0] * self.NDS
        self.dnext = 0
        self.ops = []
        self.lastw = {}
        self.readers = {}
        self.waited = {e: {} for e in self.engs}

    def _deps(self, eng, reads, writes):
        toks = []
        for k in reads:
            t = self.lastw.get(k)
            if t is not None and not (t[0] == 'e' and t[1] == eng and eng == 'pe'):
                toks.append(t)
        for k in writes:
            t = self.lastw.get(k)
            if t is not None and not (t[0] == 'e' and t[1] == eng):
                toks.append(t)
            for t in self.readers.get(k, ()):
                if not (t[0] == 'e' and t[1] == eng):
                    toks.append(t)
        return toks

    def _commit(self, tok, reads, writes):
        for k in reads:
            self.readers.setdefault(k, []).append(tok)
        for k in writes:
            self.lastw[k] = tok
            self.readers[k] = []

    def op(self, eng, fn, reads=(), writes=()):
        toks = self._deps(eng, reads, writes)
        self.cnt[eng] += 1
        tok = ('e', eng, self.cnt[eng])
        self.ops.append((eng, fn, toks, tok, 1))
        self._commit(tok, reads, writes)
        return tok

    def dma(self, q, fn, reads=(), writes=()):
        toks = self._deps(q, reads, writes)
        i = self.dnext
        self.dnext = (i + 1) % self.NDS
        if self.dcnt[i] > 0:
            toks.append(('d', i, self.dcnt[i]))
        self.dcnt[i] += 16
        tok = ('d', i, self.dcnt[i])
        self.ops.append((q, fn, toks, tok, 16))
        self._commit(tok, reads, writes)
        return tok

    def coll(self, fn, reads=(), writes=()):
        return self.op('pool', fn, reads, writes)

    def barrier(self):
        toks = [('e', e, self.cnt[e]) for e in self.engs if self.cnt[e] > 0]
        toks += [('d', i, self.dcnt[i]) for i in range(self.NDS) if self.dcnt[i] > 0]
        for e in self.engs:
            self.ops.append((e, None, [t for t in toks if not (t[0] == 'e' and t[1] == e)], None, 0))
        self.lastw = {}
        self.readers = {}

    def _wait(self, h, e, t):
        key = (t[0], t[1])
        if self.waited[e].get(key, 0) >= t[2]:
            return
        self.waited[e][key] = t[2]
        sem = self.sem[t[1]] if t[0] == 'e' else self.dsem[t[1]]
        h.wait_ge(sem, t[2])

    def emit(self):
        nc = self.nc
        per = {e: [] for e in self.engs}
        for o in self.ops:
            per[o[0]].append(o)
        with nc.Block() as blk:
            for e, attr in self.ENG_ATTR:
                lst = per[e]
                if not lst:
                    continue

                def body(h, lst=lst, e=e):
                    for (_, fn, toks, tok, inc) in lst:
                        for t in toks:
                            self._wait(h, e, t)
                        if fn is None:
                            continue
                        inst = fn(h)
                        if tok[0] == 'e':
                            inst.then_inc(self.sem[e], 1)
                        else:
                            inst.then_inc(self.dsem[tok[1]], 16)
                getattr(blk, attr)(body)
        self.ops = []


def build_program(nlayers=2, debug=False, stop_after=None, fast=False):
    nc = bass.Bass("TRN2", target_bir_lowering=False)
    es = ExitStack()
    with es:
        _build(nc, es, nlayers, debug, stop_after, fast)
    return nc


def _build(nc, es, nlayers, debug, stop_after, fast=False):
    def din(name, shape, dt=F32):
        return nc.dram_tensor(name, list(shape), dt, kind="ExternalInput")

    dbgkind = "ExternalOutput" if debug else "Internal"

    x_in = din("x", [NTOK, D])
    g_attn_pre = din("ln_attn_pre", [2, D])
    g_attn_post = din("ln_attn_post", [2, D])
    g_mlp_pre = din("ln_mlp_pre", [2, D])
    g_mlp_post = din("ln_mlp_post", [2, D])
    if fast:
        w_in = din("w_in", [1, D, 512])
        w_out = din("w_out", [1, D, 512])
        w1 = din("w_mlp_in", [1, D, 512])
        w2 = din("w_mlp_out", [1, D, 512])
    else:
        w_in = din("w_in", [2, D, DIN])
        w_out = din("w_out", [2, D, D])
        w1 = din("w_mlp_in", [2, D, DFF])
        w2 = din("w_mlp_out", [2, DFF, D])
    qn_g = din("ax_q_norm", [2, 128])
    kn_g = din("ax_k_norm", [2, 128])
    sink_in = din("sw_sink", [2, 8])
    lam_in = din("df_lambda", [2, 512])
    subln_in = din("df_subln", [2, 256])
    if fast:
        biasA = din("biasA", [1, 1, NTOK, S])
        biasC = din("biasC", [1, NTOK, S])
        biasD = din("biasD", [1, NTOK, S])
    else:
        biasA = din("biasA", [2, 8, NTOK, S])
        biasC = din("biasC", [8, NTOK, S])
        biasD = din("biasD", [4, NTOK, S])
    fz = (lambda v: 0) if fast else (lambda v: v)
    cos_in = din("rope_cos", [NTOK, 64])
    sin_in = din("rope_sin", [NTOK, 64])
    ident_in = din("ident", [128, 128], BF16)
    out = nc.dram_tensor("out", [NTOK, D], F32, kind="ExternalOutput")

    xa = nc.dram_tensor("xa", [NTOK, D], F32, kind=dbgkind)
    xb = nc.dram_tensor("xb", [NTOK, D], F32, kind=dbgkind)
    ybuf = nc.dram_tensor("ybuf", [NTOK, D], F32, kind=dbgkind)
    qbuf = nc.dram_tensor("qbuf", [32 * 128, NTOK], BF16, kind=dbgkind)
    kbuf = [nc.dram_tensor("kbuf%d" % j, [512, NTOK], BF16) for j in range(5)]
    kgat = [nc.dram_tensor("kgat%d" % j, [1024, NTOK], BF16) for j in range(5)]
    vbuf = [nc.dram_tensor("vbuf%d" % j, [NTOK, 512], BF16) for j in range(5)]
    vgat = [nc.dram_tensor("vgat%d" % j, [S, 512], BF16) for j in range(5)]
    if debug:
        dbg_mix = nc.dram_tensor("dbg_mix", [128, KC, NTOK], BF16, kind="ExternalOutput")
        dbg_k = nc.dram_tensor("dbg_k", [5, 1024, NTOK], BF16, kind="ExternalOutput")
        dbg_v = nc.dram_tensor("dbg_v", [5, S, 512], BF16, kind="ExternalOutput")

    sc = Sched(nc, es)

    def sb(name, shape, dt):
        return es.enter_context(nc.sbuf_tensor(name, list(shape), dt))

    def ps(name, shape, dt):
        return es.enter_context(nc.psum_tensor(name, list(shape), dt))

    BIG = sb("BIG", [128, KC, NTOK], BF16)
    W = [sb("W0", [128, KC * 512], BF16), sb("W1", [128, KC * 512], BF16)]
    XTF = sb("XTF", [128, D], F32)
    GB = sb("GB", [128, D], F32)
    XN = sb("XN", [128, D], BF16)
    ident = sb("ident_sb", [128, 128], BF16)
    small = sb("small", [128, 64], F32)
    SSP = sb("SSP", [128, NT, 8], F32)
    STG = [sb("STG%d" % i, [128, 512], F32) for i in range(2)]
    STGB = [sb("STGB%d" % i, [128, 512], BF16) for i in range(2)]
    H2 = sb("H2", [128, KC, 256], BF16)
    PS = [ps("PS%d" % i, [128, 512], F32) for i in range(8)]

    wcount = [0]

    def next_w():
        i = wcount[0] % 2
        wcount[0] += 1
        return i

    stg_i = [0]

    def next_stg():
        i = stg_i[0] % 2
        stg_i[0] += 1
        return i

    sm_i = [0]

    def sm():
        i = sm_i[0] % 64
        sm_i[0] += 1
        return small[:, i:i + 1], ('small', i)

    sc.dma('sp', lambda h: h.dma_start(out=ident[:, :], in_=ident_in[:, :]), writes=[('ident',)])

    def load_gain(gt, l):
        sc.dma('sp', lambda h: h.dma_start(out=GB[:, :], in_=gt[l:l + 1, :].partition_broadcast(128)),
               writes=[('GB',)])

    def rstd_from_ss(ss_ap, ss_key, n):
        a, ak = sm()
        sc.op('act', lambda h: h.activation(out=a, in_=ss_ap, func=ACTF.Sqrt, bias=EPS, scale=1.0 / n),
              reads=[ss_key], writes=[ak])
        r, rk = sm()
        sc.op('dve', lambda h: h.reciprocal(out=r, in_=a), reads=[ak], writes=[rk])
        return r, rk

    def prenorm_tile(xsrc_ap, slot, dst_fn, dst_key, psbase):
        xt = XTF
        xn = XN
        sc.dma('sp', lambda h: h.dma_start(out=xt[:, :], in_=xsrc_ap), writes=[('XT',)])
        ss, ssk = sm()
        sc.op('act', lambda h: h.activation(out=xn[:, :], in_=xt[:, :], func=ACTF.Square, accum_out=ss),
              reads=[('XT',)], writes=[('XN',), ssk])
        r, rk = rstd_from_ss(ss, ssk, D)
        sc.op('dve', lambda h: h.scalar_tensor_tensor(out=xn[:, :], in0=xt[:, :], scalar=r, in1=GB[:, :],
                                                      op0=ALU.mult, op1=ALU.mult),
              reads=[('XT',), rk, ('GB',)], writes=[('XN',)])
        for g in range(4):
            pst = PS[psbase + (g % 2)]
            pv = pst[:, :].bitcast(BF16)
            def tr(h, g=g, pv=pv):
                inst = None
                for j in range(8):
                    kc = g * 8 + j
                    inst = h.transpose(out=pv[:, j * 128:(j + 1) * 128], in_=xn[:, kc * 128:(kc + 1) * 128],
                                       identity=ident[:, :])
                return inst
            sc.op('pe', tr, reads=[('XN',), ('ident',)], writes=[('PS', psbase + (g % 2))])
            dst = dst_fn(g * 8)
            eng = 'act' if g % 2 == 0 else 'dve'
            if eng == 'act':
                sc.op('act', lambda h, dst=dst, pv=pv: h.activation(
                    out=dst, in_=pv.rearrange("p (j t) -> p j t", j=8), func=ACTF.Copy),
                    reads=[('PS', psbase + (g % 2))], writes=[dst_key])
            else:
                sc.op('dve', lambda h, dst=dst, pv=pv: h.tensor_copy(
                    out=dst, in_=pv.rearrange("p (j t) -> p j t", j=8)),
                    reads=[('PS', psbase + (g % 2))], writes=[dst_key])

    def load_w(src_ap):
        i = next_w()
        wv = W[i][:, :].rearrange("p (k n) -> p k n", n=512)
        sc.dma('pool', lambda h: h.dma_start(out=wv, in_=src_ap), writes=[('W', i)])
        return i, wv

    def postnorm_tile(tt_rows, xsrc, xdst, slot_y, slot_x, ssp_tt):
        ss, ssk = sm()
        sc.op('dve', lambda h: h.reduce_sum(out=ss, in_=SSP[:, ssp_tt, :], axis=AX.X),
              reads=[('SSP', ssp_tt)], writes=[ssk])
        r, rk = rstd_from_ss(ss, ssk, D)
        for c in range(2):
            cs = slice(c * 2048, (c + 1) * 2048)
            yt = XTF[:, 0:2048]
            xt = XTF[:, 2048:4096]
            sc.dma('sp', lambda h, cs=cs: h.dma_start(out=yt, in_=ybuf[tt_rows, cs]),
                   reads=[('ybuf', tt_rows.start, db) for db in range(8)], writes=[('XTy',)])
            sc.dma('sp', lambda h, cs=cs: h.dma_start(out=xt, in_=xsrc[tt_rows, cs]), writes=[('XTx',)])
            sc.op('dve', lambda h, cs=cs: h.scalar_tensor_tensor(out=yt, in0=yt, scalar=r, in1=GB[:, cs],
                                                             op0=ALU.mult, op1=ALU.mult),
                  reads=[('XTy',), rk, ('GB',)], writes=[('XTy',)])
            sc.op('pool', lambda h: h.tensor_tensor(out=xt, in0=xt, in1=yt, op=ALU.add),
                  reads=[('XTy',), ('XTx',)], writes=[('XTx',)])
            sc.dma('sp', lambda h, cs=cs: h.dma_start(out=xdst[tt_rows, cs], in_=xt),
                   reads=[('XTx',)], writes=[('xdst', tt_rows.start, c)])

    def evac_y(pidx, tt, db, ncols=512):
        si = next_stg()
        junk = STGB[si]
        lvl = int(os.environ.get("K_P3", "9"))
        if lvl < 1:
            return
        sc.op('act', lambda h: h.activation(out=junk[:, :], in_=PS[pidx][:, :], func=ACTF.Square,
                                            accum_out=SSP[:, tt, db:db + 1]),
              reads=[('PS', pidx)], writes=[('STGB', si), ('SSP', tt)])
        if lvl < 2:
            return
        st = STG[si]
        sc.op('dve', lambda h: h.tensor_scalar(out=st[:, :], in0=PS[pidx][:, :], scalar1=1.0, scalar2=None, op0=ALU.mult),
              reads=[('PS', pidx), ('STGB', si)], writes=[('STG', si)])
        if lvl < 3:
            return
        rows = slice(tt * 128, (tt + 1) * 128)
        sc.dma('sp', lambda h: h.dma_start(out=ybuf[rows, db * 512:(db + 1) * 512], in_=st[:, :]),
               reads=[('STG', si)], writes=[('ybuf', rows.start, db)])

    for l in range(nlayers):
        xsrc = x_in if l == 0 else xb
        xfinal = out if l == nlayers - 1 else xb

        load_gain(g_attn_pre, l)
        for tt in range(NT):
            prenorm_tile(xsrc[tt * 128:(tt + 1) * 128, :], tt % 2,
                         lambda kc0, tt=tt: BIG[:, kc0:kc0 + 8, tt * 128:(tt + 1) * 128],
                         ('BIG', tt), psbase=(tt % 2) * 2)
        sc.barrier()
        sc.emit()

        hT = BIG
        allhT = [('BIG', tt) for tt in range(NT)]
        with ExitStack() as pes:
            def psb(name, shape, dt):
                return pes.enter_context(nc.sbuf_tensor("%s_l%d" % (name, l), list(shape), dt))
            xtb = XTF[:, :].bitcast(BF16)
            gbb = GB[:, :].bitcast(BF16)
            BQ8 = xtb.rearrange("p (h t) -> p h t", h=8)
            BK2 = gbb[:, 0:2048].rearrange("p (h t) -> p h t", h=2)
            XQ = GB[:, 1024:1536]
            XQ2 = GB[:, 1536:2048]
            T1 = GB[:, 2048:2304]
            T2 = GB[:, 2304:2560]
            GQ = GB[:, 2560:2688]
            GK = GB[:, 2688:2816]
            COS = GB[:, 2816:3328].rearrange("p (t c) -> p t c", t=NT)
            SIN = GB[:, 3328:3840].rearrange("p (t c) -> p t c", t=NT)
            SS8 = GB[:, 3840:3848]
            SS8b = GB[:, 3848:3856]
            XR = XN[:, 0:512]
            sc.dma('sp', lambda h: h.dma_start(out=GQ, in_=qn_g[l:l + 1, :].partition_broadcast(128)),
                   writes=[('GQ',)])
            sc.dma('sp', lambda h: h.dma_start(out=GK, in_=kn_g[l:l + 1, :].partition_broadcast(128)),
                   writes=[('GK',)])
            sc.dma('sp', lambda h: h.dma_start(out=COS,
                                               in_=cos_in.ap().rearrange("(t p) c -> p t c", p=128)),
                   writes=[('COS',)])
            sc.dma('sp', lambda h: h.dma_start(out=SIN,
                                               in_=sin_in.ap().rearrange("(t p) c -> p t c", p=128)),
                   writes=[('SIN',)])

            acc_i = [0]

            def next_acc():
                i = acc_i[0] % 6
                acc_i[0] += 1
                return i

            def fm_chunk(wi, wv, c, dst_tensor, dst_row):
                for half in range(2):
                    pi = next_acc()
                    def mm(h, pi=pi, half=half):
                        inst = None
                        for kc in range(KC):
                            inst = h.matmul(PS[pi][:, :], lhsT=wv[:, kc, c * 128:(c + 1) * 128],
                                            rhs=hT[:, kc, half * 512:(half + 1) * 512],
                                            start=(kc == 0), stop=(kc == KC - 1))
                        return inst
                    sc.op('pe', mm, reads=[('W', wi)] + allhT, writes=[('PS', pi)])
                    si = next_stg()
                    if half == 0:
                        sc.op('act', lambda h, pi=pi, si=si: h.activation(out=STGB[si][:, :], in_=PS[pi][:, :],
                                                                          func=ACTF.Copy),
                              reads=[('PS', pi)], writes=[('STGB', si)])
                    else:
                        sc.op('dve', lambda h, pi=pi, si=si: h.tensor_copy(out=STGB[si][:, :], in_=PS[pi][:, :]),
                              reads=[('PS', pi)], writes=[('STGB', si)])
                    sc.dma('sp', lambda h, si=si, half=half: h.dma_start(
                        out=dst_tensor[dst_row:dst_row + 128, half * 512:(half + 1) * 512], in_=STGB[si][:, :]),
                        reads=[('STGB', si)], writes=[('fm', id(dst_tensor), dst_row, half)])

            def tm_block(wi, wv, c0, ncols, consume):
                for tt in range(NT):
                    pi = next_acc()
                    def mm(h, pi=pi, tt=tt):
                        inst = None
                        for kc in range(KC):
                            inst = h.matmul(PS[pi][:, 0:ncols], lhsT=hT[:, kc, tt * 128:(tt + 1) * 128],
                                            rhs=wv[:, kc, c0:c0 + ncols],
                                            start=(kc == 0), stop=(kc == KC - 1))
                        return inst
                    sc.op('pe', mm, reads=[('W', wi), ('BIG', tt)], writes=[('PS', pi)])
                    consume(tt, pi)

            def v_consume(vt, col0, ncols):
                def f(tt, pi):
                    si = next_stg()
                    sc.op('act', lambda h: h.activation(out=STGB[si][:, 0:ncols], in_=PS[pi][:, 0:ncols],
                                                        func=ACTF.Copy),
                          reads=[('PS', pi)], writes=[('STGB', si)])
                    sc.dma('sp', lambda h: h.dma_start(out=vt[tt * 128:(tt + 1) * 128, col0:col0 + ncols],
                                                       in_=STGB[si][:, 0:ncols]),
                           reads=[('STGB', si)], writes=[('vout', id(vt), tt, col0)])
                return f

            def rope_consume(nh, gtile, gkey, dst3, dkey, head0):
                W_ = nh * 128
                def f(tt, pi):
                    x3 = XQ[:, 0:W_].rearrange("p (h d) -> p h d", h=nh)
                    sc.op('act', lambda h: h.activation(out=XQ[:, 0:W_], in_=PS[pi][:, 0:W_], func=ACTF.Copy),
                          reads=[('PS', pi)], writes=[('XQ',)])
                    sc.op('dve', lambda h: h.tensor_tensor(out=XQ2[:, 0:W_], in0=XQ[:, 0:W_], in1=XQ[:, 0:W_],
                                                           op=ALU.mult),
                          reads=[('XQ',)], writes=[('XQ2',)])
                    sc.op('dve', lambda h: h.reduce_sum(out=SS8[:, 0:nh],
                                                        in_=XQ2[:, 0:W_].rearrange("p (h d) -> p h d", h=nh),
                                                        axis=AX.X),
                          reads=[('XQ2',)], writes=[('SS8',)])
                    sc.op('act', lambda h: h.activation(out=SS8b[:, 0:nh], in_=SS8[:, 0:nh], func=ACTF.Sqrt,
                                                        bias=EPS, scale=1.0 / 128),
                          reads=[('SS8',)], writes=[('SS8b',)])
                    sc.op('dve', lambda h: h.reciprocal(out=SS8[:, 0:nh], in_=SS8b[:, 0:nh]),
                          reads=[('SS8b',)], writes=[('SS8',)])
                    sc.op('dve', lambda h: h.tensor_tensor(
                        out=x3, in0=x3, in1=SS8[:, 0:nh, None].to_broadcast([128, nh, 128]), op=ALU.mult),
                        reads=[('XQ',), ('SS8',)], writes=[('XQ',)])
                    sc.op('pool', lambda h: h.tensor_tensor(
                        out=x3, in0=x3, in1=gtile[:, None, :].to_broadcast([128, nh, 128]), op=ALU.mult),
                        reads=[('XQ',), gkey], writes=[('XQ',)])
                    x4 = XQ[:, 0:W_].rearrange("p (h i two) -> p h i two", h=nh, two=2)
                    x1 = x4[:, :, :, 0]
                    x2 = x4[:, :, :, 1]
                    cb = COS[:, tt:tt + 1, :].to_broadcast([128, nh, 64])
                    sb_ = SIN[:, tt:tt + 1, :].to_broadcast([128, nh, 64])
                    t1 = T1[:, 0:nh * 64].rearrange("p (h i) -> p h i", h=nh)
                    t2 = T2[:, 0:nh * 64].rearrange("p (h i) -> p h i", h=nh)
                    r4 = XR[:, 0:W_].rearrange("p (h i two) -> p h i two", h=nh, two=2)
                    sc.op('dve', lambda h: h.tensor_tensor(out=t1, in0=x1, in1=cb, op=ALU.mult),
                          reads=[('XQ',), ('COS',)], writes=[('T1',)])
                    sc.op('pool', lambda h: h.tensor_tensor(out=t2, in0=x2, in1=sb_, op=ALU.mult),
                          reads=[('XQ',), ('SIN',)], writes=[('T2',)])
                    sc.op('dve', lambda h: h.tensor_tensor(out=r4[:, :, :, 0], in0=t1, in1=t2, op=ALU.subtract),
                          reads=[('T1',), ('T2',)], writes=[('XR', 0)])
                    sc.op('dve', lambda h: h.tensor_tensor(out=t1, in0=x1, in1=sb_, op=ALU.mult),
                          reads=[('XQ',), ('SIN',), ('XR', 0)], writes=[('T1',)])
                    sc.op('pool', lambda h: h.tensor_tensor(out=t2, in0=x2, in1=cb, op=ALU.mult),
                          reads=[('XQ',), ('COS',), ('XR', 0)], writes=[('T2',)])
                    sc.op('dve', lambda h: h.tensor_tensor(out=r4[:, :, :, 1], in0=t1, in1=t2, op=ALU.add),
                          reads=[('T1',), ('T2',)], writes=[('XR', 1)])
                    pv = PS[6 + (tt % 2)][:, :].bitcast(BF16)
                    def tr(h):
                        inst = None
                        for j in range(nh):
                            inst = h.transpose(out=pv[:, j * 128:(j + 1) * 128], in_=XR[:, j * 128:(j + 1) * 128],
                                               identity=ident[:, :])
                        return inst
                    sc.op('pe', tr, reads=[('XR', 0), ('XR', 1), ('ident',)], writes=[('PS', 6 + (tt % 2))])
                    sc.op('act', lambda h: h.activation(
                        out=dst3[:, head0:head0 + nh, tt * 128:(tt + 1) * 128],
                        in_=pv[:, 0:W_].rearrange("p (j t) -> p j t", j=nh), func=ACTF.Copy),
                        reads=[('PS', 6 + (tt % 2))], writes=[(dkey, head0, tt)])
                return f

            for nb in range(18):
                src = w_in[fz(l)].rearrange("(kc p) n -> p kc n", p=128)[:, :, fz(nb) * 512:(fz(nb) + 1) * 512]
                wi, wv = load_w(src)
                if nb in (0, 1):
                    for c in range(4):
                        fm_chunk(wi, wv, c, qbuf, (4 * nb + c) * 128)
                elif nb in (2, 3):
                    for c in range(4):
                        fm_chunk(wi, wv, c, kbuf[nb - 2], c * 128)
                elif nb in (4, 5):
                    tm_block(wi, wv, 0, 512, v_consume(vbuf[nb - 4], 0, 512))
                elif nb in (6, 7):
                    tm_block(wi, wv, 0, 512, rope_consume(4, GQ, ('GQ',), BQ8, 'BQ8', 4 * (nb - 6)))
                elif nb == 8:
                    tm_block(wi, wv, 0, 256, rope_consume(2, GK, ('GK',), BK2, 'BK2', 0))
                    tm_block(wi, wv, 256, 256, v_consume(vbuf[2], 0, 256))
                elif nb in (9, 10):
                    for c in range(4):
                        fm_chunk(wi, wv, c, qbuf, (16 + 4 * (nb - 9) + c) * 128)
                elif nb == 11:
                    for c in range(2):
                        fm_chunk(wi, wv, c, kbuf[2], 256 + c * 128)
                    tm_block(wi, wv, 256, 256, v_consume(vbuf[2], 256, 256))
                elif nb in (12, 13):
                    for c in range(4):
                        fm_chunk(wi, wv, c, qbuf, (24 + 4 * (nb - 12) + c) * 128)
                elif nb in (14, 15):
                    for c in range(4):
                        fm_chunk(wi, wv, c, kbuf[3 + nb - 14], c * 128)
                else:
                    tm_block(wi, wv, 0, 512, v_consume(vbuf[3 + nb - 16], 0, 512))
            for hq in range(8):
                sc.dma('sp', lambda h, hq=hq: h.dma_start(out=qbuf[(8 + hq) * 128:(9 + hq) * 128, :],
                                                          in_=BQ8[:, hq, :]),
                       reads=[('BQ8', 4 * (hq // 4), tt) for tt in range(NT)], writes=[('qbuf_b', hq)])
            for kv in range(2):
                sc.dma('sp', lambda h, kv=kv: h.dma_start(out=kbuf[2][kv * 128:(kv + 1) * 128, :],
                                                          in_=BK2[:, kv, :]),
                       reads=[('BK2', 0, tt) for tt in range(NT)], writes=[('kbuf_b', kv)])
            sc.barrier()
            for j in range(5):
                sc.coll(lambda h, j=j: h.collective_compute("AllGather", ALU.bypass, replica_groups=PAIRS,
                                                            ins=[kbuf[j].ap().opt()], outs=[kgat[j].ap().opt()]))
                sc.coll(lambda h, j=j: h.collective_compute("AllGather", ALU.bypass, replica_groups=PAIRS,
                                                            ins=[vbuf[j].ap().opt()], outs=[vgat[j].ap().opt()]))
            sc.barrier()
            if debug and l == 0:
                for j in range(5):
                    sc.dma('sp', lambda h, j=j: h.dma_start(out=dbg_k[j], in_=kgat[j][:, :]))
                    sc.dma('sp', lambda h, j=j: h.dma_start(out=dbg_v[j], in_=vgat[j][:, :]))
                sc.barrier()
            sc.emit()
        if stop_after == 'proj':
            break

        mixT = BIG
        with ExitStack() as pes:
            def psb(name, shape, dt):
                return pes.enter_context(nc.sbuf_tensor("%s_l%d" % (name, l), list(shape), dt))
            VT = [W[0][:, 0:8192].rearrange("p (k c) -> p k c", c=512),
                  W[0][:, 8192:16384].rearrange("p (k c) -> p k c", c=512)]
            w1f = W[1][:, :].bitcast(F32)
            BIAS = [w1f[:, 0:2048], w1f[:, 2048:4096]]
            S_SB = [w1f[:, 4096:6144], w1f[:, 6144:8192]]
            xtb = XTF[:, :].bitcast(BF16)
            KTt = [xtb[:, 0:2048], xtb[:, 2048:4096]]
            Pt = [xtb[:, 4096:6144], xtb[:, 6144:8192]]
            gbb = GB[:, :].bitcast(BF16)
            PTt = [gbb[:, 0:2048], gbb[:, 2048:4096]]
            QTt = [gbb[:, 4096:5120], gbb[:, 5120:6144]]
            h2f = H2[:, :, :].rearrange("p k t -> p (k t)").bitcast(F32)
            O1N = h2f[:, 0:256]
            OD = h2f[:, 256:512]
            LAMT = h2f[:, 512:1024]
            LAMP = h2f[:, 1024:1536]
            SUBG = h2f[:, 1536:1792]
            ODB = psb("ODB", [128, 256], BF16)
            ONB = [psb("ONB0", [128, 128], BF16), psb("ONB1", [128, 128], BF16)]
            SINK = psb("SINK", [128, 8], F32)
            LAMS = psb("LAMS", [128, 8], F32)
            lambda_init = 0.8 - 0.6 * math.exp(-0.3 * l)

            sc.dma('sp', lambda h: h.dma_start(out=SUBG, in_=subln_in[l:l + 1, :].partition_broadcast(128)),
                   writes=[('SUBG',)])
            sc.dma('sp', lambda h: h.dma_start(out=SINK[:, :], in_=sink_in[l:l + 1, :].partition_broadcast(128)),
                   writes=[('SINK',)])
            sc.dma('sp', lambda h: h.dma_start(out=LAMT, in_=lam_in[l:l + 1, :].partition_broadcast(128)),
                   writes=[('LAMT',)])
            lt = LAMT.rearrange("p (a d) -> p a d", a=4)
            lp = LAMP[:, 0:256].rearrange("p (a d) -> p a d", a=2)
            sc.op('dve', lambda h: h.tensor_tensor(out=lp[:, 0, :], in0=lt[:, 0, :], in1=lt[:, 1, :], op=ALU.mult),
                  reads=[('LAMT',)], writes=[('LAMP', 0)])
            sc.op('dve', lambda h: h.tensor_tensor(out=lp[:, 1, :], in0=lt[:, 2, :], in1=lt[:, 3, :], op=ALU.mult),
                  reads=[('LAMT',)], writes=[('LAMP', 1)])
            sc.op('dve', lambda h: h.reduce_sum(out=LAMS[:, 0:2], in_=lp, axis=AX.X),
                  reads=[('LAMP', 0), ('LAMP', 1)], writes=[('LAMS', 0)])
            sc.op('act', lambda h: h.activation(out=LAMS[:, 2:4], in_=LAMS[:, 0:2], func=ACTF.Exp),
                  reads=[('LAMS', 0)], writes=[('LAMS', 1)])
            sc.op('dve', lambda h: h.scalar_tensor_tensor(out=LAMS[:, 4:5], in0=LAMS[:, 3:4], scalar=-lambda_init,
                                                          in1=LAMS[:, 2:3], op0=ALU.add, op1=ALU.subtract),
                  reads=[('LAMS', 1)], writes=[('LAMS', 2)])
            neglam = LAMS[:, 4:5]

            S_PS_keys = [('PS', i) for i in range(4)]
            def s_ps(j):
                return PS[j][:, :]
            PT_PS = [PS[4][:, :].bitcast(BF16), PS[5][:, :].bitcast(BF16)]
            O_PS = PS[6]
            OT_PS = PS[7][:, :].bitcast(BF16)

            units = []
            for hh in range(8):
                units.append(dict(q=hh, kj=hh // 4, kr=(hh % 4) * 128, vj=hh // 4, vc=(hh % 4) * 128, vw=128,
                                  bias=('A', hh), sink=None, e=hh, kind='n'))
            for hq in range(8):
                kv = hq // 4
                units.append(dict(q=8 + hq, kj=2, kr=kv * 128, vj=2, vc=kv * 128, vw=128,
                                  bias=None, sink=None, e=8 + hq, kind='n'))
            for hq in range(8):
                kv = hq // 4
                units.append(dict(q=16 + hq, kj=2, kr=256 + kv * 128, vj=2, vc=256 + kv * 128, vw=128,
                                  bias=('C', hq), sink=hq, e=16 + hq, kind='n'))
            for hh in range(4):
                units.append(dict(q=24 + hh, kj=3, kr=hh * 128, vj=3 + hh // 2, vc=(hh % 2) * 256, vw=256,
                                  bias=('D', hh), sink=None, e=24 + 2 * hh, kind='d1'))
                units.append(dict(q=28 + hh, kj=4, kr=hh * 128, vj=3 + hh // 2, vc=(hh % 2) * 256, vw=256,
                                  bias=('D', hh), sink=None, e=24 + 2 * hh, kind='d2'))

            cur_v = [None, None]
            v_rr = [0]
            it = [0]
            for ui, u in enumerate(units):
                ub = ui % 2
                kt = KTt[ub]
                for r in range(2):
                    sc.dma('sp', lambda h, r=r, kt=kt, u=u: h.dma_start(
                        out=kt[:, r * 1024:(r + 1) * 1024],
                        in_=kgat[u['kj']][r * 512 + u['kr']:r * 512 + u['kr'] + 128, :]),
                        writes=[('KT', ub, r)])
                qt = QTt[ub]
                sc.dma('sp', lambda h, qt=qt, u=u: h.dma_start(out=qt, in_=qbuf[u['q'] * 128:(u['q'] + 1) * 128, :]),
                       writes=[('QT', ub)])
                if u['vj'] in cur_v:
                    vb = cur_v.index(u['vj'])
                else:
                    vb = v_rr[0] % 2
                    v_rr[0] += 1
                    cur_v[vb] = u['vj']
                    sc.dma('sp', lambda h, vb=vb, u=u: h.dma_start(
                        out=VT[vb], in_=vgat[u['vj']].ap().rearrange("(k p) c -> p k c", p=128)),
                        writes=[('VT', vb)])
                vt = VT[vb]
                vw = u['vw']
                for qt_i in range(NT):
                    b2 = it[0] % 2
                    it[0] += 1
                    qs = slice(qt_i * 128, (qt_i + 1) * 128)
                    if u['bias'] is not None:
                        kind, hh = u['bias']
                        if kind == 'A':
                            bsrc = biasA[fz(l), fz(hh), qs, :]
                        elif kind == 'C':
                            bsrc = biasC[fz(hh), qs, :]
                        else:
                            bsrc = biasD[fz(hh), qs, :]
                        sc.dma('sp', lambda h, bsrc=bsrc, b2=b2: h.dma_start(out=BIAS[b2], in_=bsrc),
                               writes=[('BIAS', b2)])
                    def qk(h, qt=qt, kt=kt, qs=qs):
                        inst = None
                        for j in range(4):
                            inst = h.matmul(s_ps(j), lhsT=qt[:, qs], rhs=kt[:, j * 512:(j + 1) * 512],
                                            start=True, stop=True)
                        return inst
                    sc.op('pe', qk, reads=[('QT', ub), ('KT', ub, 0), ('KT', ub, 1)], writes=S_PS_keys)
                    m, mk = sm()
                    nm, nmk = sm()
                    rs, rsk = sm()
                    pt_ = Pt[b2]
                    if u['bias'] is not None:
                        ssb = S_SB[b2]
                        for j in range(4):
                            sc.op('dve', lambda h, j=j, ssb=ssb, b2=b2: h.scalar_tensor_tensor(
                                out=ssb[:, j * 512:(j + 1) * 512], in0=s_ps(j), scalar=SCALE,
                                in1=BIAS[b2][:, j * 512:(j + 1) * 512], op0=ALU.mult, op1=ALU.add),
                                reads=[('PS', j), ('BIAS', b2)], writes=[('S_SB', b2, j)])
                        sc.op('dve', lambda h, ssb=ssb, m=m: h.reduce_max(out=m, in_=ssb[:, 0:S], axis=AX.X),
                              reads=[('S_SB', b2, j) for j in range(4)], writes=[mk])
                        if u['sink'] is not None:
                            m2, m2k = sm()
                            sk = SINK[:, u['sink']:u['sink'] + 1]
                            sc.op('dve', lambda h, m=m, m2=m2, sk=sk: h.tensor_tensor(out=m2, in0=m, in1=sk,
                                                                                     op=ALU.max),
                                  reads=[mk, ('SINK',)], writes=[m2k])
                            m, mk = m2, m2k
                        sc.op('dve', lambda h, m=m, nm=nm: h.tensor_scalar(out=nm, in0=m, scalar1=-1.0, scalar2=None,
                                                                         op0=ALU.mult),
                              reads=[mk], writes=[nmk])
                        sc.op('act', lambda h, ssb=ssb, pt_=pt_, nm=nm, rs=rs: h.activation(
                            out=pt_, in_=ssb[:, 0:S], func=ACTF.Exp, bias=nm, scale=1.0, accum_out=rs),
                            reads=[('S_SB', b2, j) for j in range(4)] + [nmk], writes=[('P', b2), rsk])
                    else:
                        mcols = [sm() for _ in range(4)]
                        for j in range(4):
                            sc.op('dve', lambda h, j=j, mc=mcols[j][0]: h.reduce_max(out=mc, in_=s_ps(j), axis=AX.X),
                                  reads=[('PS', j)], writes=[mcols[j][1]])
                        ma, mak = sm()
                        mb, mbk = sm()
                        sc.op('dve', lambda h, ma=ma, a0=mcols[0][0], a1=mcols[1][0]: h.tensor_tensor(out=ma, in0=a0, in1=a1,
                                                                      op=ALU.max),
                              reads=[mcols[0][1], mcols[1][1]], writes=[mak])
                        sc.op('dve', lambda h, mb=mb, a0=mcols[2][0], a1=mcols[3][0]: h.tensor_tensor(out=mb, in0=a0, in1=a1,
                                                                      op=ALU.max),
                              reads=[mcols[2][1], mcols[3][1]], writes=[mbk])
                        sc.op('dve', lambda h, m=m, ma=ma, mb=mb: h.tensor_tensor(out=m, in0=ma, in1=mb, op=ALU.max),
                              reads=[mak, mbk], writes=[mk])
                        sc.op('dve', lambda h, m=m, nm=nm: h.tensor_scalar(out=nm, in0=m, scalar1=-SCALE,
                                                                         scalar2=None, op0=ALU.mult),
                              reads=[mk], writes=[nmk])
                        rparts = [sm() for _ in range(4)]
                        for j in range(4):
                            sc.op('act', lambda h, j=j, pt_=pt_, nm=nm, rp=rparts[j][0]: h.activation(
                                out=pt_[:, j * 512:(j + 1) * 512], in_=s_ps(j), func=ACTF.Exp, bias=nm, scale=SCALE,
                                accum_out=rp),
                                reads=[('PS', j), nmk], writes=[('P', b2, j), rparts[j][1]])
                        ra, rak = sm()
                        rb, rbk = sm()
                        sc.op('dve', lambda h, ra=ra, a0=rparts[0][0], a1=rparts[1][0]: h.tensor_tensor(out=ra, in0=a0, in1=a1,
                                                                      op=ALU.add),
                              reads=[rparts[0][1], rparts[1][1]], writes=[rak])
                        sc.op('dve', lambda h, rb=rb, a0=rparts[2][0], a1=rparts[3][0]: h.tensor_tensor(out=rb, in0=a0, in1=a1,
                                                                      op=ALU.add),
                              reads=[rparts[2][1], rparts[3][1]], writes=[rbk])
                        sc.op('dve', lambda h, rs=rs, ra=ra, rb=rb: h.tensor_tensor(out=rs, in0=ra, in1=rb,
                                                                                    op=ALU.add),
                              reads=[rak, rbk], writes=[rsk])
                    pkeys = [('P', b2)] + [('P', b2, j) for j in range(4)]
                    if u['sink'] is not None:
                        e1, e1k = sm()
                        sk = SINK[:, u['sink']:u['sink'] + 1]
                        sc.op('act', lambda h, e1=e1, sk=sk, nm=nm: h.activation(out=e1, in_=sk, func=ACTF.Exp,
                                                                              bias=nm, scale=1.0),
                              reads=[('SINK',), nmk], writes=[e1k])
                        rs2, rs2k = sm()
                        sc.op('dve', lambda h, rs2=rs2, rs=rs, e1=e1: h.tensor_tensor(out=rs2, in0=rs, in1=e1,
                                                                                      op=ALU.add),
                              reads=[rsk, e1k], writes=[rs2k])
                        rs, rsk = rs2, rs2k
                    ri, rik = sm()
                    sc.op('dve', lambda h, ri=ri, rs=rs: h.reciprocal(out=ri, in_=rs), reads=[rsk], writes=[rik])
                    ptt = PTt[b2]
                    for g in range(2):
                        def tr(h, g=g, pt_=pt_):
                            inst = None
                            for j in range(8):
                                k_ = g * 8 + j
                                inst = h.transpose(out=PT_PS[g][:, j * 128:(j + 1) * 128],
                                                   in_=pt_[:, k_ * 128:(k_ + 1) * 128], identity=ident[:, :])
                            return inst
                        sc.op('pe', tr, reads=pkeys + [('ident',)], writes=[('PS', 4 + g)])
                        if g == 0:
                            sc.op('act', lambda h, g=g, ptt=ptt: h.activation(
                                out=ptt[:, g * 1024:(g + 1) * 1024], in_=PT_PS[g], func=ACTF.Copy),
                                reads=[('PS', 4 + g)], writes=[('PT', b2, g)])
                        else:
                            sc.op('dve', lambda h, g=g, ptt=ptt: h.tensor_copy(
                                out=ptt[:, g * 1024:(g + 1) * 1024], in_=PT_PS[g]),
                                reads=[('PS', 4 + g)], writes=[('PT', b2, g)])
                    def pv(h, ptt=ptt, vt=vt, u=u, vw=vw):
                        inst = None
                        for k_ in range(KT):
                            inst = h.matmul(O_PS[:, 0:vw], lhsT=ptt[:, k_ * 128:(k_ + 1) * 128],
                                            rhs=vt[:, k_, u['vc']:u['vc'] + vw],
                                            start=(k_ == 0), stop=(k_ == KT - 1))
                        return inst
                    sc.op('pe', pv, reads=[('PT', b2, 0), ('PT', b2, 1), ('VT', vb)], writes=[('PS', 6)])
                    if u['kind'] == 'n':
                        onb = ONB[b2]
                        sc.op('dve', lambda h, onb=onb, ri=ri: h.tensor_scalar(out=onb[:, :], in0=O_PS[:, 0:128],
                                                                              scalar1=ri, scalar2=None, op0=ALU.mult),
                              reads=[('PS', 6), rik], writes=[('ONB', b2)])
                        sc.op('pe', lambda h, onb=onb: h.transpose(out=OT_PS[:, 0:128], in_=onb[:, :],
                                                                   identity=ident[:, :]),
                              reads=[('ONB', b2), ('ident',)], writes=[('PS', 7)])
                        sc.op('dve', lambda h, u=u, qs=qs: h.tensor_copy(out=mixT[:, u['e'], qs], in_=OT_PS[:, 0:128]),
                              reads=[('PS', 7)], writes=[('mixT', u['e'], qt_i)])
                    elif u['kind'] == 'd1':
                        sc.op('dve', lambda h, ri=ri: h.tensor_scalar(out=O1N, in0=O_PS[:, 0:256], scalar1=ri,
                                                                      scalar2=None, op0=ALU.mult),
                              reads=[('PS', 6), rik], writes=[('O1N',)])
                        sc.dma('sp', lambda h, qs=qs: h.dma_start(out=ybuf[qs, 0:256], in_=O1N),
                               reads=[('O1N',)], writes=[('yb1', qt_i)])
                    else:
                        sc.dma('sp', lambda h, qs=qs: h.dma_start(out=O1N, in_=ybuf[qs, 0:256]),
                               reads=[('yb1', qt_i)], writes=[('O1N',)])
                        c2, c2k = sm()
                        sc.op('dve', lambda h, c2=c2, ri=ri: h.tensor_tensor(out=c2, in0=ri, in1=neglam, op=ALU.mult),
                              reads=[rik, ('LAMS', 2)], writes=[c2k])
                        sc.op('dve', lambda h, c2=c2: h.scalar_tensor_tensor(out=OD, in0=O_PS[:, 0:256],
                                                                             scalar=c2, in1=O1N,
                                                                             op0=ALU.mult, op1=ALU.add),
                              reads=[('PS', 6), c2k, ('O1N',)], writes=[('OD',)])
                        ss, ssk = sm()
                        sc.op('act', lambda h, ss=ss: h.activation(out=ODB[:, :], in_=OD, func=ACTF.Square,
                                                                   accum_out=ss),
                              reads=[('OD',)], writes=[('ODB',), ssk])
                        a_, ak_ = sm()
                        sc.op('act', lambda h, a_=a_, ss=ss: h.activation(out=a_, in_=ss, func=ACTF.Sqrt, bias=EPS,
                                                                         scale=1.0 / 256),
                              reads=[ssk], writes=[ak_])
                        r_, rk_ = sm()
                        sc.op('dve', lambda h, r_=r_, a_=a_: h.reciprocal(out=r_, in_=a_), reads=[ak_], writes=[rk_])
                        r3, r3k = sm()
                        sc.op('dve', lambda h, r3=r3, r_=r_: h.tensor_scalar(out=r3, in0=r_,
                                                                            scalar1=(1.0 - lambda_init),
                                                                            scalar2=None, op0=ALU.mult),
                              reads=[rk_], writes=[r3k])
                        sc.op('dve', lambda h, r3=r3: h.scalar_tensor_tensor(out=ODB[:, :], in0=OD, scalar=r3,
                                                                             in1=SUBG, op0=ALU.mult,
                                                                             op1=ALU.mult),
                              reads=[('OD',), r3k, ('SUBG',), ('ODB',)], writes=[('ODB',)])
                        def tr2(h):
                            h.transpose(out=OT_PS[:, 0:128], in_=ODB[:, 0:128], identity=ident[:, :])
                            return h.transpose(out=OT_PS[:, 128:256], in_=ODB[:, 128:256], identity=ident[:, :])
                        sc.op('pe', tr2, reads=[('ODB',), ('ident',)], writes=[('PS', 7)])
                        sc.op('dve', lambda h, u=u, qs=qs: h.tensor_copy(
                            out=mixT[:, u['e']:u['e'] + 2, qs],
                            in_=OT_PS[:, 0:256].rearrange("p (j t) -> p j t", j=2)),
                            reads=[('PS', 7)], writes=[('mixT', u['e'], qt_i)])
            sc.barrier()
            if debug and l == 0:
                sc.dma('sp', lambda h: h.dma_start(out=dbg_mix[:, :, :], in_=BIG[:, :, :]))
                sc.barrier()
            sc.emit()
        if stop_after == 'attn':
            break

        for db in range(8):
            src = w_out[fz(l)].rearrange("(kc p) n -> p kc n", p=128)[:, :, fz(db) * 512:(fz(db) + 1) * 512]
            wi, wv = load_w(src)
            for tt in range(NT):
                pi = (db * NT + tt) % 8
                def mm(h, pi=pi, tt=tt, wv=wv):
                    inst = None
                    for kc in range(KC):
                        inst = h.matmul(PS[pi][:, :], lhsT=mixT[:, kc, tt * 128:(tt + 1) * 128], rhs=wv[:, kc, :],
                                        start=(kc == 0), stop=(kc == KC - 1))
                    return inst
                sc.op('pe', mm, reads=[('W', wi), ('BIGALL',)], writes=[('PS', pi)])
                evac_y(pi, tt, db)
        load_gain(g_attn_post, l)
        for tt in range(NT):
            if os.environ.get("K_SKIP_POSTNORM"):
                break
            postnorm_tile(slice(tt * 128, (tt + 1) * 128), xsrc, xa, 0, 1, tt)
        sc.barrier()
        sc.emit()
        if stop_after == 'wout':
            break

        uT = BIG[:, :, :].rearrange("p k t -> p (k t)").rearrange("p (f t) -> p f t", t=256)
        for tb in range(4):
            load_gain(g_mlp_pre, l)
            for i in range(2):
                tt = tb * 2 + i
                prenorm_tile(xa[tt * 128:(tt + 1) * 128, :], i,
                             lambda kc0, i=i: H2[:, kc0:kc0 + 8, i * 128:(i + 1) * 128], ('H2', i), psbase=i * 2)
            h2keys = [('H2', 0), ('H2', 1)]
            for fb in range(32):
                src = w1[fz(l)].rearrange("(kc p) n -> p kc n", p=128)[:, :, fz(fb) * 512:(fz(fb) + 1) * 512]
                wi, wv = load_w(src)
                for c in range(4):
                    fc = fb * 4 + c
                    pi = 4 + (fc % 4)
                    def mm(h, pi=pi, c=c, wv=wv):
                        inst = None
                        for kc in range(KC):
                            inst = h.matmul(PS[pi][:, 0:256], lhsT=wv[:, kc, c * 128:(c + 1) * 128],
                                            rhs=H2[:, kc, :], start=(kc == 0), stop=(kc == KC - 1))
                        return inst
                    sc.op('pe', mm, reads=[('W', wi)] + h2keys, writes=[('PS', pi)])
                    si = next_stg()
                    sc.op('act', lambda h, pi=pi, si=si: h.activation(out=STG[si][:, 0:256], in_=PS[pi][:, 0:256],
                                                                      func=ACTF.Relu),
                          reads=[('PS', pi)], writes=[('STG', si)])
                    eng = 'dve' if fc % 2 == 0 else 'pool'
                    sc.op(eng, lambda h, si=si, fc=fc: h.tensor_tensor(out=uT[:, fc, :], in0=STG[si][:, 0:256],
                                                                       in1=STG[si][:, 0:256], op=ALU.mult),
                          reads=[('STG', si)], writes=[('uT', fc)])
            ukeys = [[('uT', fc) for fc in range(j * 32, (j + 1) * 32)] for j in range(4)]
            for db in range(8):
                for j in range(4):
                    src = w2[fz(l)].rearrange("(fc p) n -> p fc n", p=128)[:, fz(j) * 32:(fz(j) + 1) * 32, fz(db) * 512:(fz(db) + 1) * 512]
                    wi, wv = load_w(src)
                    for i in range(2):
                        pi = (db % 2) * 2 + i
                        def mm(h, pi=pi, i=i, j=j, wv=wv):
                            inst = None
                            for k_ in range(32):
                                fc = j * 32 + k_
                                inst = h.matmul(PS[pi][:, :], lhsT=uT[:, fc, i * 128:(i + 1) * 128], rhs=wv[:, k_, :],
                                                start=(fc == 0), stop=(fc == 127))
                            return inst
                        sc.op('pe', mm, reads=[('W', wi)] + ukeys[j], writes=[('PS', pi)])
                for i in range(2):
                    evac_y((db % 2) * 2 + i, tb * 2 + i, db)
            load_gain(g_mlp_post, l)
            for i in range(2):
                tt = tb * 2 + i
                postnorm_tile(slice(tt * 128, (tt + 1) * 128), xa, xfinal, 0, 1, tt)
            sc.barrier()
            sc.emit()
    sc.barrier()
    sc.emit()


def _t5_bucket_np(rel):
    import jax
    import jax.numpy as jnp
    cpu = jax.devices("cpu")[0]
    with jax.default_device(cpu):
        rel = jnp.asarray(rel, dtype=jnp.int32)
        nb = 16
        max_exact = 8
        base = jnp.where(rel > 0, nb, 0)
        n = jnp.abs(rel)
        n_f = jnp.maximum(n, 1).astype(jnp.float32)
        large = max_exact + (jnp.log(n_f / max_exact) / math.log(128 / max_exact) * (nb - max_exact)).astype(jnp.int32)
        large = jnp.minimum(large, nb - 1)
        return np.asarray(base + jnp.where(n < max_exact, n, large))


_CONST = {}


def _consts():
    if _CONST:
        return _CONST
    rel1d = np.arange(-(S - 1), S, dtype=np.int32)
    bucket1d = _t5_bucket_np(rel1d)
    per_half = []
    for hf in range(2):
        qpos = hf * NTOK + np.arange(NTOK)
        kpos = np.arange(S)
        rel = kpos[None, :] - qpos[:, None]
        bidx = bucket1d[rel + S - 1]
        cmask = np.abs(rel) <= 128
        r = qpos // 64
        c = qpos % 64
        rs = np.clip(r - 4, 0, 32 - 8)
        cs = np.clip(c - 8, 0, 64 - 16)
        kr = kpos // 64
        kc = kpos % 64
        amask = (kr[None, :] >= rs[:, None]) & (kr[None, :] < rs[:, None] + 8) & \
                (kc[None, :] >= cs[:, None]) & (kc[None, :] < cs[:, None] + 16)
        dr = np.clip(kr[None, :] - r[:, None] + 7, 0, 14)
        dc = np.clip(kc[None, :] - c[:, None] + 15, 0, 30)
        row = (qpos // 64).astype(np.float32)
        col = (qpos % 64).astype(np.float32)
        inv = (10000.0 ** (-np.arange(32, dtype=np.float32) / 32)).astype(np.float32)
        ang = np.concatenate([row[:, None] * inv, col[:, None] * inv], axis=-1).astype(np.float32)
        per_half.append(dict(bidx=bidx, cmask=cmask, amask=amask, dr=dr, dc=dc,
                             cos=np.cos(ang).astype(np.float32), sin=np.sin(ang).astype(np.float32)))
    _CONST['h'] = per_half
    _CONST['ident'] = np.eye(128, dtype=np.float32).astype(ml_dtypes.bfloat16)
    return _CONST


def make_in_maps(inputs):
    cst = _consts()
    f = lambda a: np.ascontiguousarray(np.asarray(a, dtype=np.float32))
    x = f(inputs["x"])
    shared = {k: f(inputs[k]) for k in ["ln_attn_pre", "ln_attn_post", "ln_mlp_pre", "ln_mlp_post", "w_in", "w_out",
                                        "w_mlp_in", "w_mlp_out", "ax_q_norm", "ax_k_norm", "sw_sink", "df_subln"]}
    shared["df_lambda"] = f(inputs["df_lambda"]).reshape(2, 512)
    shared["ident"] = cst['ident']
    rpb = f(inputs["na_rpb"])
    t5 = f(inputs["t5_table"])
    halves = []
    for hf in range(2):
        c = cst['h'][hf]
        bA = np.empty((2, 8, NTOK, S), np.float32)
        for l in range(2):
            for h in range(8):
                bA[l, h] = np.where(c['amask'], rpb[l, h][c['dr'], c['dc']], np.float32(NEG))
        bC = np.empty((8, NTOK, S), np.float32)
        for h in range(8):
            bC[h] = np.where(c['cmask'], t5[:, h][c['bidx']], np.float32(NEG))
        bD = np.empty((4, NTOK, S), np.float32)
        for h in range(4):
            bD[h] = t5[:, 8 + h][c['bidx']]
        halves.append(dict(biasA=bA, biasC=bC, biasD=bD, rope_cos=c['cos'], rope_sin=c['sin']))
    in_maps = []
    for core in range(8):
        b, hf = core // 2, core % 2
        m = dict(shared)
        m.update(halves[hf])
        m["x"] = np.ascontiguousarray(x[b, hf * NTOK:(hf + 1) * NTOK, :])
        in_maps.append(m)
    return in_maps


_NC = {}


def kernel(**inputs):
    if 'nc' not in _NC:
        _NC['nc'] = build_program()
    nc = _NC['nc']
    in_maps = make_in_maps(inputs)
    res = run_bass_kernel_spmd(nc, in_maps, core_ids=list(range(8)))
    outp = np.empty((4, S, D), np.float32)
    for core in range(8):
        b, hf = core // 2, core % 2
        outp[b, hf * NTOK:(hf + 1) * NTOK, :] = res.results[core]["out"]
    return outp
```

```python
import math
import os
from contextlib import ExitStack
import numpy as np
import ml_dtypes
import concourse.bass as bass
import concourse.mybir as mybir
from concourse.bass_utils import run_bass_kernel_spmd

F32 = mybir.dt.float32
BF16 = mybir.dt.bfloat16
ALU = mybir.AluOpType
ACTF = mybir.ActivationFunctionType
AX = mybir.AxisListType

D = 4096
KC = 32
NTOK = 1024
NT = 8
S = 2048
KT = 16
DIN = 9216
DFF = 16384
EPS = 1e-6
SCALE = 128 ** -0.5
NEG = -30000.0
PAIRS = [[0, 1], [2, 3], [4, 5], [6, 7]]


class Sched:
    ENG_ATTR = [('pe', 'tensor'), ('act', 'scalar'), ('dve', 'vector'), ('pool', 'gpsimd'), ('sp', 'sync')]

    def __init__(self, nc, es):
        self.nc = nc
        self.engs = [e for e, _ in self.ENG_ATTR]
        self.sem = {e: es.enter_context(nc.semaphore("s_" + e)) for e in self.engs}
        self.cnt = {e: 0 for e in self.engs}
        self.NDS = 40
        self.dsem = [es.enter_context(nc.semaphore("d%d" % i)) for i in range(self.NDS)]
        self.dcnt = [0] * self.NDS
        self.dnext = 0
        self.ops = []
        self.lastw = {}
        self.readers = {}
        self.waited = {e: {} for e in self.engs}

    def _deps(self, eng, reads, writes):
        toks = []
        for k in reads:
            t = self.lastw.get(k)
            if t is not None and not (t[0] == 'e' and t[1] == eng and eng == 'pe'):
                toks.append(t)
        for k in writes:
            t = self.lastw.get(k)
            if t is not None and not (t[0] == 'e' and t[1] == eng):
                toks.append(t)
            for t in self.readers.get(k, ()):
                if not (t[0] == 'e' and t[1] == eng):
                    toks.append(t)
        return toks

    def _commit(self, tok, reads, writes):
        for k in reads:
            self.readers.setdefault(k, []).append(tok)
        for k in writes:
            self.lastw[k] = tok
            self.readers[k] = []

    def op(self, eng, fn, reads=(), writes=()):
        toks = self._deps(eng, reads, writes)
        self.cnt[eng] += 1
        tok = ('e', eng, self.cnt[eng])
        self.ops.append((eng, fn, toks, tok, 1))
        self._commit(tok, reads, writes)
        return tok

    def dma(self, q, fn, reads=(), writes=()):
        toks = self._deps(q, reads, writes)
        i = self.dnext
        self.dnext = (i + 1) % self.NDS
        if self.dcnt[i] > 0:
            toks.append(('d', i, self.dcnt[i]))
        self.dcnt[i] += 16
        tok = ('d', i, self.dcnt[i])
        self.ops.append((q, fn, toks, tok, 16))
        self._commit(tok, reads, writes)
        return tok

    def coll(self, fn, reads=(), writes=()):
        return self.op('pool', fn, reads, writes)

    def barrier(self):
        toks = [('e', e, self.cnt[e]) for e in self.engs if self.cnt[e] > 0]
        toks += [('d', i, self.dcnt[i]) for i in range(self.NDS) if self.dcnt[i] > 0]
        for e in self.engs:
            self.ops.append((e, None, [t for t in toks if not (t[0] == 'e' and t[1] == e)], None, 0))
        self.lastw = {}
        self.readers = {}

    def _wait(self, h, e, t):
        key = (t[0], t[1])
        if self.waited[e].get(key, 0) >= t[2]:
            return
        self.waited[e][key] = t[2]
        sem = self.sem[t[1]] if t[0] == 'e' else self.dsem[t[1]]
        h.wait_ge(sem, t[2])

    def emit(self):
        nc = self.nc
        per = {e: [] for e in self.engs}
        for o in self.ops:
            per[o[0]].append(o)
        with nc.Block() as blk:
            for e, attr in self.ENG_ATTR:
                lst = per[e]
                if not lst:
                    continue

                def body(h, lst=lst, e=e):
                    for (_, fn, toks, tok, inc) in lst:
                        for t in toks:
                            self._wait(h, e, t)
                        if fn is None:
                            continue
                        inst = fn(h)
                        if tok[0] == 'e':
                            inst.then_inc(self.sem[e], 1)
                        else:
                            inst.then_inc(self.dsem[tok[1]], 16)
                getattr(blk, attr)(body)
        self.ops = []


def build_program(nlayers=2, debug=False, stop_after=None, fast=False):
    nc = bass.Bass("TRN2", target_bir_lowering=False)
    es = ExitStack()
    with es:
        _build(nc, es, nlayers, debug, stop_after, fast)
    return nc


def _build(nc, es, nlayers, debug, stop_after, fast=False):
    def din(name, shape, dt=F32):
        return nc.dram_tensor(name, list(shape), dt, kind="ExternalInput")

    dbgkind = "ExternalOutput" if debug else "Internal"

    x_in = din("x", [NTOK, D])
    g_attn_pre = din("ln_attn_pre", [2, D])
    g_attn_post = din("ln_attn_post", [2, D])
    g_mlp_pre = din("ln_mlp_pre", [2, D])
    g_mlp_post = din("ln_mlp_post", [2, D])
    if fast:
        w_in = din("w_in", [1, D, 512])
        w_out = din("w_out", [1, D, 512])
        w1 = din("w_mlp_in", [1, D, 512])
        w2 = din("w_mlp_out", [1, D, 512])
    else:
        w_in = din("w_in", [2, D, DIN])
        w_out = din("w_out", [2, D, D])
        w1 = din("w_mlp_in", [2, D, DFF])
        w2 = din("w_mlp_out", [2, DFF, D])
    qn_g = din("ax_q_norm", [2, 128])
    kn_g = din("ax_k_norm", [2, 128])
    sink_in = din("sw_sink", [2, 8])
    lam_in = din("df_lambda", [2, 512])
    subln_in = din("df_subln", [2, 256])
    if fast:
        biasA = din("biasA", [1, 1, NTOK, S])
        biasC = din("biasC", [1, NTOK, S])
        biasD = din("biasD", [1, NTOK, S])
    else:
        biasA = din("biasA", [2, 8, NTOK, S])
        biasC = din("biasC", [8, NTOK, S])
        biasD = din("biasD", [4, NTOK, S])
    fz = (lambda v: 0) if fast else (lambda v: v)
    cos_in = din("rope_cos", [NTOK, 64])
    sin_in = din("rope_sin", [NTOK, 64])
    ident_in = din("ident", [128, 128], BF16)
    out = nc.dram_tensor("out", [NTOK, D], F32, kind="ExternalOutput")

    xa = nc.dram_tensor("xa", [NTOK, D], F32, kind=dbgkind)
    xb = nc.dram_tensor("xb", [NTOK, D], F32, kind=dbgkind)
    ybuf = nc.dram_tensor("ybuf", [NTOK, D], F32, kind=dbgkind)
    qbuf = nc.dram_tensor("qbuf", [32 * 128, NTOK], BF16, kind=dbgkind)
    kbuf = [nc.dram_tensor("kbuf%d" % j, [512, NTOK], BF16) for j in range(5)]
    kgat = [nc.dram_tensor("kgat%d" % j, [1024, NTOK], BF16) for j in range(5)]
    vbuf = [nc.dram_tensor("vbuf%d" % j, [NTOK, 512], BF16) for j in range(5)]
    vgat = [nc.dram_tensor("vgat%d" % j, [S, 512], BF16) for j in range(5)]
    if debug:
        dbg_mix = nc.dram_tensor("dbg_mix", [128, KC, NTOK], BF16, kind="ExternalOutput")
        dbg_k = nc.dram_tensor("dbg_k", [5, 1024, NTOK], BF16, kind="ExternalOutput")
        dbg_v = nc.dram_tensor("dbg_v", [5, S, 512], BF16, kind="ExternalOutput")

    sc = Sched(nc, es)

    def sb(name, shape, dt):
        return es.enter_context(nc.sbuf_tensor(name, list(shape), dt))

    def ps(name, shape, dt):
        return es.enter_context(nc.psum_tensor(name, list(shape), dt))

    BIG = sb("BIG", [128, KC, NTOK], BF16)
    W = [sb("W0", [128, KC * 512], BF16), sb("W1", [128, KC * 512], BF16)]
    XTF = sb("XTF", [128, D], F32)
    GB = sb("GB", [128, D], F32)
    XN = sb("XN", [128, D], BF16)
    ident = sb("ident_sb", [128, 128], BF16)
    small = sb("small", [128, 64], F32)
    SSP = sb("SSP", [128, NT, 8], F32)
    STG = [sb("STG%d" % i, [128, 512], F32) for i in range(2)]
    STGB = [sb("STGB%d" % i, [128, 512], BF16) for i in range(2)]
    H2 = sb("H2", [128, KC, 256], BF16)
    PS = [ps("PS%d" % i, [128, 512], F32) for i in range(8)]

    auxq = ['sp']
    wcount = [0]

    def next_w():
        i = wcount[0] % 2
        wcount[0] += 1
        return i

    stg_i = [0]

    def next_stg():
        i = stg_i[0] % 2
        stg_i[0] += 1
        return i

    sm_i = [0]

    def sm():
        i = sm_i[0] % 64
        sm_i[0] += 1
        return small[:, i:i + 1], ('small', i)

    sc.dma('sp', lambda h: h.dma_start(out=ident[:, :], in_=ident_in[:, :]), writes=[('ident',)])

    def load_gain(gt, l):
        sc.dma(auxq[0], lambda h: h.dma_start(out=GB[:, :], in_=gt[l:l + 1, :].partition_broadcast(128)),
               writes=[('GB',)])

    def rstd_from_ss(ss_ap, ss_key, n):
        a, ak = sm()
        sc.op('act', lambda h: h.activation(out=a, in_=ss_ap, func=ACTF.Sqrt, bias=EPS, scale=1.0 / n),
              reads=[ss_key], writes=[ak])
        r, rk = sm()
        sc.op('dve', lambda h: h.reciprocal(out=r, in_=a), reads=[ak], writes=[rk])
        return r, rk

    def prenorm_tile(xsrc_ap, slot, dst_fn, dst_key, psbase):
        xt = XTF
        xn = XN
        sc.dma(auxq[0], lambda h: h.dma_start(out=xt[:, :], in_=xsrc_ap), writes=[('XT',)])
        ss, ssk = sm()
        sc.op('act', lambda h: h.activation(out=xn[:, :], in_=xt[:, :], func=ACTF.Square, accum_out=ss),
              reads=[('XT',)], writes=[('XN',), ssk])
        r, rk = rstd_from_ss(ss, ssk, D)
        sc.op('dve', lambda h: h.scalar_tensor_tensor(out=xn[:, :], in0=xt[:, :], scalar=r, in1=GB[:, :],
                                                      op0=ALU.mult, op1=ALU.mult),
              reads=[('XT',), rk, ('GB',)], writes=[('XN',)])
        for g in range(4):
            pst = PS[psbase + (g % 2)]
            pv = pst[:, :].bitcast(BF16)
            def tr(h, g=g, pv=pv):
                inst = None
                for j in range(8):
                    kc = g * 8 + j
                    inst = h.transpose(out=pv[:, j * 128:(j + 1) * 128], in_=xn[:, kc * 128:(kc + 1) * 128],
                                       identity=ident[:, :])
                return inst
            sc.op('pe', tr, reads=[('XN',), ('ident',)], writes=[('PS', psbase + (g % 2))])
            dst = dst_fn(g * 8)
            eng = 'act' if g % 2 == 0 else 'dve'
            if eng == 'act':
                sc.op('act', lambda h, dst=dst, pv=pv: h.activation(
                    out=dst, in_=pv.rearrange("p (j t) -> p j t", j=8), func=ACTF.Copy),
                    reads=[('PS', psbase + (g % 2))], writes=[dst_key])
            else:
                sc.op('dve', lambda h, dst=dst, pv=pv: h.tensor_copy(
                    out=dst, in_=pv.rearrange("p (j t) -> p j t", j=8)),
                    reads=[('PS', psbase + (g % 2))], writes=[dst_key])

    def load_w(src_ap):
        i = next_w()
        wv = W[i][:, :].rearrange("p (k n) -> p k n", n=512)
        sc.dma('pool', lambda h: h.dma_start(out=wv, in_=src_ap), writes=[('W', i)])
        return i, wv

    def postnorm_tile(tt_rows, xsrc, xdst, slot_y, slot_x, ssp_tt):
        ss, ssk = sm()
        sc.op('dve', lambda h: h.reduce_sum(out=ss, in_=SSP[:, ssp_tt, :], axis=AX.X),
              reads=[('SSP', ssp_tt)], writes=[ssk])
        r, rk = rstd_from_ss(ss, ssk, D)
        for c in range(2):
            cs = slice(c * 2048, (c + 1) * 2048)
            yt = XTF[:, 0:2048]
            xt = XTF[:, 2048:4096]
            sc.dma(auxq[0], lambda h, cs=cs: h.dma_start(out=yt, in_=ybuf[tt_rows, cs]),
                   reads=[('ybuf', tt_rows.start, db) for db in range(8)], writes=[('XTy',)])
            sc.dma(auxq[0], lambda h, cs=cs: h.dma_start(out=xt, in_=xsrc[tt_rows, cs]), writes=[('XTx',)])
            sc.op('dve', lambda h, cs=cs: h.scalar_tensor_tensor(out=yt, in0=yt, scalar=r, in1=GB[:, cs],
                                                             op0=ALU.mult, op1=ALU.mult),
                  reads=[('XTy',), rk, ('GB',)], writes=[('XTy',)])
            sc.op('pool', lambda h: h.tensor_tensor(out=xt, in0=xt, in1=yt, op=ALU.add),
                  reads=[('XTy',), ('XTx',)], writes=[('XTx',)])
            sc.dma(auxq[0], lambda h, cs=cs: h.dma_start(out=xdst[tt_rows, cs], in_=xt),
                   reads=[('XTx',)], writes=[('xdst', tt_rows.start, c)])

    def evac_y(pidx, tt, db, ncols=512):
        si = next_stg()
        junk = STGB[si]
        lvl = int(os.environ.get("K_P3", "9"))
        if lvl < 1:
            return
        sc.op('act', lambda h: h.activation(out=junk[:, :], in_=PS[pidx][:, :], func=ACTF.Square,
                                            accum_out=SSP[:, tt, db:db + 1]),
              reads=[('PS', pidx)], writes=[('STGB', si), ('SSP', tt)])
        if lvl < 2:
            return
        st = STG[si]
        sc.op('dve', lambda h: h.tensor_scalar(out=st[:, :], in0=PS[pidx][:, :], scalar1=1.0, scalar2=None, op0=ALU.mult),
              reads=[('PS', pidx), ('STGB', si)], writes=[('STG', si)])
        if lvl < 3:
            return
        rows = slice(tt * 128, (tt + 1) * 128)
        sc.dma(auxq[0], lambda h: h.dma_start(out=ybuf[rows, db * 512:(db + 1) * 512], in_=st[:, :]),
               reads=[('STG', si)], writes=[('ybuf', rows.start, db)])

    for l in range(nlayers):
        xsrc = x_in if l == 0 else xb
        xfinal = out if l == nlayers - 1 else xb

        load_gain(g_attn_pre, l)
        for tt in range(NT):
            prenorm_tile(xsrc[tt * 128:(tt + 1) * 128, :], tt % 2,
                         lambda kc0, tt=tt: BIG[:, kc0:kc0 + 8, tt * 128:(tt + 1) * 128],
                         ('BIG', tt), psbase=(tt % 2) * 2)
        sc.barrier()
        sc.emit()

        hT = BIG
        allhT = [('BIG', tt) for tt in range(NT)]
        with ExitStack() as pes:
            def psb(name, shape, dt):
                return pes.enter_context(nc.sbuf_tensor("%s_l%d" % (name, l), list(shape), dt))
            xtb = XTF[:, :].bitcast(BF16)
            gbb = GB[:, :].bitcast(BF16)
            BQ8 = xtb.rearrange("p (h t) -> p h t", h=8)
            BK2 = gbb[:, 0:2048].rearrange("p (h t) -> p h t", h=2)
            XQ = GB[:, 1024:1536]
            XQ2 = GB[:, 1536:2048]
            T1 = GB[:, 2048:2304]
            T2 = GB[:, 2304:2560]
            GQ = GB[:, 2560:2688]
            GK = GB[:, 2688:2816]
            COS = GB[:, 2816:3328].rearrange("p (t c) -> p t c", t=NT)
            SIN = GB[:, 3328:3840].rearrange("p (t c) -> p t c", t=NT)
            SS8 = GB[:, 3840:3848]
            SS8b = GB[:, 3848:3856]
            XR = XN[:, 0:512]
            sc.dma('sp', lambda h: h.dma_start(out=GQ, in_=qn_g[l:l + 1, :].partition_broadcast(128)),
                   writes=[('GQ',)])
            sc.dma('sp', lambda h: h.dma_start(out=GK, in_=kn_g[l:l + 1, :].partition_broadcast(128)),
                   writes=[('GK',)])
            sc.dma('sp', lambda h: h.dma_start(out=COS,
                                               in_=cos_in.ap().rearrange("(t p) c -> p t c", p=128)),
                   writes=[('COS',)])
            sc.dma('sp', lambda h: h.dma_start(out=SIN,
                                               in_=sin_in.ap().rearrange("(t p) c -> p t c", p=128)),
                   writes=[('SIN',)])

            acc_i = [0]

            def next_acc():
                i = acc_i[0] % 6
                acc_i[0] += 1
                return i

            def fm_chunk(wi, wv, c, dst_tensor, dst_row):
                for half in range(2):
                    pi = next_acc()
                    def mm(h, pi=pi, half=half):
                        inst = None
                        for kc in range(KC):
                            inst = h.matmul(PS[pi][:, :], lhsT=wv[:, kc, c * 128:(c + 1) * 128],
                                            rhs=hT[:, kc, half * 512:(half + 1) * 512],
                                            start=(kc == 0), stop=(kc == KC - 1))
                        return inst
                    sc.op('pe', mm, reads=[('W', wi)] + allhT, writes=[('PS', pi)])
                    si = next_stg()
                    if half == 0:
                        sc.op('act', lambda h, pi=pi, si=si: h.activation(out=STGB[si][:, :], in_=PS[pi][:, :],
                                                                          func=ACTF.Copy),
                              reads=[('PS', pi)], writes=[('STGB', si)])
                    else:
                        sc.op('dve', lambda h, pi=pi, si=si: h.tensor_copy(out=STGB[si][:, :], in_=PS[pi][:, :]),
                              reads=[('PS', pi)], writes=[('STGB', si)])
                    sc.dma('sp', lambda h, si=si, half=half: h.dma_start(
                        out=dst_tensor[dst_row:dst_row + 128, half * 512:(half + 1) * 512], in_=STGB[si][:, :]),
                        reads=[('STGB', si)], writes=[('fm', id(dst_tensor), dst_row, half)])

            def tm_block(wi, wv, c0, ncols, consume):
                for tt in range(NT):
                    pi = next_acc()
                    def mm(h, pi=pi, tt=tt):
                        inst = None
                        for kc in range(KC):
                            inst = h.matmul(PS[pi][:, 0:ncols], lhsT=hT[:, kc, tt * 128:(tt + 1) * 128],
                                            rhs=wv[:, kc, c0:c0 + ncols],
                                            start=(kc == 0), stop=(kc == KC - 1))
                        return inst
                    sc.op('pe', mm, reads=[('W', wi), ('BIG', tt)], writes=[('PS', pi)])
                    consume(tt, pi)

            def v_consume(vt, col0, ncols):
                def f(tt, pi):
                    si = next_stg()
                    sc.op('act', lambda h: h.activation(out=STGB[si][:, 0:ncols], in_=PS[pi][:, 0:ncols],
                                                        func=ACTF.Copy),
                          reads=[('PS', pi)], writes=[('STGB', si)])
                    sc.dma('sp', lambda h: h.dma_start(out=vt[tt * 128:(tt + 1) * 128, col0:col0 + ncols],
                                                       in_=STGB[si][:, 0:ncols]),
                           reads=[('STGB', si)], writes=[('vout', id(vt), tt, col0)])
                return f

            def rope_consume(nh, gtile, gkey, dst3, dkey, head0):
                W_ = nh * 128
                def f(tt, pi):
                    x3 = XQ[:, 0:W_].rearrange("p (h d) -> p h d", h=nh)
                    sc.op('act', lambda h: h.activation(out=XQ[:, 0:W_], in_=PS[pi][:, 0:W_], func=ACTF.Copy),
                          reads=[('PS', pi)], writes=[('XQ',)])
                    sc.op('dve', lambda h: h.tensor_tensor(out=XQ2[:, 0:W_], in0=XQ[:, 0:W_], in1=XQ[:, 0:W_],
                                                           op=ALU.mult),
                          reads=[('XQ',)], writes=[('XQ2',)])
                    sc.op('dve', lambda h: h.reduce_sum(out=SS8[:, 0:nh],
                                                        in_=XQ2[:, 0:W_].rearrange("p (h d) -> p h d", h=nh),
                                                        axis=AX.X),
                          reads=[('XQ2',)], writes=[('SS8',)])
                    sc.op('act', lambda h: h.activation(out=SS8b[:, 0:nh], in_=SS8[:, 0:nh], func=ACTF.Sqrt,
                                                        bias=EPS, scale=1.0 / 128),
                          reads=[('SS8',)], writes=[('SS8b',)])
                    sc.op('dve', lambda h: h.reciprocal(out=SS8[:, 0:nh], in_=SS8b[:, 0:nh]),
                          reads=[('SS8b',)], writes=[('SS8',)])
                    sc.op('dve', lambda h: h.tensor_tensor(
                        out=x3, in0=x3, in1=SS8[:, 0:nh, None].to_broadcast([128, nh, 128]), op=ALU.mult),
                        reads=[('XQ',), ('SS8',)], writes=[('XQ',)])
                    sc.op('pool', lambda h: h.tensor_tensor(
                        out=x3, in0=x3, in1=gtile[:, None, :].to_broadcast([128, nh, 128]), op=ALU.mult),
                        reads=[('XQ',), gkey], writes=[('XQ',)])
                    x4 = XQ[:, 0:W_].rearrange("p (h i two) -> p h i two", h=nh, two=2)
                    x1 = x4[:, :, :, 0]
                    x2 = x4[:, :, :, 1]
                    cb = COS[:, tt:tt + 1, :].to_broadcast([128, nh, 64])
                    sb_ = SIN[:, tt:tt + 1, :].to_broadcast([128, nh, 64])
                    t1 = T1[:, 0:nh * 64].rearrange("p (h i) -> p h i", h=nh)
                    t2 = T2[:, 0:nh * 64].rearrange("p (h i) -> p h i", h=nh)
                    r4 = XR[:, 0:W_].rearrange("p (h i two) -> p h i two", h=nh, two=2)
                    sc.op('dve', lambda h: h.tensor_tensor(out=t1, in0=x1, in1=cb, op=ALU.mult),
                          reads=[('XQ',), ('COS',)], writes=[('T1',)])
                    sc.op('pool', lambda h: h.tensor_tensor(out=t2, in0=x2, in1=sb_, op=ALU.mult),
                          reads=[('XQ',), ('SIN',)], writes=[('T2',)])
                    sc.op('dve', lambda h: h.tensor_tensor(out=r4[:, :, :, 0], in0=t1, in1=t2, op=ALU.subtract),
                          reads=[('T1',), ('T2',)], writes=[('XR', 0)])
                    sc.op('dve', lambda h: h.tensor_tensor(out=t1, in0=x1, in1=sb_, op=ALU.mult),
                          reads=[('XQ',), ('SIN',), ('XR', 0)], writes=[('T1',)])
                    sc.op('pool', lambda h: h.tensor_tensor(out=t2, in0=x2, in1=cb, op=ALU.mult),
                          reads=[('XQ',), ('COS',), ('XR', 0)], writes=[('T2',)])
                    sc.op('dve', lambda h: h.tensor_tensor(out=r4[:, :, :, 1], in0=t1, in1=t2, op=ALU.add),
                          reads=[('T1',), ('T2',)], writes=[('XR', 1)])
                    pv = PS[6 + (tt % 2)][:, :].bitcast(BF16)
                    def tr(h):
                        inst = None
                        for j in range(nh):
                            inst = h.transpose(out=pv[:, j * 128:(j + 1) * 128], in_=XR[:, j * 128:(j + 1) * 128],
                                               identity=ident[:, :])
                        return inst
                    sc.op('pe', tr, reads=[('XR', 0), ('XR', 1), ('ident',)], writes=[('PS', 6 + (tt % 2))])
                    sc.op('act', lambda h: h.activation(
                        out=dst3[:, head0:head0 + nh, tt * 128:(tt + 1) * 128],
                        in_=pv[:, 0:W_].rearrange("p (j t) -> p j t", j=nh), func=ACTF.Copy),
                        reads=[('PS', 6 + (tt % 2))], writes=[(dkey, head0, tt)])
                return f

            for nb in range(18):
                src = w_in[fz(l)].rearrange("(kc p) n -> p kc n", p=128)[:, :, fz(nb) * 512:(fz(nb) + 1) * 512]
                wi, wv = load_w(src)
                if nb in (0, 1):
                    for c in range(4):
                        fm_chunk(wi, wv, c, qbuf, (4 * nb + c) * 128)
                elif nb in (2, 3):
                    for c in range(4):
                        fm_chunk(wi, wv, c, kbuf[nb - 2], c * 128)
                elif nb in (4, 5):
                    tm_block(wi, wv, 0, 512, v_consume(vbuf[nb - 4], 0, 512))
                elif nb in (6, 7):
                    tm_block(wi, wv, 0, 512, rope_consume(4, GQ, ('GQ',), BQ8, 'BQ8', 4 * (nb - 6)))
                elif nb == 8:
                    tm_block(wi, wv, 0, 256, rope_consume(2, GK, ('GK',), BK2, 'BK2', 0))
                    tm_block(wi, wv, 256, 256, v_consume(vbuf[2], 0, 256))
                elif nb in (9, 10):
                    for c in range(4):
                        fm_chunk(wi, wv, c, qbuf, (16 + 4 * (nb - 9) + c) * 128)
                elif nb == 11:
                    for c in range(2):
                        fm_chunk(wi, wv, c, kbuf[2], 256 + c * 128)
                    tm_block(wi, wv, 256, 256, v_consume(vbuf[2], 256, 256))
                elif nb in (12, 13):
                    for c in range(4):
                        fm_chunk(wi, wv, c, qbuf, (24 + 4 * (nb - 12) + c) * 128)
                elif nb in (14, 15):
                    for c in range(4):
                        fm_chunk(wi, wv, c, kbuf[3 + nb - 14], c * 128)
                else:
                    tm_block(wi, wv, 0, 512, v_consume(vbuf[3 + nb - 16], 0, 512))
            for hq in range(8):
                sc.dma('sp', lambda h, hq=hq: h.dma_start(out=qbuf[(8 + hq) * 128:(9 + hq) * 128, :],
                                                          in_=BQ8[:, hq, :]),
                       reads=[('BQ8', 4 * (hq // 4), tt) for tt in range(NT)], writes=[('qbuf_b', hq)])
            for kv in range(2):
                sc.dma('sp', lambda h, kv=kv: h.dma_start(out=kbuf[2][kv * 128:(kv + 1) * 128, :],
                                                          in_=BK2[:, kv, :]),
                       reads=[('BK2', 0, tt) for tt in range(NT)], writes=[('kbuf_b', kv)])
            sc.barrier()
            for j in range(5):
                sc.coll(lambda h, j=j: h.collective_compute("AllGather", ALU.bypass, replica_groups=PAIRS,
                                                            ins=[kbuf[j].ap().opt()], outs=[kgat[j].ap().opt()]))
                sc.coll(lambda h, j=j: h.collective_compute("AllGather", ALU.bypass, replica_groups=PAIRS,
                                                            ins=[vbuf[j].ap().opt()], outs=[vgat[j].ap().opt()]))
            sc.barrier()
            if debug and l == 0:
                for j in range(5):
                    sc.dma('sp', lambda h, j=j: h.dma_start(out=dbg_k[j], in_=kgat[j][:, :]))
                    sc.dma('sp', lambda h, j=j: h.dma_start(out=dbg_v[j], in_=vgat[j][:, :]))
                sc.barrier()
            sc.emit()
        if stop_after == 'proj':
            break

        mixT = BIG
        with ExitStack() as pes:
            def psb(name, shape, dt):
                return pes.enter_context(nc.sbuf_tensor("%s_l%d" % (name, l), list(shape), dt))
            VT = [W[0][:, 0:8192].rearrange("p (k c) -> p k c", c=512),
                  W[0][:, 8192:16384].rearrange("p (k c) -> p k c", c=512)]
            w1f = W[1][:, :].bitcast(F32)
            BIAS = [w1f[:, 0:2048], w1f[:, 2048:4096]]
            S_SB = [w1f[:, 4096:6144], w1f[:, 6144:8192]]
            xtb = XTF[:, :].bitcast(BF16)
            KTt = [xtb[:, 0:2048], xtb[:, 2048:4096]]
            Pt = [xtb[:, 4096:6144], xtb[:, 6144:8192]]
            gbb = GB[:, :].bitcast(BF16)
            PTt = [gbb[:, 0:2048], gbb[:, 2048:4096]]
            QTt = [gbb[:, 4096:5120], gbb[:, 5120:6144]]
            h2f = H2[:, :, :].rearrange("p k t -> p (k t)").bitcast(F32)
            O1N = h2f[:, 0:256]
            OD = h2f[:, 256:512]
            LAMT = h2f[:, 512:1024]
            LAMP = h2f[:, 1024:1536]
            SUBG = h2f[:, 1536:1792]
            ODB = psb("ODB", [128, 256], BF16)
            ONB = [psb("ONB0", [128, 128], BF16), psb("ONB1", [128, 128], BF16)]
            SINK = psb("SINK", [128, 8], F32)
            LAMS = psb("LAMS", [128, 8], F32)
            lambda_init = 0.8 - 0.6 * math.exp(-0.3 * l)

            sc.dma('sp', lambda h: h.dma_start(out=SUBG, in_=subln_in[l:l + 1, :].partition_broadcast(128)),
                   writes=[('SUBG',)])
            sc.dma('sp', lambda h: h.dma_start(out=SINK[:, :], in_=sink_in[l:l + 1, :].partition_broadcast(128)),
                   writes=[('SINK',)])
            sc.dma('sp', lambda h: h.dma_start(out=LAMT, in_=lam_in[l:l + 1, :].partition_broadcast(128)),
                   writes=[('LAMT',)])
            lt = LAMT.rearrange("p (a d) -> p a d", a=4)
            lp = LAMP[:, 0:256].rearrange("p (a d) -> p a d", a=2)
            sc.op('dve', lambda h: h.tensor_tensor(out=lp[:, 0, :], in0=lt[:, 0, :], in1=lt[:, 1, :], op=ALU.mult),
                  reads=[('LAMT',)], writes=[('LAMP', 0)])
            sc.op('dve', lambda h: h.tensor_tensor(out=lp[:, 1, :], in0=lt[:, 2, :], in1=lt[:, 3, :], op=ALU.mult),
                  reads=[('LAMT',)], writes=[('LAMP', 1)])
            sc.op('dve', lambda h: h.reduce_sum(out=LAMS[:, 0:2], in_=lp, axis=AX.X),
                  reads=[('LAMP', 0), ('LAMP', 1)], writes=[('LAMS', 0)])
            sc.op('act', lambda h: h.activation(out=LAMS[:, 2:4], in_=LAMS[:, 0:2], func=ACTF.Exp),
                  reads=[('LAMS', 0)], writes=[('LAMS', 1)])
            sc.op('dve', lambda h: h.scalar_tensor_tensor(out=LAMS[:, 4:5], in0=LAMS[:, 3:4], scalar=-lambda_init,
                                                          in1=LAMS[:, 2:3], op0=ALU.add, op1=ALU.subtract),
                  reads=[('LAMS', 1)], writes=[('LAMS', 2)])
            neglam = LAMS[:, 4:5]

            S_PS_keys = [('PS', i) for i in range(4)]
            def s_ps(j):
                return PS[j][:, :]
            PT_PS = [PS[4][:, :].bitcast(BF16), PS[5][:, :].bitcast(BF16)]
            O_PS = PS[6]
            OT_PS = PS[7][:, :].bitcast(BF16)

            units = []
            for hh in range(8):
                units.append(dict(q=hh, kj=hh // 4, kr=(hh % 4) * 128, vj=hh // 4, vc=(hh % 4) * 128, vw=128,
                                  bias=('A', hh), sink=None, e=hh, kind='n'))
            for hq in range(8):
                kv = hq // 4
                units.append(dict(q=8 + hq, kj=2, kr=kv * 128, vj=2, vc=kv * 128, vw=128,
                                  bias=None, sink=None, e=8 + hq, kind='n'))
            for hq in range(8):
                kv = hq // 4
                units.append(dict(q=16 + hq, kj=2, kr=256 + kv * 128, vj=2, vc=256 + kv * 128, vw=128,
                                  bias=('C', hq), sink=hq, e=16 + hq, kind='n'))
            for hh in range(4):
                units.append(dict(q=24 + hh, kj=3, kr=hh * 128, vj=3 + hh // 2, vc=(hh % 2) * 256, vw=256,
                                  bias=('D', hh), sink=None, e=24 + 2 * hh, kind='d1'))
                units.append(dict(q=28 + hh, kj=4, kr=hh * 128, vj=3 + hh // 2, vc=(hh % 2) * 256, vw=256,
                                  bias=('D', hh), sink=None, e=24 + 2 * hh, kind='d2'))

            cur_v = [None, None]
            v_rr = [0]
            it = [0]
            for ui, u in enumerate(units):
                ub = ui % 2
                kt = KTt[ub]
                for r in range(2):
                    sc.dma('sp', lambda h, r=r, kt=kt, u=u: h.dma_start(
                        out=kt[:, r * 1024:(r + 1) * 1024],
                        in_=kgat[u['kj']][r * 512 + u['kr']:r * 512 + u['kr'] + 128, :]),
                        writes=[('KT', ub, r)])
                qt = QTt[ub]
                sc.dma('sp', lambda h, qt=qt, u=u: h.dma_start(out=qt, in_=qbuf[u['q'] * 128:(u['q'] + 1) * 128, :]),
                       writes=[('QT', ub)])
                if u['vj'] in cur_v:
                    vb = cur_v.index(u['vj'])
                else:
                    vb = v_rr[0] % 2
                    v_rr[0] += 1
                    cur_v[vb] = u['vj']
                    sc.dma('sp', lambda h, vb=vb, u=u: h.dma_start(
                        out=VT[vb], in_=vgat[u['vj']].ap().rearrange("(k p) c -> p k c", p=128)),
                        writes=[('VT', vb)])
                vt = VT[vb]
                vw = u['vw']
                for qt_i in range(NT):
                    b2 = it[0] % 2
                    it[0] += 1
                    qs = slice(qt_i * 128, (qt_i + 1) * 128)
                    if u['bias'] is not None:
                        kind, hh = u['bias']
                        if kind == 'A':
                            bsrc = biasA[fz(l), fz(hh), qs, :]
                        elif kind == 'C':
                            bsrc = biasC[fz(hh), qs, :]
                        else:
                            bsrc = biasD[fz(hh), qs, :]
                        sc.dma('sp', lambda h, bsrc=bsrc, b2=b2: h.dma_start(out=BIAS[b2], in_=bsrc),
                               writes=[('BIAS', b2)])
                    def qk(h, qt=qt, kt=kt, qs=qs):
                        inst = None
                        for j in range(4):
                            inst = h.matmul(s_ps(j), lhsT=qt[:, qs], rhs=kt[:, j * 512:(j + 1) * 512],
                                            start=True, stop=True)
                        return inst
                    sc.op('pe', qk, reads=[('QT', ub), ('KT', ub, 0), ('KT', ub, 1)], writes=S_PS_keys)
                    m, mk = sm()
                    nm, nmk = sm()
                    rs, rsk = sm()
                    pt_ = Pt[b2]
                    if u['bias'] is not None:
                        ssb = S_SB[b2]
                        for j in range(4):
                            sc.op('dve', lambda h, j=j, ssb=ssb, b2=b2: h.scalar_tensor_tensor(
                                out=ssb[:, j * 512:(j + 1) * 512], in0=s_ps(j), scalar=SCALE,
                                in1=BIAS[b2][:, j * 512:(j + 1) * 512], op0=ALU.mult, op1=ALU.add),
                                reads=[('PS', j), ('BIAS', b2)], writes=[('S_SB', b2, j)])
                        sc.op('dve', lambda h, ssb=ssb, m=m: h.reduce_max(out=m, in_=ssb[:, 0:S], axis=AX.X),
                              reads=[('S_SB', b2, j) for j in range(4)], writes=[mk])
                        if u['sink'] is not None:
                            m2, m2k = sm()
                            sk = SINK[:, u['sink']:u['sink'] + 1]
                            sc.op('dve', lambda h, m=m, m2=m2, sk=sk: h.tensor_tensor(out=m2, in0=m, in1=sk,
                                                                                     op=ALU.max),
                                  reads=[mk, ('SINK',)], writes=[m2k])
                            m, mk = m2, m2k
                        sc.op('dve', lambda h, m=m, nm=nm: h.tensor_scalar(out=nm, in0=m, scalar1=-1.0, scalar2=None,
                                                                         op0=ALU.mult),
                              reads=[mk], writes=[nmk])
                        sc.op('act', lambda h, ssb=ssb, pt_=pt_, nm=nm, rs=rs: h.activation(
                            out=pt_, in_=ssb[:, 0:S], func=ACTF.Exp, bias=nm, scale=1.0, accum_out=rs),
                            reads=[('S_SB', b2, j) for j in range(4)] + [nmk], writes=[('P', b2), rsk])
                    else:
                        mcols = [sm() for _ in range(4)]
                        for j in range(4):
                            sc.op('dve', lambda h, j=j, mc=mcols[j][0]: h.reduce_max(out=mc, in_=s_ps(j), axis=AX.X),
                                  reads=[('PS', j)], writes=[mcols[j][1]])
                        ma, mak = sm()
                        mb, mbk = sm()
                        sc.op('dve', lambda h, ma=ma, a0=mcols[0][0], a1=mcols[1][0]: h.tensor_tensor(out=ma, in0=a0, in1=a1,
                                                                      op=ALU.max),
                              reads=[mcols[0][1], mcols[1][1]], writes=[mak])
                        sc.op('dve', lambda h, mb=mb, a0=mcols[2][0], a1=mcols[3][0]: h.tensor_tensor(out=mb, in0=a0, in1=a1,
                                                                      op=ALU.max),
                              reads=[mcols[2][1], mcols[3][1]], writes=[mbk])
                        sc.op('dve', lambda h, m=m, ma=ma, mb=mb: h.tensor_tensor(out=m, in0=ma, in1=mb, op=ALU.max),
                              reads=[mak, mbk], writes=[mk])
                        sc.op('dve', lambda h, m=m, nm=nm: h.tensor_scalar(out=nm, in0=m, scalar1=-SCALE,
                                                                         scalar2=None, op0=ALU.mult),
                              reads=[mk], writes=[nmk])
                        rparts = [sm() for _ in range(4)]
                        for j in range(4):
                            sc.op('act', lambda h, j=j, pt_=pt_, nm=nm, rp=rparts[j][0]: h.activation(
                                out=pt_[:, j * 512:(j + 1) * 512], in_=s_ps(j), func=ACTF.Exp, bias=nm, scale=SCALE,
                                accum_out=rp),
                                reads=[('PS', j), nmk], writes=[('P', b2, j), rparts[j][1]])
                        ra, rak = sm()
                        rb, rbk = sm()
                        sc.op('dve', lambda h, ra=ra, a0=rparts[0][0], a1=rparts[1][0]: h.tensor_tensor(out=ra, in0=a0, in1=a1,
                                                                      op=ALU.add),
                              reads=[rparts[0][1], rparts[1][1]], writes=[rak])
                        sc.op('dve', lambda h, rb=rb, a0=rparts[2][0], a1=rparts[3][0]: h.tensor_tensor(out=rb, in0=a0, in1=a1,
                                                                      op=ALU.add),
                              reads=[rparts[2][1], rparts[3][1]], writes=[rbk])
                        sc.op('dve', lambda h, rs=rs, ra=ra, rb=rb: h.tensor_tensor(out=rs, in0=ra, in1=rb,
                                                                                    op=ALU.add),
                              reads=[rak, rbk], writes=[rsk])
                    pkeys = [('P', b2)] + [('P', b2, j) for j in range(4)]
                    if u['sink'] is not None:
                        e1, e1k = sm()
                        sk = SINK[:, u['sink']:u['sink'] + 1]
                        sc.op('act', lambda h, e1=e1, sk=sk, nm=nm: h.activation(out=e1, in_=sk, func=ACTF.Exp,
                                                                              bias=nm, scale=1.0),
                              reads=[('SINK',), nmk], writes=[e1k])
                        rs2, rs2k = sm()
                        sc.op('dve', lambda h, rs2=rs2, rs=rs, e1=e1: h.tensor_tensor(out=rs2, in0=rs, in1=e1,
                                                                                      op=ALU.add),
                              reads=[rsk, e1k], writes=[rs2k])
                        rs, rsk = rs2, rs2k
                    ri, rik = sm()
                    sc.op('dve', lambda h, ri=ri, rs=rs: h.reciprocal(out=ri, in_=rs), reads=[rsk], writes=[rik])
                    ptt = PTt[b2]
                    for g in range(2):
                        def tr(h, g=g, pt_=pt_):
                            inst = None
                            for j in range(8):
                                k_ = g * 8 + j
                                inst = h.transpose(out=PT_PS[g][:, j * 128:(j + 1) * 128],
                                                   in_=pt_[:, k_ * 128:(k_ + 1) * 128], identity=ident[:, :])
                            return inst
                        sc.op('pe', tr, reads=pkeys + [('ident',)], writes=[('PS', 4 + g)])
                        if g == 0:
                            sc.op('act', lambda h, g=g, ptt=ptt: h.activation(
                                out=ptt[:, g * 1024:(g + 1) * 1024], in_=PT_PS[g], func=ACTF.Copy),
                                reads=[('PS', 4 + g)], writes=[('PT', b2, g)])
                        else:
                            sc.op('dve', lambda h, g=g, ptt=ptt: h.tensor_copy(
                                out=ptt[:, g * 1024:(g + 1) * 1024], in_=PT_PS[g]),
                                reads=[('PS', 4 + g)], writes=[('PT', b2, g)])
                    def pv(h, ptt=ptt, vt=vt, u=u, vw=vw):
                        inst = None
                        for k_ in range(KT):
                            inst = h.matmul(O_PS[:, 0:vw], lhsT=ptt[:, k_ * 128:(k_ + 1) * 128],
                                            rhs=vt[:, k_, u['vc']:u['vc'] + vw],
                                            start=(k_ == 0), stop=(k_ == KT - 1))
                        return inst
                    sc.op('pe', pv, reads=[('PT', b2, 0), ('PT', b2, 1), ('VT', vb)], writes=[('PS', 6)])
                    if u['kind'] == 'n':
                        onb = ONB[b2]
                        sc.op('dve', lambda h, onb=onb, ri=ri: h.tensor_scalar(out=onb[:, :], in0=O_PS[:, 0:128],
                                                                              scalar1=ri, scalar2=None, op0=ALU.mult),
                              reads=[('PS', 6), rik], writes=[('ONB', b2)])
                        sc.op('pe', lambda h, onb=onb: h.transpose(out=OT_PS[:, 0:128], in_=onb[:, :],
                                                                   identity=ident[:, :]),
                              reads=[('ONB', b2), ('ident',)], writes=[('PS', 7)])
                        sc.op('dve', lambda h, u=u, qs=qs: h.tensor_copy(out=mixT[:, u['e'], qs], in_=OT_PS[:, 0:128]),
                              reads=[('PS', 7)], writes=[('mixT', u['e'], qt_i)])
                    elif u['kind'] == 'd1':
                        sc.op('dve', lambda h, ri=ri: h.tensor_scalar(out=O1N, in0=O_PS[:, 0:256], scalar1=ri,
                                                                      scalar2=None, op0=ALU.mult),
                              reads=[('PS', 6), rik], writes=[('O1N',)])
                        sc.dma('sp', lambda h, qs=qs: h.dma_start(out=ybuf[qs, 0:256], in_=O1N),
                               reads=[('O1N',)], writes=[('yb1', qt_i)])
                    else:
                        sc.dma('sp', lambda h, qs=qs: h.dma_start(out=O1N, in_=ybuf[qs, 0:256]),
                               reads=[('yb1', qt_i)], writes=[('O1N',)])
                        c2, c2k = sm()
                        sc.op('dve', lambda h, c2=c2, ri=ri: h.tensor_tensor(out=c2, in0=ri, in1=neglam, op=ALU.mult),
                              reads=[rik, ('LAMS', 2)], writes=[c2k])
                        sc.op('dve', lambda h, c2=c2: h.scalar_tensor_tensor(out=OD, in0=O_PS[:, 0:256],
                                                                             scalar=c2, in1=O1N,
                                                                             op0=ALU.mult, op1=ALU.add),
                              reads=[('PS', 6), c2k, ('O1N',)], writes=[('OD',)])
                        ss, ssk = sm()
                        sc.op('act', lambda h, ss=ss: h.activation(out=ODB[:, :], in_=OD, func=ACTF.Square,
                                                                   accum_out=ss),
                              reads=[('OD',)], writes=[('ODB',), ssk])
                        a_, ak_ = sm()
                        sc.op('act', lambda h, a_=a_, ss=ss: h.activation(out=a_, in_=ss, func=ACTF.Sqrt, bias=EPS,
                                                                         scale=1.0 / 256),
                              reads=[ssk], writes=[ak_])
                        r_, rk_ = sm()
                        sc.op('dve', lambda h, r_=r_, a_=a_: h.reciprocal(out=r_, in_=a_), reads=[ak_], writes=[rk_])
                        r3, r3k = sm()
                        sc.op('dve', lambda h, r3=r3, r_=r_: h.tensor_scalar(out=r3, in0=r_,
                                                                            scalar1=(1.0 - lambda_init),
                                                                            scalar2=None, op0=ALU.mult),
                              reads=[rk_], writes=[r3k])
                        sc.op('dve', lambda h, r3=r3: h.scalar_tensor_tensor(out=ODB[:, :], in0=OD, scalar=r3,
                                                                             in1=SUBG, op0=ALU.mult,
                                                                             op1=ALU.mult),
                              reads=[('OD',), r3k, ('SUBG',), ('ODB',)], writes=[('ODB',)])
                        def tr2(h):
                            h.transpose(out=OT_PS[:, 0:128], in_=ODB[:, 0:128], identity=ident[:, :])
                            return h.transpose(out=OT_PS[:, 128:256], in_=ODB[:, 128:256], identity=ident[:, :])
                        sc.op('pe', tr2, reads=[('ODB',), ('ident',)], writes=[('PS', 7)])
                        sc.op('dve', lambda h, u=u, qs=qs: h.tensor_copy(
                            out=mixT[:, u['e']:u['e'] + 2, qs],
                            in_=OT_PS[:, 0:256].rearrange("p (j t) -> p j t", j=2)),
                            reads=[('PS', 7)], writes=[('mixT', u['e'], qt_i)])
            sc.barrier()
            if debug and l == 0:
                sc.dma('sp', lambda h: h.dma_start(out=dbg_mix[:, :, :], in_=BIG[:, :, :]))
                sc.barrier()
            sc.emit()
        if stop_after == 'attn':
            break

        for db in range(8):
            src = w_out[fz(l)].rearrange("(kc p) n -> p kc n", p=128)[:, :, fz(db) * 512:(fz(db) + 1) * 512]
            wi, wv = load_w(src)
            for tt in range(NT):
                pi = (db * NT + tt) % 8
                def mm(h, pi=pi, tt=tt, wv=wv):
                    inst = None
                    for kc in range(KC):
                        inst = h.matmul(PS[pi][:, :], lhsT=mixT[:, kc, tt * 128:(tt + 1) * 128], rhs=wv[:, kc, :],
                                        start=(kc == 0), stop=(kc == KC - 1))
                    return inst
                sc.op('pe', mm, reads=[('W', wi), ('BIGALL',)], writes=[('PS', pi)])
                evac_y(pi, tt, db)
        load_gain(g_attn_post, l)
        for tt in range(NT):
            if os.environ.get("K_SKIP_POSTNORM"):
                break
            postnorm_tile(slice(tt * 128, (tt + 1) * 128), xsrc, xa, 0, 1, tt)
        sc.barrier()
        sc.emit()
        if stop_after == 'wout':
            break

        uT = BIG[:, :, :].rearrange("p k t -> p (k t)").rearrange("p (f t) -> p f t", t=256)
        w0f = W[0][:, :].bitcast(F32)
        SG = [w0f[:, 0:4096], w0f[:, 4096:8192]]
        WU = [W[1][:, k * 4096:(k + 1) * 4096] for k in range(4)]
        ucnt = [0]

        def load_unit(src3, a, b):
            n = ucnt[0]
            ucnt[0] += 1
            si_, k = n % 2, n % 4
            sc.dma('sp', lambda h: h.dma_start(out=SG[si_].rearrange("p (a b) -> p a b", a=a), in_=src3),
                   writes=[('SG', si_)])
            if n % 2 == 0:
                sc.op('act', lambda h: h.activation(out=WU[k], in_=SG[si_], func=ACTF.Copy),
                      reads=[('SG', si_)], writes=[('WU', k)])
            else:
                sc.op('dve', lambda h: h.tensor_copy(out=WU[k], in_=SG[si_]),
                      reads=[('SG', si_)], writes=[('WU', k)])
            return k, WU[k].rearrange("p (a b) -> p a b", a=a)

        auxq[0] = 'pool'
        for tb in range(4):
            load_gain(g_mlp_pre, l)
            for i in range(2):
                tt = tb * 2 + i
                prenorm_tile(xa[tt * 128:(tt + 1) * 128, :], i,
                             lambda kc0, i=i: H2[:, kc0:kc0 + 8, i * 128:(i + 1) * 128], ('H2', i), psbase=i * 2)
            h2keys = [('H2', 0), ('H2', 1)]
            w1v = w1[fz(l)].rearrange("(kc p) n -> p kc n", p=128)
            for fb2 in range(64):
                pbase = 4 + (fb2 % 2) * 2
                for hh_ in range(2):
                    src = w1v[:, hh_ * 16:(hh_ + 1) * 16, fz(fb2) * 256:(fz(fb2) + 1) * 256]
                    k, wu = load_unit(src, 16, 256)
                    for c in range(2):
                        pi = pbase + c
                        def mm(h, pi=pi, c=c, wu=wu, hh_=hh_):
                            inst = None
                            for kk in range(16):
                                inst = h.matmul(PS[pi][:, 0:256], lhsT=wu[:, kk, c * 128:(c + 1) * 128],
                                                rhs=H2[:, hh_ * 16 + kk, :],
                                                start=(hh_ == 0 and kk == 0), stop=(hh_ == 1 and kk == 15))
                            return inst
                        sc.op('pe', mm, reads=[('WU', k)] + h2keys, writes=[('PS', pi)])
                for c in range(2):
                    pi = pbase + c
                    fc = fb2 * 2 + c
                    si = next_stg()
                    sc.op('act', lambda h, pi=pi, si=si: h.activation(out=STG[si][:, 0:256], in_=PS[pi][:, 0:256],
                                                                      func=ACTF.Relu),
                          reads=[('PS', pi)], writes=[('STG', si)])
                    sc.op('pool', lambda h, si=si, fc=fc: h.tensor_tensor(out=uT[:, fc, :], in0=STG[si][:, 0:256],
                                                                          in1=STG[si][:, 0:256], op=ALU.mult),
                          reads=[('STG', si)], writes=[('uT', fc)])
            w2v = w2[fz(l)].rearrange("(fc p) n -> p fc n", p=128)
            for db in range(8):
                for j8 in range(16):
                    src = w2v[:, fz(j8) * 8:(fz(j8) + 1) * 8, fz(db) * 512:(fz(db) + 1) * 512]
                    k, wu = load_unit(src, 8, 512)
                    for i in range(2):
                        pi = (db % 2) * 2 + i
                        def mm(h, pi=pi, i=i, j8=j8, wu=wu):
                            inst = None
                            for k_ in range(8):
                                fc = j8 * 8 + k_
                                inst = h.matmul(PS[pi][:, :], lhsT=uT[:, fc, i * 128:(i + 1) * 128], rhs=wu[:, k_, :],
                                                start=(fc == 0), stop=(fc == 127))
                            return inst
                        sc.op('pe', mm, reads=[('WU', k)] + [('uT', fc) for fc in range(j8 * 8, j8 * 8 + 8)],
                              writes=[('PS', pi)])
                for i in range(2):
                    evac_y((db % 2) * 2 + i, tb * 2 + i, db)
            load_gain(g_mlp_post, l)
            for i in range(2):
                tt = tb * 2 + i
                postnorm_tile(slice(tt * 128, (tt + 1) * 128), xa, xfinal, 0, 1, tt)
            sc.barrier()
            sc.emit()
        auxq[0] = 'sp'
    sc.barrier()
    sc.emit()


def _t5_bucket_np(rel):
    import jax
    import jax.numpy as jnp
    cpu = jax.devices("cpu")[0]
    with jax.default_device(cpu):
        rel = jnp.asarray(rel, dtype=jnp.int32)
        nb = 16
        max_exact = 8
        base = jnp.where(rel > 0, nb, 0)
        n = jnp.abs(rel)
        n_f = jnp.maximum(n, 1).astype(jnp.float32)
        large = max_exact + (jnp.log(n_f / max_exact) / math.log(128 / max_exact) * (nb - max_exact)).astype(jnp.int32)
        large = jnp.minimum(large, nb - 1)
        return np.asarray(base + jnp.where(n < max_exact, n, large))


_CONST = {}


def _consts():
    if _CONST:
        return _CONST
    rel1d = np.arange(-(S - 1), S, dtype=np.int32)
    bucket1d = _t5_bucket_np(rel1d)
    per_half = []
    for hf in range(2):
        qpos = hf * NTOK + np.arange(NTOK)
        kpos = np.arange(S)
        rel = kpos[None, :] - qpos[:, None]
        bidx = bucket1d[rel + S - 1]
        cmask = np.abs(rel) <= 128
        r = qpos // 64
        c = qpos % 64
        rs = np.clip(r - 4, 0, 32 - 8)
        cs = np.clip(c - 8, 0, 64 - 16)
        kr = kpos // 64
        kc = kpos % 64
        amask = (kr[None, :] >= rs[:, None]) & (kr[None, :] < rs[:, None] + 8) & \
                (kc[None, :] >= cs[:, None]) & (kc[None, :] < cs[:, None] + 16)
        dr = np.clip(kr[None, :] - r[:, None] + 7, 0, 14)
        dc = np.clip(kc[None, :] - c[:, None] + 15, 0, 30)
        row = (qpos // 64).astype(np.float32)
        col = (qpos % 64).astype(np.float32)
        inv = (10000.0 ** (-np.arange(32, dtype=np.float32) / 32)).astype(np.float32)
        ang = np.concatenate([row[:, None] * inv, col[:, None] * inv], axis=-1).astype(np.float32)
        per_half.append(dict(bidx=bidx, cmask=cmask, amask=amask, dr=dr, dc=dc,
                             cos=np.cos(ang).astype(np.float32), sin=np.sin(ang).astype(np.float32)))
    _CONST['h'] = per_half
    _CONST['ident'] = np.eye(128, dtype=np.float32).astype(ml_dtypes.bfloat16)
    return _CONST


def make_in_maps(inputs):
    cst = _consts()
    f = lambda a: np.ascontiguousarray(np.asarray(a, dtype=np.float32))
    x = f(inputs["x"])
    shared = {k: f(inputs[k]) for k in ["ln_attn_pre", "ln_attn_post", "ln_mlp_pre", "ln_mlp_post", "w_in", "w_out",
                                        "w_mlp_in", "w_mlp_out", "ax_q_norm", "ax_k_norm", "sw_sink", "df_subln"]}
    shared["df_lambda"] = f(inputs["df_lambda"]).reshape(2, 512)
    shared["ident"] = cst['ident']
    rpb = f(inputs["na_rpb"])
    t5 = f(inputs["t5_table"])
    halves = []
    for hf in range(2):
        c = cst['h'][hf]
        bA = np.empty((2, 8, NTOK, S), np.float32)
        for l in range(2):
            for h in range(8):
                bA[l, h] = np.where(c['amask'], rpb[l, h][c['dr'], c['dc']], np.float32(NEG))
        bC = np.empty((8, NTOK, S), np.float32)
        for h in range(8):
            bC[h] = np.where(c['cmask'], t5[:, h][c['bidx']], np.float32(NEG))
        bD = np.empty((4, NTOK, S), np.float32)
        for h in range(4):
            bD[h] = t5[:, 8 + h][c['bidx']]
        halves.append(dict(biasA=bA, biasC=bC, biasD=bD, rope_cos=c['cos'], rope_sin=c['sin']))
    in_maps = []
    for core in range(8):
        b, hf = core // 2, core % 2
        m = dict(shared)
        m.update(halves[hf])
        m["x"] = np.ascontiguousarray(x[b, hf * NTOK:(hf + 1) * NTOK, :])
        in_maps.append(m)
    return in_maps


_NC = {}


def kernel(**inputs):
    if 'nc' not in _NC:
        _NC['nc'] = build_program()
    nc = _NC['nc']
    in_maps = make_in_maps(inputs)
    res = run_bass_kernel_spmd(nc, in_maps, core_ids=list(range(8)))
    outp = np.empty((4, S, D), np.float32)
    for core in range(8):
        b, hf = core // 2, core % 2
        outp[b, hf * NTOK:(hf + 1) * NTOK, :] = res.results[core]["out"]
    return outp
```

```python
import math
import os
from contextlib import ExitStack
import numpy as np
import ml_dtypes
import concourse.bass as bass
import concourse.mybir as mybir
from concourse.bass_utils import run_bass_kernel_spmd

F32 = mybir.dt.float32
BF16 = mybir.dt.bfloat16
ALU = mybir.AluOpType
ACTF = mybir.ActivationFunctionType
AX = mybir.AxisListType

D = 4096
KC = 32
NTOK = 1024
NT = 8
S = 2048
KT = 16
DIN = 9216
DFF = 16384
EPS = 1e-6
SCALE = 128 ** -0.5
NEG = -30000.0
PAIRS = [[0, 1], [2, 3], [4, 5], [6, 7]]


class Sched:
    ENG_ATTR = [('pe', 'tensor'), ('act', 'scalar'), ('dve', 'vector'), ('pool', 'gpsimd'), ('sp', 'sync')]

    def __init__(self, nc, es):
        self.nc = nc
        self.engs = [e for e, _ in self.ENG_ATTR]
        self.sem = {e: es.enter_context(nc.semaphore("s_" + e)) for e in self.engs}
        self.cnt = {e: 0 for e in self.engs}
        self.NDS = 40
        self.dsem = [es.enter_context(nc.semaphore("d%d" % i)) for i in range(self.NDS)]
        self.dcnt = [0] * self.NDS
        self.dnext = 0
        self.ops = []
        self.lastw = {}
        self.readers = {}
        self.waited = {e: {} for e in self.engs}

    def _deps(self, eng, reads, writes):
        toks = []
        for k in reads:
            t = self.lastw.get(k)
            if t is not None and not (t[0] == 'e' and t[1] == eng and eng == 'pe'):
                toks.append(t)
        for k in writes:
            t = self.lastw.get(k)
            if t is not None and not (t[0] == 'e' and t[1] == eng):
                toks.append(t)
            for t in self.readers.get(k, ()):
                if not (t[0] == 'e' and t[1] == eng):
                    toks.append(t)
        return toks

    def _commit(self, tok, reads, writes):
        for k in reads:
            self.readers.setdefault(k, []).append(tok)
        for k in writes:
            self.lastw[k] = tok
            self.readers[k] = []

    def op(self, eng, fn, reads=(), writes=()):
        toks = self._deps(eng, reads, writes)
        self.cnt[eng] += 1
        tok = ('e', eng, self.cnt[eng])
        self.ops.append((eng, fn, toks, tok, 1))
        self._commit(tok, reads, writes)
        return tok

    def dma(self, q, fn, reads=(), writes=()):
        toks = self._deps(q, reads, writes)
        i = self.dnext
        self.dnext = (i + 1) % self.NDS
        if self.dcnt[i] > 0:
            toks.append(('d', i, self.dcnt[i]))
        self.dcnt[i] += 16
        tok = ('d', i, self.dcnt[i])
        self.ops.append((q, fn, toks, tok, 16))
        self._commit(tok, reads, writes)
        return tok

    def coll(self, fn, reads=(), writes=()):
        return self.op('pool', fn, reads, writes)

    def barrier(self):
        toks = [('e', e, self.cnt[e]) for e in self.engs if self.cnt[e] > 0]
        toks += [('d', i, self.dcnt[i]) for i in range(self.NDS) if self.dcnt[i] > 0]
        for e in self.engs:
            self.ops.append((e, None, [t for t in toks if not (t[0] == 'e' and t[1] == e)], None, 0))
        self.lastw = {}
        self.readers = {}

    def _wait(self, h, e, t):
        key = (t[0], t[1])
        if self.waited[e].get(key, 0) >= t[2]:
            return
        self.waited[e][key] = t[2]
        sem = self.sem[t[1]] if t[0] == 'e' else self.dsem[t[1]]
        h.wait_ge(sem, t[2])

    def emit(self):
        nc = self.nc
        per = {e: [] for e in self.engs}
        for o in self.ops:
            per[o[0]].append(o)
        with nc.Block() as blk:
            for e, attr in self.ENG_ATTR:
                lst = per[e]
                if not lst:
                    continue

                def body(h, lst=lst, e=e):
                    for (_, fn, toks, tok, inc) in lst:
                        for t in toks:
                            self._wait(h, e, t)
                        if fn is None:
                            continue
                        inst = fn(h)
                        if tok[0] == 'e':
                            inst.then_inc(self.sem[e], 1)
                        else:
                            inst.then_inc(self.dsem[tok[1]], 16)
                getattr(blk, attr)(body)
        self.ops = []


def build_program(nlayers=2, debug=False, stop_after=None, fast=False):
    nc = bass.Bass("TRN2", target_bir_lowering=False)
    es = ExitStack()
    with es:
        _build(nc, es, nlayers, debug, stop_after, fast)
    return nc


def _build(nc, es, nlayers, debug, stop_after, fast=False):
    def din(name, shape, dt=F32):
        return nc.dram_tensor(name, list(shape), dt, kind="ExternalInput")

    dbgkind = "ExternalOutput" if debug else "Internal"

    x_in = din("x", [NTOK, D])
    g_attn_pre = din("ln_attn_pre", [2, D])
    g_attn_post = din("ln_attn_post", [2, D])
    g_mlp_pre = din("ln_mlp_pre", [2, D])
    g_mlp_post = din("ln_mlp_post", [2, D])
    if fast:
        w_in = din("w_in", [1, D, 512])
        w_out = din("w_out", [1, D, 512])
        w1 = din("w_mlp_in", [1, D, 512])
        w2 = din("w_mlp_out", [1, D, 512])
    else:
        w_in = din("w_in", [2, D, DIN])
        w_out = din("w_out", [2, D, D])
        w1 = din("w_mlp_in", [2, D, DFF])
        w2 = din("w_mlp_out", [2, DFF, D])
    qn_g = din("ax_q_norm", [2, 128])
    kn_g = din("ax_k_norm", [2, 128])
    sink_in = din("sw_sink", [2, 8])
    lam_in = din("df_lambda", [2, 512])
    subln_in = din("df_subln", [2, 256])
    if fast:
        biasA = din("biasA", [1, 1, NTOK, S])
        biasC = din("biasC", [1, NTOK, S])
        biasD = din("biasD", [1, NTOK, S])
    else:
        biasA = din("biasA", [2, 8, NTOK, S])
        biasC = din("biasC", [8, NTOK, S])
        biasD = din("biasD", [4, NTOK, S])
    fz = (lambda v: 0) if fast else (lambda v: v)
    cos_in = din("rope_cos", [NTOK, 64])
    sin_in = din("rope_sin", [NTOK, 64])
    ident_in = din("ident", [128, 128], BF16)
    out = nc.dram_tensor("out", [NTOK, D], F32, kind="ExternalOutput")

    xa = nc.dram_tensor("xa", [NTOK, D], F32, kind=dbgkind)
    xb = nc.dram_tensor("xb", [NTOK, D], F32, kind=dbgkind)
    ybuf = nc.dram_tensor("ybuf", [NTOK, D], F32, kind=dbgkind)
    qbuf = nc.dram_tensor("qbuf", [32 * 128, NTOK], BF16, kind=dbgkind)
    kbuf = [nc.dram_tensor("kbuf%d" % j, [512, NTOK], BF16) for j in range(5)]
    kgat = [nc.dram_tensor("kgat%d" % j, [1024, NTOK], BF16) for j in range(5)]
    vbuf = [nc.dram_tensor("vbuf%d" % j, [NTOK, 512], BF16) for j in range(5)]
    vgat = [nc.dram_tensor("vgat%d" % j, [S, 512], BF16) for j in range(5)]
    if debug:
        dbg_mix = nc.dram_tensor("dbg_mix", [128, KC, NTOK], BF16, kind="ExternalOutput")
        dbg_k = nc.dram_tensor("dbg_k", [5, 1024, NTOK], BF16, kind="ExternalOutput")
        dbg_v = nc.dram_tensor("dbg_v", [5, S, 512], BF16, kind="ExternalOutput")

    wsc = nc.dram_tensor("wsc", [256, 128, 4096], BF16)
    sc = Sched(nc, es)

    def sb(name, shape, dt):
        return es.enter_context(nc.sbuf_tensor(name, list(shape), dt))

    def ps(name, shape, dt):
        return es.enter_context(nc.psum_tensor(name, list(shape), dt))

    BIG = sb("BIG", [128, KC, NTOK], BF16)
    W = [sb("W0", [128, KC * 512], BF16), sb("W1", [128, KC * 512], BF16)]
    XTF = sb("XTF", [128, D], F32)
    GB = sb("GB", [128, D], F32)
    XN = sb("XN", [128, D], BF16)
    ident = sb("ident_sb", [128, 128], BF16)
    small = sb("small", [128, 64], F32)
    SSP = sb("SSP", [128, NT, 8], F32)
    STG = [sb("STG%d" % i, [128, 512], F32) for i in range(2)]
    STGB = [sb("STGB%d" % i, [128, 512], BF16) for i in range(2)]
    H2 = sb("H2", [128, KC, 256], BF16)
    PS = [ps("PS%d" % i, [128, 512], F32) for i in range(8)]

    auxq = ['sp']
    wcount = [0]

    def next_w():
        i = wcount[0] % 2
        wcount[0] += 1
        return i

    stg_i = [0]

    def next_stg():
        i = stg_i[0] % 2
        stg_i[0] += 1
        return i

    sm_i = [0]

    def sm():
        i = sm_i[0] % 64
        sm_i[0] += 1
        return small[:, i:i + 1], ('small', i)

    sc.dma('sp', lambda h: h.dma_start(out=ident[:, :], in_=ident_in[:, :]), writes=[('ident',)])

    def load_gain(gt, l):
        sc.dma(auxq[0], lambda h: h.dma_start(out=GB[:, :], in_=gt[l:l + 1, :].partition_broadcast(128)),
               writes=[('GB',)])

    def rstd_from_ss(ss_ap, ss_key, n):
        a, ak = sm()
        sc.op('act', lambda h: h.activation(out=a, in_=ss_ap, func=ACTF.Sqrt, bias=EPS, scale=1.0 / n),
              reads=[ss_key], writes=[ak])
        r, rk = sm()
        sc.op('dve', lambda h: h.reciprocal(out=r, in_=a), reads=[ak], writes=[rk])
        return r, rk

    def prenorm_tile(xsrc_ap, slot, dst_fn, dst_key, psbase):
        xt = XTF
        xn = XN
        sc.dma(auxq[0], lambda h: h.dma_start(out=xt[:, :], in_=xsrc_ap), writes=[('XT',)])
        ss, ssk = sm()
        sc.op('act', lambda h: h.activation(out=xn[:, :], in_=xt[:, :], func=ACTF.Square, accum_out=ss),
              reads=[('XT',)], writes=[('XN',), ssk])
        r, rk = rstd_from_ss(ss, ssk, D)
        sc.op('dve', lambda h: h.scalar_tensor_tensor(out=xn[:, :], in0=xt[:, :], scalar=r, in1=GB[:, :],
                                                      op0=ALU.mult, op1=ALU.mult),
              reads=[('XT',), rk, ('GB',)], writes=[('XN',)])
        for g in range(4):
            pst = PS[psbase + (g % 2)]
            pv = pst[:, :].bitcast(BF16)
            def tr(h, g=g, pv=pv):
                inst = None
                for j in range(8):
                    kc = g * 8 + j
                    inst = h.transpose(out=pv[:, j * 128:(j + 1) * 128], in_=xn[:, kc * 128:(kc + 1) * 128],
                                       identity=ident[:, :])
                return inst
            sc.op('pe', tr, reads=[('XN',), ('ident',)], writes=[('PS', psbase + (g % 2))])
            dst = dst_fn(g * 8)
            eng = 'act' if g % 2 == 0 else 'dve'
            if eng == 'act':
                sc.op('act', lambda h, dst=dst, pv=pv: h.activation(
                    out=dst, in_=pv.rearrange("p (j t) -> p j t", j=8), func=ACTF.Copy),
                    reads=[('PS', psbase + (g % 2))], writes=[dst_key])
            else:
                sc.op('dve', lambda h, dst=dst, pv=pv: h.tensor_copy(
                    out=dst, in_=pv.rearrange("p (j t) -> p j t", j=8)),
                    reads=[('PS', psbase + (g % 2))], writes=[dst_key])

    def load_w(src_ap):
        i = next_w()
        wv = W[i][:, :].rearrange("p (k n) -> p k n", n=512)
        sc.dma('pool', lambda h: h.dma_start(out=wv, in_=src_ap), writes=[('W', i)])
        return i, wv

    def postnorm_tile(tt_rows, xsrc, xdst, slot_y, slot_x, ssp_tt):
        ss, ssk = sm()
        sc.op('dve', lambda h: h.reduce_sum(out=ss, in_=SSP[:, ssp_tt, :], axis=AX.X),
              reads=[('SSP', ssp_tt)], writes=[ssk])
        r, rk = rstd_from_ss(ss, ssk, D)
        for c in range(2):
            cs = slice(c * 2048, (c + 1) * 2048)
            yt = XTF[:, 0:2048]
            xt = XTF[:, 2048:4096]
            sc.dma(auxq[0], lambda h, cs=cs: h.dma_start(out=yt, in_=ybuf[tt_rows, cs]),
                   reads=[('ybuf', tt_rows.start, db) for db in range(8)], writes=[('XTy',)])
            sc.dma(auxq[0], lambda h, cs=cs: h.dma_start(out=xt, in_=xsrc[tt_rows, cs]), writes=[('XTx',)])
            sc.op('dve', lambda h, cs=cs: h.scalar_tensor_tensor(out=yt, in0=yt, scalar=r, in1=GB[:, cs],
                                                             op0=ALU.mult, op1=ALU.mult),
                  reads=[('XTy',), rk, ('GB',)], writes=[('XTy',)])
            sc.op('pool', lambda h: h.tensor_tensor(out=xt, in0=xt, in1=yt, op=ALU.add),
                  reads=[('XTy',), ('XTx',)], writes=[('XTx',)])
            sc.dma(auxq[0], lambda h, cs=cs: h.dma_start(out=xdst[tt_rows, cs], in_=xt),
                   reads=[('XTx',)], writes=[('xdst', tt_rows.start, c)])

    def evac_y(pidx, tt, db, ncols=512):
        si = next_stg()
        junk = STGB[si]
        lvl = int(os.environ.get("K_P3", "9"))
        if lvl < 1:
            return
        sc.op('act', lambda h: h.activation(out=junk[:, :], in_=PS[pidx][:, :], func=ACTF.Square,
                                            accum_out=SSP[:, tt, db:db + 1]),
              reads=[('PS', pidx)], writes=[('STGB', si), ('SSP', tt)])
        if lvl < 2:
            return
        st = STG[si]
        sc.op('dve', lambda h: h.tensor_scalar(out=st[:, :], in0=PS[pidx][:, :], scalar1=1.0, scalar2=None, op0=ALU.mult),
              reads=[('PS', pidx), ('STGB', si)], writes=[('STG', si)])
        if lvl < 3:
            return
        rows = slice(tt * 128, (tt + 1) * 128)
        sc.dma(auxq[0], lambda h: h.dma_start(out=ybuf[rows, db * 512:(db + 1) * 512], in_=st[:, :]),
               reads=[('STG', si)], writes=[('ybuf', rows.start, db)])

    for l in range(nlayers):
        xsrc = x_in if l == 0 else xb
        xfinal = out if l == nlayers - 1 else xb

        load_gain(g_attn_pre, l)
        for tt in range(NT):
            prenorm_tile(xsrc[tt * 128:(tt + 1) * 128, :], tt % 2,
                         lambda kc0, tt=tt: BIG[:, kc0:kc0 + 8, tt * 128:(tt + 1) * 128],
                         ('BIG', tt), psbase=(tt % 2) * 2)
        sc.barrier()
        sc.emit()

        hT = BIG
        allhT = [('BIG', tt) for tt in range(NT)]
        with ExitStack() as pes:
            def psb(name, shape, dt):
                return pes.enter_context(nc.sbuf_tensor("%s_l%d" % (name, l), list(shape), dt))
            xtb = XTF[:, :].bitcast(BF16)
            gbb = GB[:, :].bitcast(BF16)
            BQ8 = xtb.rearrange("p (h t) -> p h t", h=8)
            BK2 = gbb[:, 0:2048].rearrange("p (h t) -> p h t", h=2)
            XQ = GB[:, 1024:1536]
            XQ2 = GB[:, 1536:2048]
            T1 = GB[:, 2048:2304]
            T2 = GB[:, 2304:2560]
            GQ = GB[:, 2560:2688]
            GK = GB[:, 2688:2816]
            COS = GB[:, 2816:3328].rearrange("p (t c) -> p t c", t=NT)
            SIN = GB[:, 3328:3840].rearrange("p (t c) -> p t c", t=NT)
            SS8 = GB[:, 3840:3848]
            SS8b = GB[:, 3848:3856]
            XR = XN[:, 0:512]
            sc.dma('sp', lambda h: h.dma_start(out=GQ, in_=qn_g[l:l + 1, :].partition_broadcast(128)),
                   writes=[('GQ',)])
            sc.dma('sp', lambda h: h.dma_start(out=GK, in_=kn_g[l:l + 1, :].partition_broadcast(128)),
                   writes=[('GK',)])
            sc.dma('sp', lambda h: h.dma_start(out=COS,
                                               in_=cos_in.ap().rearrange("(t p) c -> p t c", p=128)),
                   writes=[('COS',)])
            sc.dma('sp', lambda h: h.dma_start(out=SIN,
                                               in_=sin_in.ap().rearrange("(t p) c -> p t c", p=128)),
                   writes=[('SIN',)])

            acc_i = [0]

            def next_acc():
                i = acc_i[0] % 6
                acc_i[0] += 1
                return i

            def fm_chunk(wi, wv, c, dst_tensor, dst_row):
                for half in range(2):
                    pi = next_acc()
                    def mm(h, pi=pi, half=half):
                        inst = None
                        for kc in range(KC):
                            inst = h.matmul(PS[pi][:, :], lhsT=wv[:, kc, c * 128:(c + 1) * 128],
                                            rhs=hT[:, kc, half * 512:(half + 1) * 512],
                                            start=(kc == 0), stop=(kc == KC - 1))
                        return inst
                    sc.op('pe', mm, reads=[('W', wi)] + allhT, writes=[('PS', pi)])
                    si = next_stg()
                    if half == 0:
                        sc.op('act', lambda h, pi=pi, si=si: h.activation(out=STGB[si][:, :], in_=PS[pi][:, :],
                                                                          func=ACTF.Copy),
                              reads=[('PS', pi)], writes=[('STGB', si)])
                    else:
                        sc.op('dve', lambda h, pi=pi, si=si: h.tensor_copy(out=STGB[si][:, :], in_=PS[pi][:, :]),
                              reads=[('PS', pi)], writes=[('STGB', si)])
                    sc.dma('sp', lambda h, si=si, half=half: h.dma_start(
                        out=dst_tensor[dst_row:dst_row + 128, half * 512:(half + 1) * 512], in_=STGB[si][:, :]),
                        reads=[('STGB', si)], writes=[('fm', id(dst_tensor), dst_row, half)])

            def tm_block(wi, wv, c0, ncols, consume):
                for tt in range(NT):
                    pi = next_acc()
                    def mm(h, pi=pi, tt=tt):
                        inst = None
                        for kc in range(KC):
                            inst = h.matmul(PS[pi][:, 0:ncols], lhsT=hT[:, kc, tt * 128:(tt + 1) * 128],
                                            rhs=wv[:, kc, c0:c0 + ncols],
                                            start=(kc == 0), stop=(kc == KC - 1))
                        return inst
                    sc.op('pe', mm, reads=[('W', wi), ('BIG', tt)], writes=[('PS', pi)])
                    consume(tt, pi)

            def v_consume(vt, col0, ncols):
                def f(tt, pi):
                    si = next_stg()
                    sc.op('act', lambda h: h.activation(out=STGB[si][:, 0:ncols], in_=PS[pi][:, 0:ncols],
                                                        func=ACTF.Copy),
                          reads=[('PS', pi)], writes=[('STGB', si)])
                    sc.dma('sp', lambda h: h.dma_start(out=vt[tt * 128:(tt + 1) * 128, col0:col0 + ncols],
                                                       in_=STGB[si][:, 0:ncols]),
                           reads=[('STGB', si)], writes=[('vout', id(vt), tt, col0)])
                return f

            def rope_consume(nh, gtile, gkey, dst3, dkey, head0):
                W_ = nh * 128
                def f(tt, pi):
                    x3 = XQ[:, 0:W_].rearrange("p (h d) -> p h d", h=nh)
                    sc.op('act', lambda h: h.activation(out=XQ[:, 0:W_], in_=PS[pi][:, 0:W_], func=ACTF.Copy),
                          reads=[('PS', pi)], writes=[('XQ',)])
                    sc.op('dve', lambda h: h.tensor_tensor(out=XQ2[:, 0:W_], in0=XQ[:, 0:W_], in1=XQ[:, 0:W_],
                                                           op=ALU.mult),
                          reads=[('XQ',)], writes=[('XQ2',)])
                    sc.op('dve', lambda h: h.reduce_sum(out=SS8[:, 0:nh],
                                                        in_=XQ2[:, 0:W_].rearrange("p (h d) -> p h d", h=nh),
                                                        axis=AX.X),
                          reads=[('XQ2',)], writes=[('SS8',)])
                    sc.op('act', lambda h: h.activation(out=SS8b[:, 0:nh], in_=SS8[:, 0:nh], func=ACTF.Sqrt,
                                                        bias=EPS, scale=1.0 / 128),
                          reads=[('SS8',)], writes=[('SS8b',)])
                    sc.op('dve', lambda h: h.reciprocal(out=SS8[:, 0:nh], in_=SS8b[:, 0:nh]),
                          reads=[('SS8b',)], writes=[('SS8',)])
                    sc.op('dve', lambda h: h.tensor_tensor(
                        out=x3, in0=x3, in1=SS8[:, 0:nh, None].to_broadcast([128, nh, 128]), op=ALU.mult),
                        reads=[('XQ',), ('SS8',)], writes=[('XQ',)])
                    sc.op('pool', lambda h: h.tensor_tensor(
                        out=x3, in0=x3, in1=gtile[:, None, :].to_broadcast([128, nh, 128]), op=ALU.mult),
                        reads=[('XQ',), gkey], writes=[('XQ',)])
                    x4 = XQ[:, 0:W_].rearrange("p (h i two) -> p h i two", h=nh, two=2)
                    x1 = x4[:, :, :, 0]
                    x2 = x4[:, :, :, 1]
                    cb = COS[:, tt:tt + 1, :].to_broadcast([128, nh, 64])
                    sb_ = SIN[:, tt:tt + 1, :].to_broadcast([128, nh, 64])
                    t1 = T1[:, 0:nh * 64].rearrange("p (h i) -> p h i", h=nh)
                    t2 = T2[:, 0:nh * 64].rearrange("p (h i) -> p h i", h=nh)
                    r4 = XR[:, 0:W_].rearrange("p (h i two) -> p h i two", h=nh, two=2)
                    sc.op('dve', lambda h: h.tensor_tensor(out=t1, in0=x1, in1=cb, op=ALU.mult),
                          reads=[('XQ',), ('COS',)], writes=[('T1',)])
                    sc.op('pool', lambda h: h.tensor_tensor(out=t2, in0=x2, in1=sb_, op=ALU.mult),
                          reads=[('XQ',), ('SIN',)], writes=[('T2',)])
                    sc.op('dve', lambda h: h.tensor_tensor(out=r4[:, :, :, 0], in0=t1, in1=t2, op=ALU.subtract),
                          reads=[('T1',), ('T2',)], writes=[('XR', 0)])
                    sc.op('dve', lambda h: h.tensor_tensor(out=t1, in0=x1, in1=sb_, op=ALU.mult),
                          reads=[('XQ',), ('SIN',), ('XR', 0)], writes=[('T1',)])
                    sc.op('pool', lambda h: h.tensor_tensor(out=t2, in0=x2, in1=cb, op=ALU.mult),
                          reads=[('XQ',), ('COS',), ('XR', 0)], writes=[('T2',)])
                    sc.op('dve', lambda h: h.tensor_tensor(out=r4[:, :, :, 1], in0=t1, in1=t2, op=ALU.add),
                          reads=[('T1',), ('T2',)], writes=[('XR', 1)])
                    pv = PS[6 + (tt % 2)][:, :].bitcast(BF16)
                    def tr(h):
                        inst = None
                        for j in range(nh):
                            inst = h.transpose(out=pv[:, j * 128:(j + 1) * 128], in_=XR[:, j * 128:(j + 1) * 128],
                                               identity=ident[:, :])
                        return inst
                    sc.op('pe', tr, reads=[('XR', 0), ('XR', 1), ('ident',)], writes=[('PS', 6 + (tt % 2))])
                    sc.op('act', lambda h: h.activation(
                        out=dst3[:, head0:head0 + nh, tt * 128:(tt + 1) * 128],
                        in_=pv[:, 0:W_].rearrange("p (j t) -> p j t", j=nh), func=ACTF.Copy),
                        reads=[('PS', 6 + (tt % 2))], writes=[(dkey, head0, tt)])
                return f

            for nb in range(18):
                src = w_in[fz(l)].rearrange("(kc p) n -> p kc n", p=128)[:, :, fz(nb) * 512:(fz(nb) + 1) * 512]
                wi, wv = load_w(src)
                if nb in (0, 1):
                    for c in range(4):
                        fm_chunk(wi, wv, c, qbuf, (4 * nb + c) * 128)
                elif nb in (2, 3):
                    for c in range(4):
                        fm_chunk(wi, wv, c, kbuf[nb - 2], c * 128)
                elif nb in (4, 5):
                    tm_block(wi, wv, 0, 512, v_consume(vbuf[nb - 4], 0, 512))
                elif nb in (6, 7):
                    tm_block(wi, wv, 0, 512, rope_consume(4, GQ, ('GQ',), BQ8, 'BQ8', 4 * (nb - 6)))
                elif nb == 8:
                    tm_block(wi, wv, 0, 256, rope_consume(2, GK, ('GK',), BK2, 'BK2', 0))
                    tm_block(wi, wv, 256, 256, v_consume(vbuf[2], 0, 256))
                elif nb in (9, 10):
                    for c in range(4):
                        fm_chunk(wi, wv, c, qbuf, (16 + 4 * (nb - 9) + c) * 128)
                elif nb == 11:
                    for c in range(2):
                        fm_chunk(wi, wv, c, kbuf[2], 256 + c * 128)
                    tm_block(wi, wv, 256, 256, v_consume(vbuf[2], 256, 256))
                elif nb in (12, 13):
                    for c in range(4):
                        fm_chunk(wi, wv, c, qbuf, (24 + 4 * (nb - 12) + c) * 128)
                elif nb in (14, 15):
                    for c in range(4):
                        fm_chunk(wi, wv, c, kbuf[3 + nb - 14], c * 128)
                else:
                    tm_block(wi, wv, 0, 512, v_consume(vbuf[3 + nb - 16], 0, 512))
            for hq in range(8):
                sc.dma('sp', lambda h, hq=hq: h.dma_start(out=qbuf[(8 + hq) * 128:(9 + hq) * 128, :],
                                                          in_=BQ8[:, hq, :]),
                       reads=[('BQ8', 4 * (hq // 4), tt) for tt in range(NT)], writes=[('qbuf_b', hq)])
            for kv in range(2):
                sc.dma('sp', lambda h, kv=kv: h.dma_start(out=kbuf[2][kv * 128:(kv + 1) * 128, :],
                                                          in_=BK2[:, kv, :]),
                       reads=[('BK2', 0, tt) for tt in range(NT)], writes=[('kbuf_b', kv)])
            sc.barrier()
            for j in range(5):
                sc.coll(lambda h, j=j: h.collective_compute("AllGather", ALU.bypass, replica_groups=PAIRS,
                                                            ins=[kbuf[j].ap().opt()], outs=[kgat[j].ap().opt()]))
                sc.coll(lambda h, j=j: h.collective_compute("AllGather", ALU.bypass, replica_groups=PAIRS,
                                                            ins=[vbuf[j].ap().opt()], outs=[vgat[j].ap().opt()]))
            sc.barrier()
            if debug and l == 0:
                for j in range(5):
                    sc.dma('sp', lambda h, j=j: h.dma_start(out=dbg_k[j], in_=kgat[j][:, :]))
                    sc.dma('sp', lambda h, j=j: h.dma_start(out=dbg_v[j], in_=vgat[j][:, :]))
                sc.barrier()
            sc.emit()
        if stop_after == 'proj':
            break

        mixT = BIG
        with ExitStack() as pes:
            def psb(name, shape, dt):
                return pes.enter_context(nc.sbuf_tensor("%s_l%d" % (name, l), list(shape), dt))
            VT = [W[0][:, 0:8192].rearrange("p (k c) -> p k c", c=512),
                  W[0][:, 8192:16384].rearrange("p (k c) -> p k c", c=512)]
            w1f = W[1][:, :].bitcast(F32)
            BIAS = [w1f[:, 0:2048], w1f[:, 2048:4096]]
            S_SB = [w1f[:, 4096:6144], w1f[:, 6144:8192]]
            xtb = XTF[:, :].bitcast(BF16)
            KTt = [xtb[:, 0:2048], xtb[:, 2048:4096]]
            Pt = [xtb[:, 4096:6144], xtb[:, 6144:8192]]
            gbb = GB[:, :].bitcast(BF16)
            PTt = [gbb[:, 0:2048], gbb[:, 2048:4096]]
            QTt = [gbb[:, 4096:5120], gbb[:, 5120:6144]]
            h2f = H2[:, :, :].rearrange("p k t -> p (k t)").bitcast(F32)
            O1N = h2f[:, 0:256]
            OD = h2f[:, 256:512]
            LAMT = h2f[:, 512:1024]
            LAMP = h2f[:, 1024:1536]
            SUBG = h2f[:, 1536:1792]
            ODB = psb("ODB", [128, 256], BF16)
            ONB = [psb("ONB0", [128, 128], BF16), psb("ONB1", [128, 128], BF16)]
            SINK = psb("SINK", [128, 8], F32)
            LAMS = psb("LAMS", [128, 8], F32)
            lambda_init = 0.8 - 0.6 * math.exp(-0.3 * l)

            sc.dma('sp', lambda h: h.dma_start(out=SUBG, in_=subln_in[l:l + 1, :].partition_broadcast(128)),
                   writes=[('SUBG',)])
            sc.dma('sp', lambda h: h.dma_start(out=SINK[:, :], in_=sink_in[l:l + 1, :].partition_broadcast(128)),
                   writes=[('SINK',)])
            sc.dma('sp', lambda h: h.dma_start(out=LAMT, in_=lam_in[l:l + 1, :].partition_broadcast(128)),
                   writes=[('LAMT',)])
            lt = LAMT.rearrange("p (a d) -> p a d", a=4)
            lp = LAMP[:, 0:256].rearrange("p (a d) -> p a d", a=2)
            sc.op('dve', lambda h: h.tensor_tensor(out=lp[:, 0, :], in0=lt[:, 0, :], in1=lt[:, 1, :], op=ALU.mult),
                  reads=[('LAMT',)], writes=[('LAMP', 0)])
            sc.op('dve', lambda h: h.tensor_tensor(out=lp[:, 1, :], in0=lt[:, 2, :], in1=lt[:, 3, :], op=ALU.mult),
                  reads=[('LAMT',)], writes=[('LAMP', 1)])
            sc.op('dve', lambda h: h.reduce_sum(out=LAMS[:, 0:2], in_=lp, axis=AX.X),
                  reads=[('LAMP', 0), ('LAMP', 1)], writes=[('LAMS', 0)])
            sc.op('act', lambda h: h.activation(out=LAMS[:, 2:4], in_=LAMS[:, 0:2], func=ACTF.Exp),
                  reads=[('LAMS', 0)], writes=[('LAMS', 1)])
            sc.op('dve', lambda h: h.scalar_tensor_tensor(out=LAMS[:, 4:5], in0=LAMS[:, 3:4], scalar=-lambda_init,
                                                          in1=LAMS[:, 2:3], op0=ALU.add, op1=ALU.subtract),
                  reads=[('LAMS', 1)], writes=[('LAMS', 2)])
            neglam = LAMS[:, 4:5]

            S_PS_keys = [('PS', i) for i in range(4)]
            def s_ps(j):
                return PS[j][:, :]
            PT_PS = [PS[4][:, :].bitcast(BF16), PS[5][:, :].bitcast(BF16)]
            O_PS = PS[6]
            OT_PS = PS[7][:, :].bitcast(BF16)

            units = []
            for hh in range(8):
                units.append(dict(q=hh, kj=hh // 4, kr=(hh % 4) * 128, vj=hh // 4, vc=(hh % 4) * 128, vw=128,
                                  bias=('A', hh), sink=None, e=hh, kind='n'))
            for hq in range(8):
                kv = hq // 4
                units.append(dict(q=8 + hq, kj=2, kr=kv * 128, vj=2, vc=kv * 128, vw=128,
                                  bias=None, sink=None, e=8 + hq, kind='n'))
            for hq in range(8):
                kv = hq // 4
                units.append(dict(q=16 + hq, kj=2, kr=256 + kv * 128, vj=2, vc=256 + kv * 128, vw=128,
                                  bias=('C', hq), sink=hq, e=16 + hq, kind='n'))
            for hh in range(4):
                units.append(dict(q=24 + hh, kj=3, kr=hh * 128, vj=3 + hh // 2, vc=(hh % 2) * 256, vw=256,
                                  bias=('D', hh), sink=None, e=24 + 2 * hh, kind='d1'))
                units.append(dict(q=28 + hh, kj=4, kr=hh * 128, vj=3 + hh // 2, vc=(hh % 2) * 256, vw=256,
                                  bias=('D', hh), sink=None, e=24 + 2 * hh, kind='d2'))

            cur_v = [None, None]
            v_rr = [0]
            it = [0]
            for ui, u in enumerate(units):
                ub = ui % 2
                kt = KTt[ub]
                for r in range(2):
                    sc.dma('sp', lambda h, r=r, kt=kt, u=u: h.dma_start(
                        out=kt[:, r * 1024:(r + 1) * 1024],
                        in_=kgat[u['kj']][r * 512 + u['kr']:r * 512 + u['kr'] + 128, :]),
                        writes=[('KT', ub, r)])
                qt = QTt[ub]
                sc.dma('sp', lambda h, qt=qt, u=u: h.dma_start(out=qt, in_=qbuf[u['q'] * 128:(u['q'] + 1) * 128, :]),
                       writes=[('QT', ub)])
                if u['vj'] in cur_v:
                    vb = cur_v.index(u['vj'])
                else:
                    vb = v_rr[0] % 2
                    v_rr[0] += 1
                    cur_v[vb] = u['vj']
                    sc.dma('sp', lambda h, vb=vb, u=u: h.dma_start(
                        out=VT[vb], in_=vgat[u['vj']].ap().rearrange("(k p) c -> p k c", p=128)),
                        writes=[('VT', vb)])
                vt = VT[vb]
                vw = u['vw']
                for qt_i in range(NT):
                    b2 = it[0] % 2
                    it[0] += 1
                    qs = slice(qt_i * 128, (qt_i + 1) * 128)
                    if u['bias'] is not None:
                        kind, hh = u['bias']
                        if kind == 'A':
                            bsrc = biasA[fz(l), fz(hh), qs, :]
                        elif kind == 'C':
                            bsrc = biasC[fz(hh), qs, :]
                        else:
                            bsrc = biasD[fz(hh), qs, :]
                        sc.dma('sp', lambda h, bsrc=bsrc, b2=b2: h.dma_start(out=BIAS[b2], in_=bsrc),
                               writes=[('BIAS', b2)])
                    def qk(h, qt=qt, kt=kt, qs=qs):
                        inst = None
                        for j in range(4):
                            inst = h.matmul(s_ps(j), lhsT=qt[:, qs], rhs=kt[:, j * 512:(j + 1) * 512],
                                            start=True, stop=True)
                        return inst
                    sc.op('pe', qk, reads=[('QT', ub), ('KT', ub, 0), ('KT', ub, 1)], writes=S_PS_keys)
                    m, mk = sm()
                    nm, nmk = sm()
                    rs, rsk = sm()
                    pt_ = Pt[b2]
                    if u['bias'] is not None:
                        ssb = S_SB[b2]
                        for j in range(4):
                            sc.op('dve', lambda h, j=j, ssb=ssb, b2=b2: h.scalar_tensor_tensor(
                                out=ssb[:, j * 512:(j + 1) * 512], in0=s_ps(j), scalar=SCALE,
                                in1=BIAS[b2][:, j * 512:(j + 1) * 512], op0=ALU.mult, op1=ALU.add),
                                reads=[('PS', j), ('BIAS', b2)], writes=[('S_SB', b2, j)])
                        sc.op('dve', lambda h, ssb=ssb, m=m: h.reduce_max(out=m, in_=ssb[:, 0:S], axis=AX.X),
                              reads=[('S_SB', b2, j) for j in range(4)], writes=[mk])
                        if u['sink'] is not None:
                            m2, m2k = sm()
                            sk = SINK[:, u['sink']:u['sink'] + 1]
                            sc.op('dve', lambda h, m=m, m2=m2, sk=sk: h.tensor_tensor(out=m2, in0=m, in1=sk,
                                                                                     op=ALU.max),
                                  reads=[mk, ('SINK',)], writes=[m2k])
                            m, mk = m2, m2k
                        sc.op('dve', lambda h, m=m, nm=nm: h.tensor_scalar(out=nm, in0=m, scalar1=-1.0, scalar2=None,
                                                                         op0=ALU.mult),
                              reads=[mk], writes=[nmk])
                        sc.op('act', lambda h, ssb=ssb, pt_=pt_, nm=nm, rs=rs: h.activation(
                            out=pt_, in_=ssb[:, 0:S], func=ACTF.Exp, bias=nm, scale=1.0, accum_out=rs),
                            reads=[('S_SB', b2, j) for j in range(4)] + [nmk], writes=[('P', b2), rsk])
                    else:
                        mcols = [sm() for _ in range(4)]
                        for j in range(4):
                            sc.op('dve', lambda h, j=j, mc=mcols[j][0]: h.reduce_max(out=mc, in_=s_ps(j), axis=AX.X),
                                  reads=[('PS', j)], writes=[mcols[j][1]])
                        ma, mak = sm()
                        mb, mbk = sm()
                        sc.op('dve', lambda h, ma=ma, a0=mcols[0][0], a1=mcols[1][0]: h.tensor_tensor(out=ma, in0=a0, in1=a1,
                                                                      op=ALU.max),
                              reads=[mcols[0][1], mcols[1][1]], writes=[mak])
                        sc.op('dve', lambda h, mb=mb, a0=mcols[2][0], a1=mcols[3][0]: h.tensor_tensor(out=mb, in0=a0, in1=a1,
                                                                      op=ALU.max),
                              reads=[mcols[2][1], mcols[3][1]], writes=[mbk])
                        sc.op('dve', lambda h, m=m, ma=ma, mb=mb: h.tensor_tensor(out=m, in0=ma, in1=mb, op=ALU.max),
                              reads=[mak, mbk], writes=[mk])
                        sc.op('dve', lambda h, m=m, nm=nm: h.tensor_scalar(out=nm, in0=m, scalar1=-SCALE,
                                                                         scalar2=None, op0=ALU.mult),
                              reads=[mk], writes=[nmk])
                        rparts = [sm() for _ in range(4)]
                        for j in range(4):
                            sc.op('act', lambda h, j=j, pt_=pt_, nm=nm, rp=rparts[j][0]: h.activation(
                                out=pt_[:, j * 512:(j + 1) * 512], in_=s_ps(j), func=ACTF.Exp, bias=nm, scale=SCALE,
                                accum_out=rp),
                                reads=[('PS', j), nmk], writes=[('P', b2, j), rparts[j][1]])
                        ra, rak = sm()
                        rb, rbk = sm()
                        sc.op('dve', lambda h, ra=ra, a0=rparts[0][0], a1=rparts[1][0]: h.tensor_tensor(out=ra, in0=a0, in1=a1,
                                                                      op=ALU.add),
                              reads=[rparts[0][1], rparts[1][1]], writes=[rak])
                        sc.op('dve', lambda h, rb=rb, a0=rparts[2][0], a1=rparts[3][0]: h.tensor_tensor(out=rb, in0=a0, in1=a1,
                                                                      op=ALU.add),
                              reads=[rparts[2][1], rparts[3][1]], writes=[rbk])
                        sc.op('dve', lambda h, rs=rs, ra=ra, rb=rb: h.tensor_tensor(out=rs, in0=ra, in1=rb,
                                                                                    op=ALU.add),
                              reads=[rak, rbk], writes=[rsk])
                    pkeys = [('P', b2)] + [('P', b2, j) for j in range(4)]
                    if u['sink'] is not None:
                        e1, e1k = sm()
                        sk = SINK[:, u['sink']:u['sink'] + 1]
                        sc.op('act', lambda h, e1=e1, sk=sk, nm=nm: h.activation(out=e1, in_=sk, func=ACTF.Exp,
                                                                              bias=nm, scale=1.0),
                              reads=[('SINK',), nmk], writes=[e1k])
                        rs2, rs2k = sm()
                        sc.op('dve', lambda h, rs2=rs2, rs=rs, e1=e1: h.tensor_tensor(out=rs2, in0=rs, in1=e1,
                                                                                      op=ALU.add),
                              reads=[rsk, e1k], writes=[rs2k])
                        rs, rsk = rs2, rs2k
                    ri, rik = sm()
                    sc.op('dve', lambda h, ri=ri, rs=rs: h.reciprocal(out=ri, in_=rs), reads=[rsk], writes=[rik])
                    ptt = PTt[b2]
                    for g in range(2):
                        def tr(h, g=g, pt_=pt_):
                            inst = None
                            for j in range(8):
                                k_ = g * 8 + j
                                inst = h.transpose(out=PT_PS[g][:, j * 128:(j + 1) * 128],
                                                   in_=pt_[:, k_ * 128:(k_ + 1) * 128], identity=ident[:, :])
                            return inst
                        sc.op('pe', tr, reads=pkeys + [('ident',)], writes=[('PS', 4 + g)])
                        if g == 0:
                            sc.op('act', lambda h, g=g, ptt=ptt: h.activation(
                                out=ptt[:, g * 1024:(g + 1) * 1024], in_=PT_PS[g], func=ACTF.Copy),
                                reads=[('PS', 4 + g)], writes=[('PT', b2, g)])
                        else:
                            sc.op('dve', lambda h, g=g, ptt=ptt: h.tensor_copy(
                                out=ptt[:, g * 1024:(g + 1) * 1024], in_=PT_PS[g]),
                                reads=[('PS', 4 + g)], writes=[('PT', b2, g)])
                    def pv(h, ptt=ptt, vt=vt, u=u, vw=vw):
                        inst = None
                        for k_ in range(KT):
                            inst = h.matmul(O_PS[:, 0:vw], lhsT=ptt[:, k_ * 128:(k_ + 1) * 128],
                                            rhs=vt[:, k_, u['vc']:u['vc'] + vw],
                                            start=(k_ == 0), stop=(k_ == KT - 1))
                        return inst
                    sc.op('pe', pv, reads=[('PT', b2, 0), ('PT', b2, 1), ('VT', vb)], writes=[('PS', 6)])
                    if u['kind'] == 'n':
                        onb = ONB[b2]
                        sc.op('dve', lambda h, onb=onb, ri=ri: h.tensor_scalar(out=onb[:, :], in0=O_PS[:, 0:128],
                                                                              scalar1=ri, scalar2=None, op0=ALU.mult),
                              reads=[('PS', 6), rik], writes=[('ONB', b2)])
                        sc.op('pe', lambda h, onb=onb: h.transpose(out=OT_PS[:, 0:128], in_=onb[:, :],
                                                                   identity=ident[:, :]),
                              reads=[('ONB', b2), ('ident',)], writes=[('PS', 7)])
                        sc.op('dve', lambda h, u=u, qs=qs: h.tensor_copy(out=mixT[:, u['e'], qs], in_=OT_PS[:, 0:128]),
                              reads=[('PS', 7)], writes=[('mixT', u['e'], qt_i)])
                    elif u['kind'] == 'd1':
                        sc.op('dve', lambda h, ri=ri: h.tensor_scalar(out=O1N, in0=O_PS[:, 0:256], scalar1=ri,
                                                                      scalar2=None, op0=ALU.mult),
                              reads=[('PS', 6), rik], writes=[('O1N',)])
                        sc.dma('sp', lambda h, qs=qs: h.dma_start(out=ybuf[qs, 0:256], in_=O1N),
                               reads=[('O1N',)], writes=[('yb1', qt_i)])
                    else:
                        sc.dma('sp', lambda h, qs=qs: h.dma_start(out=O1N, in_=ybuf[qs, 0:256]),
                               reads=[('yb1', qt_i)], writes=[('O1N',)])
                        c2, c2k = sm()
                        sc.op('dve', lambda h, c2=c2, ri=ri: h.tensor_tensor(out=c2, in0=ri, in1=neglam, op=ALU.mult),
                              reads=[rik, ('LAMS', 2)], writes=[c2k])
                        sc.op('dve', lambda h, c2=c2: h.scalar_tensor_tensor(out=OD, in0=O_PS[:, 0:256],
                                                                             scalar=c2, in1=O1N,
                                                                             op0=ALU.mult, op1=ALU.add),
                              reads=[('PS', 6), c2k, ('O1N',)], writes=[('OD',)])
                        ss, ssk = sm()
                        sc.op('act', lambda h, ss=ss: h.activation(out=ODB[:, :], in_=OD, func=ACTF.Square,
                                                                   accum_out=ss),
                              reads=[('OD',)], writes=[('ODB',), ssk])
                        a_, ak_ = sm()
                        sc.op('act', lambda h, a_=a_, ss=ss: h.activation(out=a_, in_=ss, func=ACTF.Sqrt, bias=EPS,
                                                                         scale=1.0 / 256),
                              reads=[ssk], writes=[ak_])
                        r_, rk_ = sm()
                        sc.op('dve', lambda h, r_=r_, a_=a_: h.reciprocal(out=r_, in_=a_), reads=[ak_], writes=[rk_])
                        r3, r3k = sm()
                        sc.op('dve', lambda h, r3=r3, r_=r_: h.tensor_scalar(out=r3, in0=r_,
                                                                            scalar1=(1.0 - lambda_init),
                                                                            scalar2=None, op0=ALU.mult),
                              reads=[rk_], writes=[r3k])
                        sc.op('dve', lambda h, r3=r3: h.scalar_tensor_tensor(out=ODB[:, :], in0=OD, scalar=r3,
                                                                             in1=SUBG, op0=ALU.mult,
                                                                             op1=ALU.mult),
                              reads=[('OD',), r3k, ('SUBG',), ('ODB',)], writes=[('ODB',)])
                        def tr2(h):
                            h.transpose(out=OT_PS[:, 0:128], in_=ODB[:, 0:128], identity=ident[:, :])
                            return h.transpose(out=OT_PS[:, 128:256], in_=ODB[:, 128:256], identity=ident[:, :])
                        sc.op('pe', tr2, reads=[('ODB',), ('ident',)], writes=[('PS', 7)])
                        sc.op('dve', lambda h, u=u, qs=qs: h.tensor_copy(
                            out=mixT[:, u['e']:u['e'] + 2, qs],
                            in_=OT_PS[:, 0:256].rearrange("p (j t) -> p j t", j=2)),
                            reads=[('PS', 7)], writes=[('mixT', u['e'], qt_i)])
            sc.barrier()
            if debug and l == 0:
                sc.dma('sp', lambda h: h.dma_start(out=dbg_mix[:, :, :], in_=BIG[:, :, :]))
                sc.barrier()
            sc.emit()
        if stop_after == 'attn':
            break

        for db in range(8):
            src = w_out[fz(l)].rearrange("(kc p) n -> p kc n", p=128)[:, :, fz(db) * 512:(fz(db) + 1) * 512]
            wi, wv = load_w(src)
            for tt in range(NT):
                pi = (db * NT + tt) % 8
                def mm(h, pi=pi, tt=tt, wv=wv):
                    inst = None
                    for kc in range(KC):
                        inst = h.matmul(PS[pi][:, :], lhsT=mixT[:, kc, tt * 128:(tt + 1) * 128], rhs=wv[:, kc, :],
                                        start=(kc == 0), stop=(kc == KC - 1))
                    return inst
                sc.op('pe', mm, reads=[('W', wi), ('BIGALL',)], writes=[('PS', pi)])
                evac_y(pi, tt, db)
        load_gain(g_attn_post, l)
        for tt in range(NT):
            if os.environ.get("K_SKIP_POSTNORM"):
                break
            postnorm_tile(slice(tt * 128, (tt + 1) * 128), xsrc, xa, 0, 1, tt)
        sc.barrier()
        sc.emit()
        if stop_after == 'wout':
            break

        uT = BIG[:, :, :].rearrange("p k t -> p (k t)").rearrange("p (f t) -> p f t", t=256)
        w0f = W[0][:, :].bitcast(F32)
        SG = [w0f[:, 0:4096], w0f[:, 4096:8192]]
        WU = [W[1][:, k * 4096:(k + 1) * 4096] for k in range(4)]
        ucnt = [0]

        def load_unit(src3, a, b, uid, tb):
            n = ucnt[0]
            ucnt[0] += 1
            si_, k = n % 2, n % 4
            if tb == 0:
                sc.dma('sp', lambda h: h.dma_start(out=SG[si_].rearrange("p (a b) -> p a b", a=a), in_=src3),
                       writes=[('SG', si_)])
                if n % 2 == 0:
                    sc.op('act', lambda h: h.activation(out=WU[k], in_=SG[si_], func=ACTF.Copy),
                          reads=[('SG', si_)], writes=[('WU', k)])
                else:
                    sc.op('dve', lambda h: h.tensor_copy(out=WU[k], in_=SG[si_]),
                          reads=[('SG', si_)], writes=[('WU', k)])
                sc.dma('pool', lambda h: h.dma_start(out=wsc[uid], in_=WU[k]),
                       reads=[('WU', k)], writes=[('wsc', uid)])
            else:
                sc.dma('sp', lambda h: h.dma_start(out=WU[k], in_=wsc[uid]), writes=[('WU', k)])
            return k, WU[k].rearrange("p (a b) -> p a b", a=a)

        auxq[0] = 'pool'
        for tb in range(4):
            load_gain(g_mlp_pre, l)
            for i in range(2):
                tt = tb * 2 + i
                prenorm_tile(xa[tt * 128:(tt + 1) * 128, :], i,
                             lambda kc0, i=i: H2[:, kc0:kc0 + 8, i * 128:(i + 1) * 128], ('H2', i), psbase=i * 2)
            h2keys = [('H2', 0), ('H2', 1)]
            w1v = w1[fz(l)].rearrange("(kc p) n -> p kc n", p=128)
            for fb in range(32):
                pbase = 4 if fb % 2 == 0 else 0
                for q in range(4):
                    src = w1v[:, q * 8:(q + 1) * 8, fz(fb) * 512:(fz(fb) + 1) * 512]
                    k, wu = load_unit(src, 8, 512, fb * 4 + q, tb)
                    for c in range(4):
                        pi = pbase + c
                        def mm(h, pi=pi, c=c, wu=wu, q=q):
                            inst = None
                            for kk in range(8):
                                inst = h.matmul(PS[pi][:, 0:256], lhsT=wu[:, kk, c * 128:(c + 1) * 128],
                                                rhs=H2[:, q * 8 + kk, :],
                                                start=(q == 0 and kk == 0), stop=(q == 3 and kk == 7))
                            return inst
                        sc.op('pe', mm, reads=[('WU', k)] + h2keys, writes=[('PS', pi)])
                for c in range(4):
                    pi = pbase + c
                    fc = fb * 4 + c
                    si = next_stg()
                    sc.op('act', lambda h, pi=pi, si=si: h.activation(out=STG[si][:, 0:256], in_=PS[pi][:, 0:256],
                                                                      func=ACTF.Relu),
                          reads=[('PS', pi)], writes=[('STG', si)])
                    sc.op('pool', lambda h, si=si, fc=fc: h.tensor_tensor(out=uT[:, fc, :], in0=STG[si][:, 0:256],
                                                                          in1=STG[si][:, 0:256], op=ALU.mult),
                          reads=[('STG', si)], writes=[('uT', fc)])
            w2v = w2[fz(l)].rearrange("(fc p) n -> p fc n", p=128)
            for db in range(8):
                for j8 in range(16):
                    src = w2v[:, fz(j8) * 8:(fz(j8) + 1) * 8, fz(db) * 512:(fz(db) + 1) * 512]
                    k, wu = load_unit(src, 8, 512, 128 + db * 16 + j8, tb)
                    for i in range(2):
                        pi = (db % 2) * 2 + i
                        def mm(h, pi=pi, i=i, j8=j8, wu=wu):
                            inst = None
                            for k_ in range(8):
                                fc = j8 * 8 + k_
                                inst = h.matmul(PS[pi][:, :], lhsT=uT[:, fc, i * 128:(i + 1) * 128], rhs=wu[:, k_, :],
                                                start=(fc == 0), stop=(fc == 127))
                            return inst
                        sc.op('pe', mm, reads=[('WU', k)] + [('uT', fc) for fc in range(j8 * 8, j8 * 8 + 8)],
                              writes=[('PS', pi)])
                for i in range(2):
                    evac_y((db % 2) * 2 + i, tb * 2 + i, db)
            load_gain(g_mlp_post, l)
            for i in range(2):
                tt = tb * 2 + i
                postnorm_tile(slice(tt * 128, (tt + 1) * 128), xa, xfinal, 0, 1, tt)
            sc.barrier()
            sc.emit()
        auxq[0] = 'sp'
    sc.barrier()
    sc.emit()


def _t5_bucket_np(rel):
    import jax
    import jax.numpy as jnp
    cpu = jax.devices("cpu")[0]
    with jax.default_device(cpu):
        rel = jnp.asarray(rel, dtype=jnp.int32)
        nb = 16
        max_exact = 8
        base = jnp.where(rel > 0, nb, 0)
        n = jnp.abs(rel)
        n_f = jnp.maximum(n, 1).astype(jnp.float32)
        large = max_exact + (jnp.log(n_f / max_exact) / math.log(128 / max_exact) * (nb - max_exact)).astype(jnp.int32)
        large = jnp.minimum(large, nb - 1)
        return np.asarray(base + jnp.where(n < max_exact, n, large))


_CONST = {}


def _consts():
    if _CONST:
        return _CONST
    rel1d = np.arange(-(S - 1), S, dtype=np.int32)
    bucket1d = _t5_bucket_np(rel1d)
    per_half = []
    for hf in range(2):
        qpos = hf * NTOK + np.arange(NTOK)
        kpos = np.arange(S)
        rel = kpos[None, :] - qpos[:, None]
        bidx = bucket1d[rel + S - 1]
        cmask = np.abs(rel) <= 128
        r = qpos // 64
        c = qpos % 64
        rs = np.clip(r - 4, 0, 32 - 8)
        cs = np.clip(c - 8, 0, 64 - 16)
        kr = kpos // 64
        kc = kpos % 64
        amask = (kr[None, :] >= rs[:, None]) & (kr[None, :] < rs[:, None] + 8) & \
                (kc[None, :] >= cs[:, None]) & (kc[None, :] < cs[:, None] + 16)
        dr = np.clip(kr[None, :] - r[:, None] + 7, 0, 14)
        dc = np.clip(kc[None, :] - c[:, None] + 15, 0, 30)
        row = (qpos // 64).astype(np.float32)
        col = (qpos % 64).astype(np.float32)
        inv = (10000.0 ** (-np.arange(32, dtype=np.float32) / 32)).astype(np.float32)
        ang = np.concatenate([row[:, None] * inv, col[:, None] * inv], axis=-1).astype(np.float32)
        per_half.append(dict(bidx=bidx, cmask=cmask, amask=amask, dr=dr, dc=dc,
                             cos=np.cos(ang).astype(np.float32), sin=np.sin(ang).astype(np.float32)))
    _CONST['h'] = per_half
    _CONST['ident'] = np.eye(128, dtype=np.float32).astype(ml_dtypes.bfloat16)
    return _CONST


def make_in_maps(inputs):
    cst = _consts()
    f = lambda a: np.ascontiguousarray(np.asarray(a, dtype=np.float32))
    x = f(inputs["x"])
    shared = {k: f(inputs[k]) for k in ["ln_attn_pre", "ln_attn_post", "ln_mlp_pre", "ln_mlp_post", "w_in", "w_out",
                                        "w_mlp_in", "w_mlp_out", "ax_q_norm", "ax_k_norm", "sw_sink", "df_subln"]}
    shared["df_lambda"] = f(inputs["df_lambda"]).reshape(2, 512)
    shared["ident"] = cst['ident']
    rpb = f(inputs["na_rpb"])
    t5 = f(inputs["t5_table"])
    halves = []
    for hf in range(2):
        c = cst['h'][hf]
        bA = np.empty((2, 8, NTOK, S), np.float32)
        for l in range(2):
            for h in range(8):
                bA[l, h] = np.where(c['amask'], rpb[l, h][c['dr'], c['dc']], np.float32(NEG))
        bC = np.empty((8, NTOK, S), np.float32)
        for h in range(8):
            bC[h] = np.where(c['cmask'], t5[:, h][c['bidx']], np.float32(NEG))
        bD = np.empty((4, NTOK, S), np.float32)
        for h in range(4):
            bD[h] = t5[:, 8 + h][c['bidx']]
        halves.append(dict(biasA=bA, biasC=bC, biasD=bD, rope_cos=c['cos'], rope_sin=c['sin']))
    in_maps = []
    for core in range(8):
        b, hf = core // 2, core % 2
        m = dict(shared)
        m.update(halves[hf])
        m["x"] = np.ascontiguousarray(x[b, hf * NTOK:(hf + 1) * NTOK, :])
        in_maps.append(m)
    return in_maps


_NC = {}


def kernel(**inputs):
    if 'nc' not in _NC:
        _NC['nc'] = build_program()
    nc = _NC['nc']
    in_maps = make_in_maps(inputs)
    res = run_bass_kernel_spmd(nc, in_maps, core_ids=list(range(8)))
    outp = np.empty((4, S, D), np.float32)
    for core in range(8):
        b, hf = core // 2, core % 2
        outp[b, hf * NTOK:(hf + 1) * NTOK, :] = res.results[core]["out"]
    return outp
```

```python
import math
import os
from contextlib import ExitStack
import numpy as np
import ml_dtypes
import concourse.bass as bass
import concourse.mybir as mybir
from concourse.bass_utils import run_bass_kernel_spmd

F32 = mybir.dt.float32
BF16 = mybir.dt.bfloat16
ALU = mybir.AluOpType
ACTF = mybir.ActivationFunctionType
AX = mybir.AxisListType

D = 4096
KC = 32
NTOK = 1024
NT = 8
S = 2048
KT = 16
DIN = 9216
DFF = 16384
EPS = 1e-6
SCALE = 128 ** -0.5
NEG = -30000.0
PAIRS = [[0, 1], [2, 3], [4, 5], [6, 7]]


class Sched:
    ENG_ATTR = [('pe', 'tensor'), ('act', 'scalar'), ('dve', 'vector'), ('pool', 'gpsimd'), ('sp', 'sync')]

    def __init__(self, nc, es):
        self.nc = nc
        self.engs = [e for e, _ in self.ENG_ATTR]
        self.sem = {e: es.enter_context(nc.semaphore("s_" + e)) for e in self.engs}
        self.cnt = {e: 0 for e in self.engs}
        self.NDS = 40
        self.dsem = [es.enter_context(nc.semaphore("d%d" % i)) for i in range(self.NDS)]
        self.dcnt = [0] * self.NDS
        self.dnext = 0
        self.ops = []
        self.lastw = {}
        self.readers = {}
        self.waited = {e: {} for e in self.engs}

    def _deps(self, eng, reads, writes):
        toks = []
        for k in reads:
            t = self.lastw.get(k)
            if t is not None and not (t[0] == 'e' and t[1] == eng and eng == 'pe'):
                toks.append(t)
        for k in writes:
            t = self.lastw.get(k)
            if t is not None and not (t[0] == 'e' and t[1] == eng):
                toks.append(t)
            for t in self.readers.get(k, ()):
                if not (t[0] == 'e' and t[1] == eng):
                    toks.append(t)
        return toks

    def _commit(self, tok, reads, writes):
        for k in reads:
            self.readers.setdefault(k, []).append(tok)
        for k in writes:
            self.lastw[k] = tok
            self.readers[k] = []

    def op(self, eng, fn, reads=(), writes=()):
        toks = self._deps(eng, reads, writes)
        self.cnt[eng] += 1
        tok = ('e', eng, self.cnt[eng])
        self.ops.append((eng, fn, toks, tok, 1))
        self._commit(tok, reads, writes)
        return tok

    def dma(self, q, fn, reads=(), writes=()):
        toks = self._deps(q, reads, writes)
        i = self.dnext
        self.dnext = (i + 1) % self.NDS
        if self.dcnt[i] > 0:
            toks.append(('d', i, self.dcnt[i]))
        self.dcnt[i] += 16
        tok = ('d', i, self.dcnt[i])
        self.ops.append((q, fn, toks, tok, 16))
        self._commit(tok, reads, writes)
        return tok

    def coll(self, fn, reads=(), writes=()):
        return self.op('pool', fn, reads, writes)

    def barrier(self):
        toks = [('e', e, self.cnt[e]) for e in self.engs if self.cnt[e] > 0]
        toks += [('d', i, self.dcnt[i]) for i in range(self.NDS) if self.dcnt[i] > 0]
        for e in self.engs:
            self.ops.append((e, None, [t for t in toks if not (t[0] == 'e' and t[1] == e)], None, 0))
        self.lastw = {}
        self.readers = {}

    def _wait(self, h, e, t):
        key = (t[0], t[1])
        if self.waited[e].get(key, 0) >= t[2]:
            return
        self.waited[e][key] = t[2]
        sem = self.sem[t[1]] if t[0] == 'e' else self.dsem[t[1]]
        h.wait_ge(sem, t[2])

    def emit(self):
        nc = self.nc
        per = {e: [] for e in self.engs}
        for o in self.ops:
            per[o[0]].append(o)
        with nc.Block() as blk:
            for e, attr in self.ENG_ATTR:
                lst = per[e]
                if not lst:
                    continue

                def body(h, lst=lst, e=e):
                    for (_, fn, toks, tok, inc) in lst:
                        for t in toks:
                            self._wait(h, e, t)
                        if fn is None:
                            continue
                        inst = fn(h)
                        if tok[0] == 'e':
                            inst.then_inc(self.sem[e], 1)
                        else:
                            inst.then_inc(self.dsem[tok[1]], 16)
                getattr(blk, attr)(body)
        self.ops = []


def build_program(nlayers=2, debug=False, stop_after=None, fast=False):
    nc = bass.Bass("TRN2", target_bir_lowering=False)
    es = ExitStack()
    with es:
        _build(nc, es, nlayers, debug, stop_after, fast)
    return nc


def _build(nc, es, nlayers, debug, stop_after, fast=False):
    def din(name, shape, dt=F32):
        return nc.dram_tensor(name, list(shape), dt, kind="ExternalInput")

    dbgkind = "ExternalOutput" if debug else "Internal"

    x_in = din("x", [NTOK, D])
    g_attn_pre = din("ln_attn_pre", [2, D])
    g_attn_post = din("ln_attn_post", [2, D])
    g_mlp_pre = din("ln_mlp_pre", [2, D])
    g_mlp_post = din("ln_mlp_post", [2, D])
    if fast:
        w_in = din("w_in", [1, D, 512])
        w_out = din("w_out", [1, D, 512])
        w1 = din("w_mlp_in", [1, D, 512])
        w2 = din("w_mlp_out", [1, D, 512])
    else:
        w_in = din("w_in", [2, D, DIN])
        w_out = din("w_out", [2, D, D])
        w1 = din("w_mlp_in", [2, D, DFF])
        w2 = din("w_mlp_out", [2, DFF, D])
    qn_g = din("ax_q_norm", [2, 128])
    kn_g = din("ax_k_norm", [2, 128])
    sink_in = din("sw_sink", [2, 8])
    lam_in = din("df_lambda", [2, 512])
    subln_in = din("df_subln", [2, 256])
    if fast:
        biasA = din("biasA", [1, 1, NTOK, S])
        biasC = din("biasC", [1, NTOK, S])
        biasD = din("biasD", [1, NTOK, S])
    else:
        biasA = din("biasA", [2, 8, NTOK, S])
        biasC = din("biasC", [8, NTOK, S])
        biasD = din("biasD", [4, NTOK, S])
    fz = (lambda v: 0) if fast else (lambda v: v)
    cos_in = din("rope_cos", [NTOK, 64])
    sin_in = din("rope_sin", [NTOK, 64])
    ident_in = din("ident", [128, 128], BF16)
    out = nc.dram_tensor("out", [NTOK, D], F32, kind="ExternalOutput")

    xa = nc.dram_tensor("xa", [NTOK, D], F32, kind=dbgkind)
    xb = nc.dram_tensor("xb", [NTOK, D], F32, kind=dbgkind)
    ybuf = nc.dram_tensor("ybuf", [NTOK, D], F32, kind=dbgkind)
    qbuf = nc.dram_tensor("qbuf", [32 * 128, NTOK], BF16, kind=dbgkind)
    kbuf = [nc.dram_tensor("kbuf%d" % j, [512, NTOK], BF16) for j in range(5)]
    kgat = [nc.dram_tensor("kgat%d" % j, [1024, NTOK], BF16) for j in range(5)]
    vbuf = [nc.dram_tensor("vbuf%d" % j, [NTOK, 512], BF16) for j in range(5)]
    vgat = [nc.dram_tensor("vgat%d" % j, [S, 512], BF16) for j in range(5)]
    if debug:
        dbg_mix = nc.dram_tensor("dbg_mix", [128, KC, NTOK], BF16, kind="ExternalOutput")
        dbg_k = nc.dram_tensor("dbg_k", [5, 1024, NTOK], BF16, kind="ExternalOutput")
        dbg_v = nc.dram_tensor("dbg_v", [5, S, 512], BF16, kind="ExternalOutput")

    wsc = nc.dram_tensor("wsc", [256, 128, 4096], BF16)
    sc = Sched(nc, es)

    def sb(name, shape, dt):
        return es.enter_context(nc.sbuf_tensor(name, list(shape), dt))

    def ps(name, shape, dt):
        return es.enter_context(nc.psum_tensor(name, list(shape), dt))

    BIG = sb("BIG", [128, KC, NTOK], BF16)
    W = [sb("W0", [128, KC * 512], BF16), sb("W1", [128, KC * 512], BF16)]
    XTF = sb("XTF", [128, D], F32)
    GB = sb("GB", [128, D], F32)
    XN = sb("XN", [128, D], BF16)
    ident = sb("ident_sb", [128, 128], BF16)
    small = sb("small", [128, 64], F32)
    SSP = sb("SSP", [128, NT, 8], F32)
    STG = [sb("STG%d" % i, [128, 512], F32) for i in range(2)]
    STGB = [sb("STGB%d" % i, [128, 512], BF16) for i in range(2)]
    H2 = sb("H2", [128, KC, 256], BF16)
    PS = [ps("PS%d" % i, [128, 512], F32) for i in range(8)]

    auxq = ['sp']
    wcount = [0]

    def next_w():
        i = wcount[0] % 2
        wcount[0] += 1
        return i

    stg_i = [0]

    def next_stg():
        i = stg_i[0] % 2
        stg_i[0] += 1
        return i

    sm_i = [0]

    def sm():
        i = sm_i[0] % 64
        sm_i[0] += 1
        return small[:, i:i + 1], ('small', i)

    sc.dma('sp', lambda h: h.dma_start(out=ident[:, :], in_=ident_in[:, :]), writes=[('ident',)])

    def load_gain(gt, l):
        sc.dma(auxq[0], lambda h: h.dma_start(out=GB[:, :], in_=gt[l:l + 1, :].partition_broadcast(128)),
               writes=[('GB',)])

    def rstd_from_ss(ss_ap, ss_key, n):
        a, ak = sm()
        sc.op('act', lambda h: h.activation(out=a, in_=ss_ap, func=ACTF.Sqrt, bias=EPS, scale=1.0 / n),
              reads=[ss_key], writes=[ak])
        r, rk = sm()
        sc.op('dve', lambda h: h.reciprocal(out=r, in_=a), reads=[ak], writes=[rk])
        return r, rk

    def prenorm_tile(xsrc_ap, slot, dst_fn, dst_key, psbase):
        xt = XTF
        xn = XN
        sc.dma(auxq[0], lambda h: h.dma_start(out=xt[:, :], in_=xsrc_ap), writes=[('XTy',), ('XTx',)])
        ss, ssk = sm()
        sc.op('act', lambda h: h.activation(out=xn[:, :], in_=xt[:, :], func=ACTF.Square, accum_out=ss),
              reads=[('XTy',), ('XTx',)], writes=[('XN',), ssk])
        r, rk = rstd_from_ss(ss, ssk, D)
        sc.op('dve', lambda h: h.scalar_tensor_tensor(out=xn[:, :], in0=xt[:, :], scalar=r, in1=GB[:, :],
                                                      op0=ALU.mult, op1=ALU.mult),
              reads=[('XTy',), ('XTx',), rk, ('GB',)], writes=[('XN',)])
        for g in range(4):
            pst = PS[psbase + (g % 2)]
            pv = pst[:, :].bitcast(BF16)
            def tr(h, g=g, pv=pv):
                inst = None
                for j in range(8):
                    kc = g * 8 + j
                    inst = h.transpose(out=pv[:, j * 128:(j + 1) * 128], in_=xn[:, kc * 128:(kc + 1) * 128],
                                       identity=ident[:, :])
                return inst
            sc.op('pe', tr, reads=[('XN',), ('ident',)], writes=[('PS', psbase + (g % 2))])
            dst = dst_fn(g * 8)
            eng = 'act' if g % 2 == 0 else 'dve'
            if eng == 'act':
                sc.op('act', lambda h, dst=dst, pv=pv: h.activation(
                    out=dst, in_=pv.rearrange("p (j t) -> p j t", j=8), func=ACTF.Copy),
                    reads=[('PS', psbase + (g % 2))], writes=[dst_key])
            else:
                sc.op('dve', lambda h, dst=dst, pv=pv: h.tensor_copy(
                    out=dst, in_=pv.rearrange("p (j t) -> p j t", j=8)),
                    reads=[('PS', psbase + (g % 2))], writes=[dst_key])

    def load_w(src_ap):
        i = next_w()
        wv = W[i][:, :].rearrange("p (k n) -> p k n", n=512)
        sc.dma('pool', lambda h: h.dma_start(out=wv, in_=src_ap), writes=[('W', i)])
        return i, wv

    def postnorm_tile(tt_rows, xsrc, xdst, slot_y, slot_x, ssp_tt):
        ss, ssk = sm()
        sc.op('dve', lambda h: h.reduce_sum(out=ss, in_=SSP[:, ssp_tt, :], axis=AX.X),
              reads=[('SSP', ssp_tt)], writes=[ssk])
        r, rk = rstd_from_ss(ss, ssk, D)
        for c in range(2):
            cs = slice(c * 2048, (c + 1) * 2048)
            yt = XTF[:, 0:2048]
            xt = XTF[:, 2048:4096]
            sc.dma(auxq[0], lambda h, cs=cs: h.dma_start(out=yt, in_=ybuf[tt_rows, cs]),
                   reads=[('ybuf', tt_rows.start, db) for db in range(8)], writes=[('XTy',)])
            sc.dma(auxq[0], lambda h, cs=cs: h.dma_start(out=xt, in_=xsrc[tt_rows, cs]), writes=[('XTx',)])
            sc.op('dve', lambda h, cs=cs: h.scalar_tensor_tensor(out=yt, in0=yt, scalar=r, in1=GB[:, cs],
                                                             op0=ALU.mult, op1=ALU.mult),
                  reads=[('XTy',), rk, ('GB',)], writes=[('XTy',)])
            sc.op('pool', lambda h: h.tensor_tensor(out=xt, in0=xt, in1=yt, op=ALU.add),
                  reads=[('XTy',), ('XTx',)], writes=[('XTx',)])
            sc.dma(auxq[0], lambda h, cs=cs: h.dma_start(out=xdst[tt_rows, cs], in_=xt),
                   reads=[('XTx',)], writes=[('xdst', tt_rows.start, c)])

    def evac_y(pidx, tt, db, ncols=512):
        si = next_stg()
        junk = STGB[si]
        lvl = int(os.environ.get("K_P3", "9"))
        if lvl < 1:
            return
        sc.op('act', lambda h: h.activation(out=junk[:, :], in_=PS[pidx][:, :], func=ACTF.Square,
                                            accum_out=SSP[:, tt, db:db + 1]),
              reads=[('PS', pidx)], writes=[('STGB', si), ('SSP', tt)])
        if lvl < 2:
            return
        st = STG[si]
        sc.op('dve', lambda h: h.tensor_scalar(out=st[:, :], in0=PS[pidx][:, :], scalar1=1.0, scalar2=None, op0=ALU.mult),
              reads=[('PS', pidx), ('STGB', si)], writes=[('STG', si)])
        if lvl < 3:
            return
        rows = slice(tt * 128, (tt + 1) * 128)
        sc.dma(auxq[0], lambda h: h.dma_start(out=ybuf[rows, db * 512:(db + 1) * 512], in_=st[:, :]),
               reads=[('STG', si)], writes=[('ybuf', rows.start, db)])

    for l in range(nlayers):
        xsrc = x_in if l == 0 else xb
        xfinal = out if l == nlayers - 1 else xb

        load_gain(g_attn_pre, l)
        for tt in range(NT):
            prenorm_tile(xsrc[tt * 128:(tt + 1) * 128, :], tt % 2,
                         lambda kc0, tt=tt: BIG[:, kc0:kc0 + 8, tt * 128:(tt + 1) * 128],
                         ('BIG', tt), psbase=(tt % 2) * 2)
        sc.barrier()
        sc.emit()

        hT = BIG
        allhT = [('BIG', tt) for tt in range(NT)]
        with ExitStack() as pes:
            def psb(name, shape, dt):
                return pes.enter_context(nc.sbuf_tensor("%s_l%d" % (name, l), list(shape), dt))
            xtb = XTF[:, :].bitcast(BF16)
            gbb = GB[:, :].bitcast(BF16)
            BQ8 = xtb.rearrange("p (h t) -> p h t", h=8)
            BK2 = gbb[:, 0:2048].rearrange("p (h t) -> p h t", h=2)
            XQ = GB[:, 1024:1536]
            XQ2 = GB[:, 1536:2048]
            T1 = GB[:, 2048:2304]
            T2 = GB[:, 2304:2560]
            GQ = GB[:, 2560:2688]
            GK = GB[:, 2688:2816]
            COS = GB[:, 2816:3328].rearrange("p (t c) -> p t c", t=NT)
            SIN = GB[:, 3328:3840].rearrange("p (t c) -> p t c", t=NT)
            SS8 = GB[:, 3840:3848]
            SS8b = GB[:, 3848:3856]
            XR = XN[:, 0:512]
            sc.dma('sp', lambda h: h.dma_start(out=GQ, in_=qn_g[l:l + 1, :].partition_broadcast(128)),
                   writes=[('GQ',)])
            sc.dma('sp', lambda h: h.dma_start(out=GK, in_=kn_g[l:l + 1, :].partition_broadcast(128)),
                   writes=[('GK',)])
            sc.dma('sp', lambda h: h.dma_start(out=COS,
                                               in_=cos_in.ap().rearrange("(t p) c -> p t c", p=128)),
                   writes=[('COS',)])
            sc.dma('sp', lambda h: h.dma_start(out=SIN,
                                               in_=sin_in.ap().rearrange("(t p) c -> p t c", p=128)),
                   writes=[('SIN',)])

            acc_i = [0]

            def next_acc():
                i = acc_i[0] % 6
                acc_i[0] += 1
                return i

            def fm_chunk(wi, wv, c, dst_tensor, dst_row):
                for half in range(2):
                    pi = next_acc()
                    def mm(h, pi=pi, half=half):
                        inst = None
                        for kc in range(KC):
                            inst = h.matmul(PS[pi][:, :], lhsT=wv[:, kc, c * 128:(c + 1) * 128],
                                            rhs=hT[:, kc, half * 512:(half + 1) * 512],
                                            start=(kc == 0), stop=(kc == KC - 1))
                        return inst
                    sc.op('pe', mm, reads=[('W', wi)] + allhT, writes=[('PS', pi)])
                    si = next_stg()
                    if half == 0:
                        sc.op('act', lambda h, pi=pi, si=si: h.activation(out=STGB[si][:, :], in_=PS[pi][:, :],
                                                                          func=ACTF.Copy),
                              reads=[('PS', pi)], writes=[('STGB', si)])
                    else:
                        sc.op('dve', lambda h, pi=pi, si=si: h.tensor_copy(out=STGB[si][:, :], in_=PS[pi][:, :]),
                              reads=[('PS', pi)], writes=[('STGB', si)])
                    sc.dma('sp', lambda h, si=si, half=half: h.dma_start(
                        out=dst_tensor[dst_row:dst_row + 128, half * 512:(half + 1) * 512], in_=STGB[si][:, :]),
                        reads=[('STGB', si)], writes=[('fm', id(dst_tensor), dst_row, half)])

            def tm_block(wi, wv, c0, ncols, consume):
                for tt in range(NT):
                    pi = next_acc()
                    def mm(h, pi=pi, tt=tt):
                        inst = None
                        for kc in range(KC):
                            inst = h.matmul(PS[pi][:, 0:ncols], lhsT=hT[:, kc, tt * 128:(tt + 1) * 128],
                                            rhs=wv[:, kc, c0:c0 + ncols],
                                            start=(kc == 0), stop=(kc == KC - 1))
                        return inst
                    sc.op('pe', mm, reads=[('W', wi), ('BIG', tt)], writes=[('PS', pi)])
                    consume(tt, pi)

            def v_consume(vt, col0, ncols):
                def f(tt, pi):
                    si = next_stg()
                    sc.op('act', lambda h: h.activation(out=STGB[si][:, 0:ncols], in_=PS[pi][:, 0:ncols],
                                                        func=ACTF.Copy),
                          reads=[('PS', pi)], writes=[('STGB', si)])
                    sc.dma('sp', lambda h: h.dma_start(out=vt[tt * 128:(tt + 1) * 128, col0:col0 + ncols],
                                                       in_=STGB[si][:, 0:ncols]),
                           reads=[('STGB', si)], writes=[('vout', id(vt), tt, col0)])
                return f

            def rope_consume(nh, gtile, gkey, dst3, dkey, head0):
                W_ = nh * 128
                def f(tt, pi):
                    x3 = XQ[:, 0:W_].rearrange("p (h d) -> p h d", h=nh)
                    sc.op('act', lambda h: h.activation(out=XQ[:, 0:W_], in_=PS[pi][:, 0:W_], func=ACTF.Copy),
                          reads=[('PS', pi)], writes=[('XQ',)])
                    sc.op('dve', lambda h: h.tensor_tensor(out=XQ2[:, 0:W_], in0=XQ[:, 0:W_], in1=XQ[:, 0:W_],
                                                           op=ALU.mult),
                          reads=[('XQ',)], writes=[('XQ2',)])
                    sc.op('dve', lambda h: h.reduce_sum(out=SS8[:, 0:nh],
                                                        in_=XQ2[:, 0:W_].rearrange("p (h d) -> p h d", h=nh),
                                                        axis=AX.X),
                          reads=[('XQ2',)], writes=[('SS8',)])
                    sc.op('act', lambda h: h.activation(out=SS8b[:, 0:nh], in_=SS8[:, 0:nh], func=ACTF.Sqrt,
                                                        bias=EPS, scale=1.0 / 128),
                          reads=[('SS8',)], writes=[('SS8b',)])
                    sc.op('dve', lambda h: h.reciprocal(out=SS8[:, 0:nh], in_=SS8b[:, 0:nh]),
                          reads=[('SS8b',)], writes=[('SS8',)])
                    sc.op('dve', lambda h: h.tensor_tensor(
                        out=x3, in0=x3, in1=SS8[:, 0:nh, None].to_broadcast([128, nh, 128]), op=ALU.mult),
                        reads=[('XQ',), ('SS8',)], writes=[('XQ',)])
                    sc.op('pool', lambda h: h.tensor_tensor(
                        out=x3, in0=x3, in1=gtile[:, None, :].to_broadcast([128, nh, 128]), op=ALU.mult),
                        reads=[('XQ',), gkey], writes=[('XQ',)])
                    x4 = XQ[:, 0:W_].rearrange("p (h i two) -> p h i two", h=nh, two=2)
                    x1 = x4[:, :, :, 0]
                    x2 = x4[:, :, :, 1]
                    cb = COS[:, tt:tt + 1, :].to_broadcast([128, nh, 64])
                    sb_ = SIN[:, tt:tt + 1, :].to_broadcast([128, nh, 64])
                    t1 = T1[:, 0:nh * 64].rearrange("p (h i) -> p h i", h=nh)
                    t2 = T2[:, 0:nh * 64].rearrange("p (h i) -> p h i", h=nh)
                    r4 = XR[:, 0:W_].rearrange("p (h i two) -> p h i two", h=nh, two=2)
                    sc.op('dve', lambda h: h.tensor_tensor(out=t1, in0=x1, in1=cb, op=ALU.mult),
                          reads=[('XQ',), ('COS',)], writes=[('T1',)])
                    sc.op('pool', lambda h: h.tensor_tensor(out=t2, in0=x2, in1=sb_, op=ALU.mult),
                          reads=[('XQ',), ('SIN',)], writes=[('T2',)])
                    sc.op('dve', lambda h: h.tensor_tensor(out=r4[:, :, :, 0], in0=t1, in1=t2, op=ALU.subtract),
                          reads=[('T1',), ('T2',)], writes=[('XR', 0)])
                    sc.op('dve', lambda h: h.tensor_tensor(out=t1, in0=x1, in1=sb_, op=ALU.mult),
                          reads=[('XQ',), ('SIN',), ('XR', 0)], writes=[('T1',)])
                    sc.op('pool', lambda h: h.tensor_tensor(out=t2, in0=x2, in1=cb, op=ALU.mult),
                          reads=[('XQ',), ('COS',), ('XR', 0)], writes=[('T2',)])
                    sc.op('dve', lambda h: h.tensor_tensor(out=r4[:, :, :, 1], in0=t1, in1=t2, op=ALU.add),
                          reads=[('T1',), ('T2',)], writes=[('XR', 1)])
                    pv = PS[6 + (tt % 2)][:, :].bitcast(BF16)
                    def tr(h):
                        inst = None
                        for j in range(nh):
                            inst = h.transpose(out=pv[:, j * 128:(j + 1) * 128], in_=XR[:, j * 128:(j + 1) * 128],
                                               identity=ident[:, :])
                        return inst
                    sc.op('pe', tr, reads=[('XR', 0), ('XR', 1), ('ident',)], writes=[('PS', 6 + (tt % 2))])
                    sc.op('act', lambda h: h.activation(
                        out=dst3[:, head0:head0 + nh, tt * 128:(tt + 1) * 128],
                        in_=pv[:, 0:W_].rearrange("p (j t) -> p j t", j=nh), func=ACTF.Copy),
                        reads=[('PS', 6 + (tt % 2))], writes=[(dkey, head0, tt)])
                return f

            for nb in range(18):
                src = w_in[fz(l)].rearrange("(kc p) n -> p kc n", p=128)[:, :, fz(nb) * 512:(fz(nb) + 1) * 512]
                wi, wv = load_w(src)
                if nb in (0, 1):
                    for c in range(4):
                        fm_chunk(wi, wv, c, qbuf, (4 * nb + c) * 128)
                elif nb in (2, 3):
                    for c in range(4):
                        fm_chunk(wi, wv, c, kbuf[nb - 2], c * 128)
                elif nb in (4, 5):
                    tm_block(wi, wv, 0, 512, v_consume(vbuf[nb - 4], 0, 512))
                elif nb in (6, 7):
                    tm_block(wi, wv, 0, 512, rope_consume(4, GQ, ('GQ',), BQ8, 'BQ8', 4 * (nb - 6)))
                elif nb == 8:
                    tm_block(wi, wv, 0, 256, rope_consume(2, GK, ('GK',), BK2, 'BK2', 0))
                    tm_block(wi, wv, 256, 256, v_consume(vbuf[2], 0, 256))
                elif nb in (9, 10):
                    for c in range(4):
                        fm_chunk(wi, wv, c, qbuf, (16 + 4 * (nb - 9) + c) * 128)
                elif nb == 11:
                    for c in range(2):
                        fm_chunk(wi, wv, c, kbuf[2], 256 + c * 128)
                    tm_block(wi, wv, 256, 256, v_consume(vbuf[2], 256, 256))
                elif nb in (12, 13):
                    for c in range(4):
                        fm_chunk(wi, wv, c, qbuf, (24 + 4 * (nb - 12) + c) * 128)
                elif nb in (14, 15):
                    for c in range(4):
                        fm_chunk(wi, wv, c, kbuf[3 + nb - 14], c * 128)
                else:
                    tm_block(wi, wv, 0, 512, v_consume(vbuf[3 + nb - 16], 0, 512))
            for hq in range(8):
                sc.dma('sp', lambda h, hq=hq: h.dma_start(out=qbuf[(8 + hq) * 128:(9 + hq) * 128, :],
                                                          in_=BQ8[:, hq, :]),
                       reads=[('BQ8', 4 * (hq // 4), tt) for tt in range(NT)], writes=[('qbuf_b', hq)])
            for kv in range(2):
                sc.dma('sp', lambda h, kv=kv: h.dma_start(out=kbuf[2][kv * 128:(kv + 1) * 128, :],
                                                          in_=BK2[:, kv, :]),
                       reads=[('BK2', 0, tt) for tt in range(NT)], writes=[('kbuf_b', kv)])
            sc.barrier()
            for j in range(5):
                sc.coll(lambda h, j=j: h.collective_compute("AllGather", ALU.bypass, replica_groups=PAIRS,
                                                            ins=[kbuf[j].ap().opt()], outs=[kgat[j].ap().opt()]))
                sc.coll(lambda h, j=j: h.collective_compute("AllGather", ALU.bypass, replica_groups=PAIRS,
                                                            ins=[vbuf[j].ap().opt()], outs=[vgat[j].ap().opt()]))
            sc.barrier()
            if debug and l == 0:
                for j in range(5):
                    sc.dma('sp', lambda h, j=j: h.dma_start(out=dbg_k[j], in_=kgat[j][:, :]))
                    sc.dma('sp', lambda h, j=j: h.dma_start(out=dbg_v[j], in_=vgat[j][:, :]))
                sc.barrier()
            sc.emit()
        if stop_after == 'proj':
            break

        mixT = BIG
        with ExitStack() as pes:
            def psb(name, shape, dt):
                return pes.enter_context(nc.sbuf_tensor("%s_l%d" % (name, l), list(shape), dt))
            VT = [W[0][:, 0:8192].rearrange("p (k c) -> p k c", c=512),
                  W[0][:, 8192:16384].rearrange("p (k c) -> p k c", c=512)]
            w1f = W[1][:, :].bitcast(F32)
            BIAS = [w1f[:, 0:2048], w1f[:, 2048:4096]]
            S_SB = [w1f[:, 4096:6144], w1f[:, 6144:8192]]
            xtb = XTF[:, :].bitcast(BF16)
            KTt = [xtb[:, 0:2048], xtb[:, 2048:4096]]
            Pt = [xtb[:, 4096:6144], xtb[:, 6144:8192]]
            gbb = GB[:, :].bitcast(BF16)
            PTt = [gbb[:, 0:2048], gbb[:, 2048:4096]]
            QTt = [gbb[:, 4096:5120], gbb[:, 5120:6144]]
            h2f = H2[:, :, :].rearrange("p k t -> p (k t)").bitcast(F32)
            O1N = h2f[:, 0:256]
            OD = h2f[:, 256:512]
            LAMT = h2f[:, 512:1024]
            LAMP = h2f[:, 1024:1536]
            SUBG = h2f[:, 1536:1792]
            ODB = psb("ODB", [128, 256], BF16)
            ONB = [psb("ONB0", [128, 128], BF16), psb("ONB1", [128, 128], BF16)]
            SINK = psb("SINK", [128, 8], F32)
            LAMS = psb("LAMS", [128, 8], F32)
            lambda_init = 0.8 - 0.6 * math.exp(-0.3 * l)

            sc.dma('sp', lambda h: h.dma_start(out=SUBG, in_=subln_in[l:l + 1, :].partition_broadcast(128)),
                   writes=[('SUBG',)])
            sc.dma('sp', lambda h: h.dma_start(out=SINK[:, :], in_=sink_in[l:l + 1, :].partition_broadcast(128)),
                   writes=[('SINK',)])
            sc.dma('sp', lambda h: h.dma_start(out=LAMT, in_=lam_in[l:l + 1, :].partition_broadcast(128)),
                   writes=[('LAMT',)])
            lt = LAMT.rearrange("p (a d) -> p a d", a=4)
            lp = LAMP[:, 0:256].rearrange("p (a d) -> p a d", a=2)
            sc.op('dve', lambda h: h.tensor_tensor(out=lp[:, 0, :], in0=lt[:, 0, :], in1=lt[:, 1, :], op=ALU.mult),
                  reads=[('LAMT',)], writes=[('LAMP', 0)])
            sc.op('dve', lambda h: h.tensor_tensor(out=lp[:, 1, :], in0=lt[:, 2, :], in1=lt[:, 3, :], op=ALU.mult),
                  reads=[('LAMT',)], writes=[('LAMP', 1)])
            sc.op('dve', lambda h: h.reduce_sum(out=LAMS[:, 0:2], in_=lp, axis=AX.X),
                  reads=[('LAMP', 0), ('LAMP', 1)], writes=[('LAMS', 0)])
            sc.op('act', lambda h: h.activation(out=LAMS[:, 2:4], in_=LAMS[:, 0:2], func=ACTF.Exp),
                  reads=[('LAMS', 0)], writes=[('LAMS', 1)])
            sc.op('dve', lambda h: h.scalar_tensor_tensor(out=LAMS[:, 4:5], in0=LAMS[:, 3:4], scalar=-lambda_init,
                                                          in1=LAMS[:, 2:3], op0=ALU.add, op1=ALU.subtract),
                  reads=[('LAMS', 1)], writes=[('LAMS', 2)])
            neglam = LAMS[:, 4:5]

            S_PS_keys = [('PS', i) for i in range(4)]
            def s_ps(j):
                return PS[j][:, :]
            PT_PS = [PS[4][:, :].bitcast(BF16), PS[5][:, :].bitcast(BF16)]
            O_PS = PS[6]
            OT_PS = PS[7][:, :].bitcast(BF16)

            units = []
            for hh in range(8):
                units.append(dict(q=hh, kj=hh // 4, kr=(hh % 4) * 128, vj=hh // 4, vc=(hh % 4) * 128, vw=128,
                                  bias=('A', hh), sink=None, e=hh, kind='n'))
            for hq in range(8):
                kv = hq // 4
                units.append(dict(q=8 + hq, kj=2, kr=kv * 128, vj=2, vc=kv * 128, vw=128,
                                  bias=None, sink=None, e=8 + hq, kind='n'))
            for hq in range(8):
                kv = hq // 4
                units.append(dict(q=16 + hq, kj=2, kr=256 + kv * 128, vj=2, vc=256 + kv * 128, vw=128,
                                  bias=('C', hq), sink=hq, e=16 + hq, kind='n'))
            for hh in range(4):
                units.append(dict(q=24 + hh, kj=3, kr=hh * 128, vj=3 + hh // 2, vc=(hh % 2) * 256, vw=256,
                                  bias=('D', hh), sink=None, e=24 + 2 * hh, kind='d1'))
                units.append(dict(q=28 + hh, kj=4, kr=hh * 128, vj=3 + hh // 2, vc=(hh % 2) * 256, vw=256,
                                  bias=('D', hh), sink=None, e=24 + 2 * hh, kind='d2'))

            cur_v = [None, None]
            v_rr = [0]
            it = [0]
            for ui, u in enumerate(units):
                ub = ui % 2
                kt = KTt[ub]
                for r in range(2):
                    sc.dma('sp', lambda h, r=r, kt=kt, u=u: h.dma_start(
                        out=kt[:, r * 1024:(r + 1) * 1024],
                        in_=kgat[u['kj']][r * 512 + u['kr']:r * 512 + u['kr'] + 128, :]),
                        writes=[('KT', ub, r)])
                qt = QTt[ub]
                sc.dma('sp', lambda h, qt=qt, u=u: h.dma_start(out=qt, in_=qbuf[u['q'] * 128:(u['q'] + 1) * 128, :]),
                       writes=[('QT', ub)])
                if u['vj'] in cur_v:
                    vb = cur_v.index(u['vj'])
                else:
                    vb = v_rr[0] % 2
                    v_rr[0] += 1
                    cur_v[vb] = u['vj']
                    sc.dma('sp', lambda h, vb=vb, u=u: h.dma_start(
                        out=VT[vb], in_=vgat[u['vj']].ap().rearrange("(k p) c -> p k c", p=128)),
                        writes=[('VT', vb)])
                vt = VT[vb]
                vw = u['vw']
                for qt_i in range(NT):
                    b2 = it[0] % 2
                    it[0] += 1
                    qs = slice(qt_i * 128, (qt_i + 1) * 128)
                    if u['bias'] is not None:
                        kind, hh = u['bias']
                        if kind == 'A':
                            bsrc = biasA[fz(l), fz(hh), qs, :]
                        elif kind == 'C':
                            bsrc = biasC[fz(hh), qs, :]
                        else:
                            bsrc = biasD[fz(hh), qs, :]
                        sc.dma('sp', lambda h, bsrc=bsrc, b2=b2: h.dma_start(out=BIAS[b2], in_=bsrc),
                               writes=[('BIAS', b2)])
                    def qk(h, qt=qt, kt=kt, qs=qs):
                        inst = None
                        for j in range(4):
                            inst = h.matmul(s_ps(j), lhsT=qt[:, qs], rhs=kt[:, j * 512:(j + 1) * 512],
                                            start=True, stop=True)
                        return inst
                    sc.op('pe', qk, reads=[('QT', ub), ('KT', ub, 0), ('KT', ub, 1)], writes=S_PS_keys)
                    m, mk = sm()
                    nm, nmk = sm()
                    rs, rsk = sm()
                    pt_ = Pt[b2]
                    if u['bias'] is not None:
                        ssb = S_SB[b2]
                        for j in range(4):
                            sc.op('dve', lambda h, j=j, ssb=ssb, b2=b2: h.scalar_tensor_tensor(
                                out=ssb[:, j * 512:(j + 1) * 512], in0=s_ps(j), scalar=SCALE,
                                in1=BIAS[b2][:, j * 512:(j + 1) * 512], op0=ALU.mult, op1=ALU.add),
                                reads=[('PS', j), ('BIAS', b2)], writes=[('S_SB', b2, j)])
                        sc.op('dve', lambda h, ssb=ssb, m=m: h.reduce_max(out=m, in_=ssb[:, 0:S], axis=AX.X),
                              reads=[('S_SB', b2, j) for j in range(4)], writes=[mk])
                        if u['sink'] is not None:
                            m2, m2k = sm()
                            sk = SINK[:, u['sink']:u['sink'] + 1]
                            sc.op('dve', lambda h, m=m, m2=m2, sk=sk: h.tensor_tensor(out=m2, in0=m, in1=sk,
                                                                                     op=ALU.max),
                                  reads=[mk, ('SINK',)], writes=[m2k])
                            m, mk = m2, m2k
                        sc.op('dve', lambda h, m=m, nm=nm: h.tensor_scalar(out=nm, in0=m, scalar1=-1.0, scalar2=None,
                                                                         op0=ALU.mult),
                              reads=[mk], writes=[nmk])
                        sc.op('act', lambda h, ssb=ssb, pt_=pt_, nm=nm, rs=rs: h.activation(
                            out=pt_, in_=ssb[:, 0:S], func=ACTF.Exp, bias=nm, scale=1.0, accum_out=rs),
                            reads=[('S_SB', b2, j) for j in range(4)] + [nmk], writes=[('P', b2), rsk])
                    else:
                        mcols = [sm() for _ in range(4)]
                        for j in range(4):
                            sc.op('dve', lambda h, j=j, mc=mcols[j][0]: h.reduce_max(out=mc, in_=s_ps(j), axis=AX.X),
                                  reads=[('PS', j)], writes=[mcols[j][1]])
                        ma, mak = sm()
                        mb, mbk = sm()
                        sc.op('dve', lambda h, ma=ma, a0=mcols[0][0], a1=mcols[1][0]: h.tensor_tensor(out=ma, in0=a0, in1=a1,
                                                                      op=ALU.max),
                              reads=[mcols[0][1], mcols[1][1]], writes=[mak])
                        sc.op('dve', lambda h, mb=mb, a0=mcols[2][0], a1=mcols[3][0]: h.tensor_tensor(out=mb, in0=a0, in1=a1,
                                                                      op=ALU.max),
                              reads=[mcols[2][1], mcols[3][1]], writes=[mbk])
                        sc.op('dve', lambda h, m=m, ma=ma, mb=mb: h.tensor_tensor(out=m, in0=ma, in1=mb, op=ALU.max),
                              reads=[mak, mbk], writes=[mk])
                        sc.op('dve', lambda h, m=m, nm=nm: h.tensor_scalar(out=nm, in0=m, scalar1=-SCALE,
                                                                         scalar2=None, op0=ALU.mult),
                              reads=[mk], writes=[nmk])
                        rparts = [sm() for _ in range(4)]
                        for j in range(4):
                            sc.op('act', lambda h, j=j, pt_=pt_, nm=nm, rp=rparts[j][0]: h.activation(
                                out=pt_[:, j * 512:(j + 1) * 512], in_=s_ps(j), func=ACTF.Exp, bias=nm, scale=SCALE,
                                accum_out=rp),
                                reads=[('PS', j), nmk], writes=[('P', b2, j), rparts[j][1]])
                        ra, rak = sm()
                        rb, rbk = sm()
                        sc.op('dve', lambda h, ra=ra, a0=rparts[0][0], a1=rparts[1][0]: h.tensor_tensor(out=ra, in0=a0, in1=a1,
                                                                      op=ALU.add),
                              reads=[rparts[0][1], rparts[1][1]], writes=[rak])
                        sc.op('dve', lambda h, rb=rb, a0=rparts[2][0], a1=rparts[3][0]: h.tensor_tensor(out=rb, in0=a0, in1=a1,
                                                                      op=ALU.add),
                              reads=[rparts[2][1], rparts[3][1]], writes=[rbk])
                        sc.op('dve', lambda h, rs=rs, ra=ra, rb=rb: h.tensor_tensor(out=rs, in0=ra, in1=rb,
                                                                                    op=ALU.add),
                              reads=[rak, rbk], writes=[rsk])
                    pkeys = [('P', b2)] + [('P', b2, j) for j in range(4)]
                    if u['sink'] is not None:
                        e1, e1k = sm()
                        sk = SINK[:, u['sink']:u['sink'] + 1]
                        sc.op('act', lambda h, e1=e1, sk=sk, nm=nm: h.activation(out=e1, in_=sk, func=ACTF.Exp,
                                                                              bias=nm, scale=1.0),
                              reads=[('SINK',), nmk], writes=[e1k])
                        rs2, rs2k = sm()
                        sc.op('dve', lambda h, rs2=rs2, rs=rs, e1=e1: h.tensor_tensor(out=rs2, in0=rs, in1=e1,
                                                                                      op=ALU.add),
                              reads=[rsk, e1k], writes=[rs2k])
                        rs, rsk = rs2, rs2k
                    ri, rik = sm()
                    sc.op('dve', lambda h, ri=ri, rs=rs: h.reciprocal(out=ri, in_=rs), reads=[rsk], writes=[rik])
                    ptt = PTt[b2]
                    for g in range(2):
                        def tr(h, g=g, pt_=pt_):
                            inst = None
                            for j in range(8):
                                k_ = g * 8 + j
                                inst = h.transpose(out=PT_PS[g][:, j * 128:(j + 1) * 128],
                                                   in_=pt_[:, k_ * 128:(k_ + 1) * 128], identity=ident[:, :])
                            return inst
                        sc.op('pe', tr, reads=pkeys + [('ident',)], writes=[('PS', 4 + g)])
                        if g == 0:
                            sc.op('act', lambda h, g=g, ptt=ptt: h.activation(
                                out=ptt[:, g * 1024:(g + 1) * 1024], in_=PT_PS[g], func=ACTF.Copy),
                                reads=[('PS', 4 + g)], writes=[('PT', b2, g)])
                        else:
                            sc.op('dve', lambda h, g=g, ptt=ptt: h.tensor_copy(
                                out=ptt[:, g * 1024:(g + 1) * 1024], in_=PT_PS[g]),
                                reads=[('PS', 4 + g)], writes=[('PT', b2, g)])
                    def pv(h, ptt=ptt, vt=vt, u=u, vw=vw):
                        inst = None
                        for k_ in range(KT):
                            inst = h.matmul(O_PS[:, 0:vw], lhsT=ptt[:, k_ * 128:(k_ + 1) * 128],
                                            rhs=vt[:, k_, u['vc']:u['vc'] + vw],
                                            start=(k_ == 0), stop=(k_ == KT - 1))
                        return inst
                    sc.op('pe', pv, reads=[('PT', b2, 0), ('PT', b2, 1), ('VT', vb)], writes=[('PS', 6)])
                    if u['kind'] == 'n':
                        onb = ONB[b2]
                        sc.op('dve', lambda h, onb=onb, ri=ri: h.tensor_scalar(out=onb[:, :], in0=O_PS[:, 0:128],
                                                                              scalar1=ri, scalar2=None, op0=ALU.mult),
                              reads=[('PS', 6), rik], writes=[('ONB', b2)])
                        sc.op('pe', lambda h, onb=onb: h.transpose(out=OT_PS[:, 0:128], in_=onb[:, :],
                                                                   identity=ident[:, :]),
                              reads=[('ONB', b2), ('ident',)], writes=[('PS', 7)])
                        sc.op('dve', lambda h, u=u, qs=qs: h.tensor_copy(out=mixT[:, u['e'], qs], in_=OT_PS[:, 0:128]),
                              reads=[('PS', 7)], writes=[('mixT', u['e'], qt_i)])
                    elif u['kind'] == 'd1':
                        sc.op('dve', lambda h, ri=ri: h.tensor_scalar(out=O1N, in0=O_PS[:, 0:256], scalar1=ri,
                                                                      scalar2=None, op0=ALU.mult),
                              reads=[('PS', 6), rik], writes=[('O1N',)])
                        sc.dma('sp', lambda h, qs=qs: h.dma_start(out=ybuf[qs, 0:256], in_=O1N),
                               reads=[('O1N',)], writes=[('yb1', qt_i)])
                    else:
                        sc.dma('sp', lambda h, qs=qs: h.dma_start(out=O1N, in_=ybuf[qs, 0:256]),
                               reads=[('yb1', qt_i)], writes=[('O1N',)])
                        c2, c2k = sm()
                        sc.op('dve', lambda h, c2=c2, ri=ri: h.tensor_tensor(out=c2, in0=ri, in1=neglam, op=ALU.mult),
                              reads=[rik, ('LAMS', 2)], writes=[c2k])
                        sc.op('dve', lambda h, c2=c2: h.scalar_tensor_tensor(out=OD, in0=O_PS[:, 0:256],
                                                                             scalar=c2, in1=O1N,
                                                                             op0=ALU.mult, op1=ALU.add),
                              reads=[('PS', 6), c2k, ('O1N',)], writes=[('OD',)])
                        ss, ssk = sm()
                        sc.op('act', lambda h, ss=ss: h.activation(out=ODB[:, :], in_=OD, func=ACTF.Square,
                                                                   accum_out=ss),
                              reads=[('OD',)], writes=[('ODB',), ssk])
                        a_, ak_ = sm()
                        sc.op('act', lambda h, a_=a_, ss=ss: h.activation(out=a_, in_=ss, func=ACTF.Sqrt, bias=EPS,
                                                                         scale=1.0 / 256),
                              reads=[ssk], writes=[ak_])
                        r_, rk_ = sm()
                        sc.op('dve', lambda h, r_=r_, a_=a_: h.reciprocal(out=r_, in_=a_), reads=[ak_], writes=[rk_])
                        r3, r3k = sm()
                        sc.op('dve', lambda h, r3=r3, r_=r_: h.tensor_scalar(out=r3, in0=r_,
                                                                            scalar1=(1.0 - lambda_init),
                                                                            scalar2=None, op0=ALU.mult),
                              reads=[rk_], writes=[r3k])
                        sc.op('dve', lambda h, r3=r3: h.scalar_tensor_tensor(out=ODB[:, :], in0=OD, scalar=r3,
                                                                             in1=SUBG, op0=ALU.mult,
                                                                             op1=ALU.mult),
                              reads=[('OD',), r3k, ('SUBG',), ('ODB',)], writes=[('ODB',)])
                        def tr2(h):
                            h.transpose(out=OT_PS[:, 0:128], in_=ODB[:, 0:128], identity=ident[:, :])
                            return h.transpose(out=OT_PS[:, 128:256], in_=ODB[:, 128:256], identity=ident[:, :])
                        sc.op('pe', tr2, reads=[('ODB',), ('ident',)], writes=[('PS', 7)])
                        sc.op('dve', lambda h, u=u, qs=qs: h.tensor_copy(
                            out=mixT[:, u['e']:u['e'] + 2, qs],
                            in_=OT_PS[:, 0:256].rearrange("p (j t) -> p j t", j=2)),
                            reads=[('PS', 7)], writes=[('mixT', u['e'], qt_i)])
            sc.barrier()
            if debug and l == 0:
                sc.dma('sp', lambda h: h.dma_start(out=dbg_mix[:, :, :], in_=BIG[:, :, :]))
                sc.barrier()
            sc.emit()
        if stop_after == 'attn':
            break

        for db in range(8):
            src = w_out[fz(l)].rearrange("(kc p) n -> p kc n", p=128)[:, :, fz(db) * 512:(fz(db) + 1) * 512]
            wi, wv = load_w(src)
            for tt in range(NT):
                pi = (db * NT + tt) % 8
                def mm(h, pi=pi, tt=tt, wv=wv):
                    inst = None
                    for kc in range(KC):
                        inst = h.matmul(PS[pi][:, :], lhsT=mixT[:, kc, tt * 128:(tt + 1) * 128], rhs=wv[:, kc, :],
                                        start=(kc == 0), stop=(kc == KC - 1))
                    return inst
                sc.op('pe', mm, reads=[('W', wi), ('BIGALL',)], writes=[('PS', pi)])
                evac_y(pi, tt, db)
        load_gain(g_attn_post, l)
        for tt in range(NT):
            if os.environ.get("K_SKIP_POSTNORM"):
                break
            postnorm_tile(slice(tt * 128, (tt + 1) * 128), xsrc, xa, 0, 1, tt)
        sc.barrier()
        sc.emit()
        if stop_after == 'wout':
            break

        uT = BIG[:, :, :].rearrange("p k t -> p (k t)").rearrange("p (f t) -> p f t", t=256)
        w0f = W[0][:, :].bitcast(F32)
        SG = [w0f[:, 0:4096], w0f[:, 4096:8192]]
        WU = [W[1][:, k * 4096:(k + 1) * 4096] for k in range(4)]
        ucnt = [0]

        def load_unit(src3, a, b, uid, tb):
            n = ucnt[0]
            ucnt[0] += 1
            si_, k = n % 2, n % 4
            if tb == 0:
                sc.dma('sp', lambda h: h.dma_start(out=SG[si_].rearrange("p (a b) -> p a b", a=a), in_=src3),
                       writes=[('SG', si_)])
                if n % 2 == 0:
                    sc.op('act', lambda h: h.activation(out=WU[k], in_=SG[si_], func=ACTF.Copy),
                          reads=[('SG', si_)], writes=[('WU', k)])
                else:
                    sc.op('dve', lambda h: h.tensor_copy(out=WU[k], in_=SG[si_]),
                          reads=[('SG', si_)], writes=[('WU', k)])
                sc.dma('pool', lambda h: h.dma_start(out=wsc[uid], in_=WU[k]),
                       reads=[('WU', k)], writes=[('wsc', uid)])
            else:
                sc.dma('sp', lambda h: h.dma_start(out=WU[k], in_=wsc[uid]), reads=[('wsc', uid)], writes=[('WU', k)])
            return k, WU[k].rearrange("p (a b) -> p a b", a=a)

        auxq[0] = 'pool'
        for tb in range(4):
            load_gain(g_mlp_pre, l)
            for i in range(2):
                tt = tb * 2 + i
                prenorm_tile(xa[tt * 128:(tt + 1) * 128, :], i,
                             lambda kc0, i=i: H2[:, kc0:kc0 + 8, i * 128:(i + 1) * 128], ('H2', i), psbase=i * 2)
            h2keys = [('H2', 0), ('H2', 1)]
            w1v = w1[fz(l)].rearrange("(kc p) n -> p kc n", p=128)
            for fb in range(32):
                pbase = 4 if fb % 2 == 0 else 0
                for q in range(4):
                    src = w1v[:, q * 8:(q + 1) * 8, fz(fb) * 512:(fz(fb) + 1) * 512]
                    k, wu = load_unit(src, 8, 512, fb * 4 + q, tb)
                    for c in range(4):
                        pi = pbase + c
                        def mm(h, pi=pi, c=c, wu=wu, q=q):
                            inst = None
                            for kk in range(8):
                                inst = h.matmul(PS[pi][:, 0:256], lhsT=wu[:, kk, c * 128:(c + 1) * 128],
                                                rhs=H2[:, q * 8 + kk, :],
                                                start=(q == 0 and kk == 0), stop=(q == 3 and kk == 7))
                            return inst
                        sc.op('pe', mm, reads=[('WU', k)] + h2keys, writes=[('PS', pi)])
                for c in range(4):
                    pi = pbase + c
                    fc = fb * 4 + c
                    si = next_stg()
                    sc.op('act', lambda h, pi=pi, si=si: h.activation(out=STG[si][:, 0:256], in_=PS[pi][:, 0:256],
                                                                      func=ACTF.Relu),
                          reads=[('PS', pi)], writes=[('STG', si)])
                    sc.op('pool', lambda h, si=si, fc=fc: h.tensor_tensor(out=uT[:, fc, :], in0=STG[si][:, 0:256],
                                                                          in1=STG[si][:, 0:256], op=ALU.mult),
                          reads=[('STG', si)], writes=[('uT', fc)])
            w2v = w2[fz(l)].rearrange("(fc p) n -> p fc n", p=128)
            for db in range(8):
                for j8 in range(16):
                    src = w2v[:, fz(j8) * 8:(fz(j8) + 1) * 8, fz(db) * 512:(fz(db) + 1) * 512]
                    k, wu = load_unit(src, 8, 512, 128 + db * 16 + j8, tb)
                    for i in range(2):
                        pi = (db % 2) * 2 + i
                        def mm(h, pi=pi, i=i, j8=j8, wu=wu):
                            inst = None
                            for k_ in range(8):
                                fc = j8 * 8 + k_
                                inst = h.matmul(PS[pi][:, :], lhsT=uT[:, fc, i * 128:(i + 1) * 128], rhs=wu[:, k_, :],
                                                start=(fc == 0), stop=(fc == 127))
                            return inst
                        sc.op('pe', mm, reads=[('WU', k)] + [('uT', fc) for fc in range(j8 * 8, j8 * 8 + 8)],
                              writes=[('PS', pi)])
                for i in range(2):
                    evac_y((db % 2) * 2 + i, tb * 2 + i, db)
            load_gain(g_mlp_post, l)
            for i in range(2):
                tt = tb * 2 + i
                postnorm_tile(slice(tt * 128, (tt + 1) * 128), xa, xfinal, 0, 1, tt)
        sc.barrier()
        sc.emit()
        auxq[0] = 'sp'
    sc.barrier()
    sc.emit()


def _t5_bucket_np(rel):
    import jax
    import jax.numpy as jnp
    cpu = jax.devices("cpu")[0]
    with jax.default_device(cpu):
        rel = jnp.asarray(rel, dtype=jnp.int32)
        nb = 16
        max_exact = 8
        base = jnp.where(rel > 0, nb, 0)
        n = jnp.abs(rel)
        n_f = jnp.maximum(n, 1).astype(jnp.float32)
        large = max_exact + (jnp.log(n_f / max_exact) / math.log(128 / max_exact) * (nb - max_exact)).astype(jnp.int32)
        large = jnp.minimum(large, nb - 1)
        return np.asarray(base + jnp.where(n < max_exact, n, large))


_CONST = {}


def _consts():
    if _CONST:
        return _CONST
    rel1d = np.arange(-(S - 1), S, dtype=np.int32)
    bucket1d = _t5_bucket_np(rel1d)
    per_half = []
    for hf in range(2):
        qpos = hf * NTOK + np.arange(NTOK)
        kpos = np.arange(S)
        rel = kpos[None, :] - qpos[:, None]
        bidx = bucket1d[rel + S - 1]
        cmask = np.abs(rel) <= 128
        r = qpos // 64
        c = qpos % 64
        rs = np.clip(r - 4, 0, 32 - 8)
        cs = np.clip(c - 8, 0, 64 - 16)
        kr = kpos // 64
        kc = kpos % 64
        amask = (kr[None, :] >= rs[:, None]) & (kr[None, :] < rs[:, None] + 8) & \
                (kc[None, :] >= cs[:, None]) & (kc[None, :] < cs[:, None] + 16)
        dr = np.clip(kr[None, :] - r[:, None] + 7, 0, 14)
        dc = np.clip(kc[None, :] - c[:, None] + 15, 0, 30)
        row = (qpos // 64).astype(np.float32)
        col = (qpos % 64).astype(np.float32)
        inv = (10000.0 ** (-np.arange(32, dtype=np.float32) / 32)).astype(np.float32)
        ang = np.concatenate([row[:, None] * inv, col[:, None] * inv], axis=-1).astype(np.float32)
        per_half.append(dict(bidx=bidx, cmask=cmask, amask=amask, dr=dr, dc=dc,
                             cos=np.cos(ang).astype(np.float32), sin=np.sin(ang).astype(np.float32)))
    _CONST['h'] = per_half
    _CONST['ident'] = np.eye(128, dtype=np.float32).astype(ml_dtypes.bfloat16)
    return _CONST


def make_in_maps(inputs):
    cst = _consts()
    f = lambda a: np.ascontiguousarray(np.asarray(a, dtype=np.float32))
    x = f(inputs["x"])
    shared = {k: f(inputs[k]) for k in ["ln_attn_pre", "ln_attn_post", "ln_mlp_pre", "ln_mlp_post", "w_in", "w_out",
                                        "w_mlp_in", "w_mlp_out", "ax_q_norm", "ax_k_norm", "sw_sink", "df_subln"]}
    shared["df_lambda"] = f(inputs["df_lambda"]).reshape(2, 512)
    shared["ident"] = cst['ident']
    rpb = f(inputs["na_rpb"])
    t5 = f(inputs["t5_table"])
    halves = []
    for hf in range(2):
        c = cst['h'][hf]
        bA = np.empty((2, 8, NTOK, S), np.float32)
        for l in range(2):
            for h in range(8):
                bA[l, h] = np.where(c['amask'], rpb[l, h][c['dr'], c['dc']], np.float32(NEG))
        bC = np.empty((8, NTOK, S), np.float32)
        for h in range(8):
            bC[h] = np.where(c['cmask'], t5[:, h][c['bidx']], np.float32(NEG))
        bD = np.empty((4, NTOK, S), np.float32)
        for h in range(4):
            bD[h] = t5[:, 8 + h][c['bidx']]
        halves.append(dict(biasA=bA, biasC=bC, biasD=bD, rope_cos=c['cos'], rope_sin=c['sin']))
    in_maps = []
    for core in range(8):
        b, hf = core // 2, core % 2
        m = dict(shared)
        m.update(halves[hf])
        m["x"] = np.ascontiguousarray(x[b, hf * NTOK:(hf + 1) * NTOK, :])
        in_maps.append(m)
    return in_maps


_NC = {}


def kernel(**inputs):
    if 'nc' not in _NC:
        _NC['nc'] = build_program()
    nc = _NC['nc']
    in_maps = make_in_maps(inputs)
    res = run_bass_kernel_spmd(nc, in_maps, core_ids=list(range(8)))
    outp = np.empty((4, S, D), np.float32)
    for core in range(8):
        b, hf = core // 2, core % 2
        outp[b, hf * NTOK:(hf + 1) * NTOK, :] = res.results[core]["out"]
    return outp
```

```python
import math
import os
from contextlib import ExitStack
import numpy as np
import ml_dtypes
import concourse.bass as bass
import concourse.mybir as mybir
from concourse.bass_utils import run_bass_kernel_spmd

F32 = mybir.dt.float32
BF16 = mybir.dt.bfloat16
ALU = mybir.AluOpType
ACTF = mybir.ActivationFunctionType
AX = mybir.AxisListType

D = 4096
KC = 32
NTOK = 1024
NT = 8
S = 2048
KT = 16
DIN = 9216
DFF = 16384
EPS = 1e-6
SCALE = 128 ** -0.5
NEG = -30000.0
PAIRS = [[0, 1], [2, 3], [4, 5], [6, 7]]


class Sched:
    ENG_ATTR = [('pe', 'tensor'), ('act', 'scalar'), ('dve', 'vector'), ('pool', 'gpsimd'), ('sp', 'sync')]

    def __init__(self, nc, es):
        self.nc = nc
        self.engs = [e for e, _ in self.ENG_ATTR]
        self.sem = {e: es.enter_context(nc.semaphore("s_" + e)) for e in self.engs}
        self.cnt = {e: 0 for e in self.engs}
        self.NDS = 40
        self.dsem = [es.enter_context(nc.semaphore("d%d" % i)) for i in range(self.NDS)]
        self.dcnt = [0] * self.NDS
        self.dnext = 0
        self.ops = []
        self.lastw = {}
        self.readers = {}
        self.waited = {e: {} for e in self.engs}

    def _deps(self, eng, reads, writes):
        toks = []
        for k in reads:
            t = self.lastw.get(k)
            if t is not None and not (t[0] == 'e' and t[1] == eng and eng == 'pe'):
                toks.append(t)
        for k in writes:
            t = self.lastw.get(k)
            if t is not None and not (t[0] == 'e' and t[1] == eng):
                toks.append(t)
            for t in self.readers.get(k, ()):
                if not (t[0] == 'e' and t[1] == eng):
                    toks.append(t)
        return toks

    def _commit(self, tok, reads, writes):
        for k in reads:
            self.readers.setdefault(k, []).append(tok)
        for k in writes:
            self.lastw[k] = tok
            self.readers[k] = []

    def op(self, eng, fn, reads=(), writes=()):
        toks = self._deps(eng, reads, writes)
        self.cnt[eng] += 1
        tok = ('e', eng, self.cnt[eng])
        self.ops.append((eng, fn, toks, tok, 1))
        self._commit(tok, reads, writes)
        return tok

    def dma(self, q, fn, reads=(), writes=()):
        toks = self._deps(q, reads, writes)
        i = self.dnext
        self.dnext = (i + 1) % self.NDS
        if self.dcnt[i] > 0:
            toks.append(('d', i, self.dcnt[i]))
        self.dcnt[i] += 16
        tok = ('d', i, self.dcnt[i])
        self.ops.append((q, fn, toks, tok, 16))
        self._commit(tok, reads, writes)
        return tok

    def coll(self, fn, reads=(), writes=()):
        return self.op('pool', fn, reads, writes)

    def barrier(self):
        toks = [('e', e, self.cnt[e]) for e in self.engs if self.cnt[e] > 0]
        toks += [('d', i, self.dcnt[i]) for i in range(self.NDS) if self.dcnt[i] > 0]
        for e in self.engs:
            self.ops.append((e, None, [t for t in toks if not (t[0] == 'e' and t[1] == e)], None, 0))
        self.lastw = {}
        self.readers = {}

    def _wait(self, h, e, t):
        key = (t[0], t[1])
        if self.waited[e].get(key, 0) >= t[2]:
            return
        self.waited[e][key] = t[2]
        sem = self.sem[t[1]] if t[0] == 'e' else self.dsem[t[1]]
        h.wait_ge(sem, t[2])

    def emit(self):
        nc = self.nc
        per = {e: [] for e in self.engs}
        for o in self.ops:
            per[o[0]].append(o)
        with nc.Block() as blk:
            for e, attr in self.ENG_ATTR:
                lst = per[e]
                if not lst:
                    continue

                def body(h, lst=lst, e=e):
                    for (_, fn, toks, tok, inc) in lst:
                        for t in toks:
                            self._wait(h, e, t)
                        if fn is None:
                            continue
                        inst = fn(h)
                        if tok[0] == 'e':
                            inst.then_inc(self.sem[e], 1)
                        else:
                            inst.then_inc(self.dsem[tok[1]], 16)
                getattr(blk, attr)(body)
        self.ops = []


def build_program(nlayers=2, debug=False, stop_after=None, fast=False):
    nc = bass.Bass("TRN2", target_bir_lowering=False)
    es = ExitStack()
    with es:
        _build(nc, es, nlayers, debug, stop_after, fast)
    return nc


def _build(nc, es, nlayers, debug, stop_after, fast=False):
    def din(name, shape, dt=F32):
        return nc.dram_tensor(name, list(shape), dt, kind="ExternalInput")

    dbgkind = "ExternalOutput" if debug else "Internal"

    x_in = din("x", [NTOK, D])
    g_attn_pre = din("ln_attn_pre", [2, D])
    g_attn_post = din("ln_attn_post", [2, D])
    g_mlp_pre = din("ln_mlp_pre", [2, D])
    g_mlp_post = din("ln_mlp_post", [2, D])
    if fast:
        w_in = din("w_in", [1, D, 512])
        w_out = din("w_out", [1, D, 512])
        w1 = din("w_mlp_in", [1, D, 512])
        w2 = din("w_mlp_out", [1, D, 512])
    else:
        w_in = din("w_in", [2, D, DIN])
        w_out = din("w_out", [2, D, D])
        w1 = din("w_mlp_in", [2, D, DFF])
        w2 = din("w_mlp_out", [2, DFF, D])
    qn_g = din("ax_q_norm", [2, 128])
    kn_g = din("ax_k_norm", [2, 128])
    sink_in = din("sw_sink", [2, 8])
    lam_in = din("df_lambda", [2, 512])
    subln_in = din("df_subln", [2, 256])
    if fast:
        biasA = din("biasA", [1, 1, NTOK, S])
        biasC = din("biasC", [1, NTOK, S])
        biasD = din("biasD", [1, NTOK, S])
    else:
        biasA = din("biasA", [2, 8, NTOK, S])
        biasC = din("biasC", [8, NTOK, S])
        biasD = din("biasD", [4, NTOK, S])
    fz = (lambda v: 0) if fast else (lambda v: v)
    cos_in = din("rope_cos", [NTOK, 64])
    sin_in = din("rope_sin", [NTOK, 64])
    ident_in = din("ident", [128, 128], BF16)
    out = nc.dram_tensor("out", [NTOK, D], F32, kind="ExternalOutput")

    xa = nc.dram_tensor("xa", [NTOK, D], F32, kind=dbgkind)
    xb = nc.dram_tensor("xb", [NTOK, D], F32, kind=dbgkind)
    ybuf = nc.dram_tensor("ybuf", [NTOK, D], F32, kind=dbgkind)
    qbuf = nc.dram_tensor("qbuf", [32 * 128, NTOK], BF16, kind=dbgkind)
    kbuf = [nc.dram_tensor("kbuf%d" % j, [512, NTOK], BF16) for j in range(5)]
    kgat = [nc.dram_tensor("kgat%d" % j, [1024, NTOK], BF16) for j in range(5)]
    vbuf = [nc.dram_tensor("vbuf%d" % j, [NTOK, 512], BF16) for j in range(5)]
    vgat = [nc.dram_tensor("vgat%d" % j, [S, 512], BF16) for j in range(5)]
    if debug:
        dbg_mix = nc.dram_tensor("dbg_mix", [128, KC, NTOK], BF16, kind="ExternalOutput")
        dbg_k = nc.dram_tensor("dbg_k", [5, 1024, NTOK], BF16, kind="ExternalOutput")
        dbg_v = nc.dram_tensor("dbg_v", [5, S, 512], BF16, kind="ExternalOutput")

    wsc = nc.dram_tensor("wsc", [256, 128, 4096], BF16)
    sc = Sched(nc, es)

    def sb(name, shape, dt):
        return es.enter_context(nc.sbuf_tensor(name, list(shape), dt))

    def ps(name, shape, dt):
        return es.enter_context(nc.psum_tensor(name, list(shape), dt))

    BIG = sb("BIG", [128, KC, NTOK], BF16)
    W = [sb("W0", [128, KC * 512], BF16), sb("W1", [128, KC * 512], BF16)]
    XTF = sb("XTF", [128, D], F32)
    GB = sb("GB", [128, D], F32)
    XN = sb("XN", [128, D], BF16)
    ident = sb("ident_sb", [128, 128], BF16)
    small = sb("small", [128, 64], F32)
    SSP = sb("SSP", [128, NT, 8], F32)
    STG = [sb("STG%d" % i, [128, 512], F32) for i in range(2)]
    STGB = [sb("STGB%d" % i, [128, 512], BF16) for i in range(2)]
    H2 = sb("H2", [128, KC, 256], BF16)
    PS = [ps("PS%d" % i, [128, 512], F32) for i in range(8)]

    auxq = ['sp']
    wcount = [0]

    def next_w():
        i = wcount[0] % 2
        wcount[0] += 1
        return i

    stg_i = [0]

    def next_stg():
        i = stg_i[0] % 2
        stg_i[0] += 1
        return i

    sm_i = [0]

    def sm():
        i = sm_i[0] % 64
        sm_i[0] += 1
        return small[:, i:i + 1], ('small', i)

    sc.dma('sp', lambda h: h.dma_start(out=ident[:, :], in_=ident_in[:, :]), writes=[('ident',)])

    def load_gain(gt, l):
        sc.dma(auxq[0], lambda h: h.dma_start(out=GB[:, :], in_=gt[l:l + 1, :].partition_broadcast(128)),
               writes=[('GB',)])

    def rstd_from_ss(ss_ap, ss_key, n):
        a, ak = sm()
        sc.op('act', lambda h: h.activation(out=a, in_=ss_ap, func=ACTF.Sqrt, bias=EPS, scale=1.0 / n),
              reads=[ss_key], writes=[ak])
        r, rk = sm()
        sc.op('dve', lambda h: h.reciprocal(out=r, in_=a), reads=[ak], writes=[rk])
        return r, rk

    def prenorm_tile(xsrc_ap, slot, dst_fn, dst_key, psbase):
        xt = XTF
        xn = XN
        sc.dma(auxq[0], lambda h: h.dma_start(out=xt[:, :], in_=xsrc_ap), writes=[('XTy',), ('XTx',)])
        ss, ssk = sm()
        sc.op('act', lambda h: h.activation(out=xn[:, :], in_=xt[:, :], func=ACTF.Square, accum_out=ss),
              reads=[('XTy',), ('XTx',)], writes=[('XN',), ssk])
        r, rk = rstd_from_ss(ss, ssk, D)
        sc.op('dve', lambda h: h.scalar_tensor_tensor(out=xn[:, :], in0=xt[:, :], scalar=r, in1=GB[:, :],
                                                      op0=ALU.mult, op1=ALU.mult),
              reads=[('XTy',), ('XTx',), rk, ('GB',)], writes=[('XN',)])
        for g in range(4):
            pst = PS[psbase + (g % 2)]
            pv = pst[:, :].bitcast(BF16)
            def tr(h, g=g, pv=pv):
                inst = None
                for j in range(8):
                    kc = g * 8 + j
                    inst = h.transpose(out=pv[:, j * 128:(j + 1) * 128], in_=xn[:, kc * 128:(kc + 1) * 128],
                                       identity=ident[:, :])
                return inst
            sc.op('pe', tr, reads=[('XN',), ('ident',)], writes=[('PS', psbase + (g % 2))])
            dst = dst_fn(g * 8)
            eng = 'act' if g % 2 == 0 else 'dve'
            if eng == 'act':
                sc.op('act', lambda h, dst=dst, pv=pv: h.activation(
                    out=dst, in_=pv.rearrange("p (j t) -> p j t", j=8), func=ACTF.Copy),
                    reads=[('PS', psbase + (g % 2))], writes=[dst_key])
            else:
                sc.op('dve', lambda h, dst=dst, pv=pv: h.tensor_copy(
                    out=dst, in_=pv.rearrange("p (j t) -> p j t", j=8)),
                    reads=[('PS', psbase + (g % 2))], writes=[dst_key])

    def load_w(src_ap):
        i = next_w()
        wv = W[i][:, :].rearrange("p (k n) -> p k n", n=512)
        sc.dma('pool', lambda h: h.dma_start(out=wv, in_=src_ap), writes=[('W', i)])
        return i, wv

    def postnorm_tile(tt_rows, xsrc, xdst, slot_y, slot_x, ssp_tt):
        ss, ssk = sm()
        sc.op('dve', lambda h: h.reduce_sum(out=ss, in_=SSP[:, ssp_tt, :], axis=AX.X),
              reads=[('SSP', ssp_tt)], writes=[ssk])
        r, rk = rstd_from_ss(ss, ssk, D)
        for c in range(2):
            cs = slice(c * 2048, (c + 1) * 2048)
            yt = XTF[:, 0:2048]
            xt = XTF[:, 2048:4096]
            sc.dma(auxq[0], lambda h, cs=cs: h.dma_start(out=yt, in_=ybuf[tt_rows, cs]),
                   reads=[('ybuf', tt_rows.start, db) for db in range(8)], writes=[('XTy',)])
            sc.dma(auxq[0], lambda h, cs=cs: h.dma_start(out=xt, in_=xsrc[tt_rows, cs]), writes=[('XTx',)])
            sc.op('dve', lambda h, cs=cs: h.scalar_tensor_tensor(out=yt, in0=yt, scalar=r, in1=GB[:, cs],
                                                             op0=ALU.mult, op1=ALU.mult),
                  reads=[('XTy',), rk, ('GB',)], writes=[('XTy',)])
            sc.op('pool', lambda h: h.tensor_tensor(out=xt, in0=xt, in1=yt, op=ALU.add),
                  reads=[('XTy',), ('XTx',)], writes=[('XTx',)])
            sc.dma(auxq[0], lambda h, cs=cs: h.dma_start(out=xdst[tt_rows, cs], in_=xt),
                   reads=[('XTx',)], writes=[('xdst', tt_rows.start, c)])

    def evac_y(pidx, tt, db, ncols=512):
        si = next_stg()
        junk = STGB[si]
        lvl = int(os.environ.get("K_P3", "9"))
        if lvl < 1:
            return
        sc.op('act', lambda h: h.activation(out=junk[:, :], in_=PS[pidx][:, :], func=ACTF.Square,
                                            accum_out=SSP[:, tt, db:db + 1]),
              reads=[('PS', pidx)], writes=[('STGB', si), ('SSP', tt)])
        if lvl < 2:
            return
        st = STG[si]
        sc.op('dve', lambda h: h.tensor_scalar(out=st[:, :], in0=PS[pidx][:, :], scalar1=1.0, scalar2=None, op0=ALU.mult),
              reads=[('PS', pidx), ('STGB', si)], writes=[('STG', si)])
        if lvl < 3:
            return
        rows = slice(tt * 128, (tt + 1) * 128)
        sc.dma(auxq[0], lambda h: h.dma_start(out=ybuf[rows, db * 512:(db + 1) * 512], in_=st[:, :]),
               reads=[('STG', si)], writes=[('ybuf', rows.start, db)])

    for l in range(nlayers):
        xsrc = x_in if l == 0 else xb
        xfinal = out if l == nlayers - 1 else xb

        load_gain(g_attn_pre, l)
        for tt in range(NT):
            prenorm_tile(xsrc[tt * 128:(tt + 1) * 128, :], tt % 2,
                         lambda kc0, tt=tt: BIG[:, kc0:kc0 + 8, tt * 128:(tt + 1) * 128],
                         ('BIG', tt), psbase=(tt % 2) * 2)
        sc.barrier()
        sc.emit()

        hT = BIG
        allhT = [('BIG', tt) for tt in range(NT)]
        with ExitStack() as pes:
            def psb(name, shape, dt):
                return pes.enter_context(nc.sbuf_tensor("%s_l%d" % (name, l), list(shape), dt))
            xtb = XTF[:, :].bitcast(BF16)
            gbb = GB[:, :].bitcast(BF16)
            BQ8 = xtb.rearrange("p (h t) -> p h t", h=8)
            BK2 = gbb[:, 0:2048].rearrange("p (h t) -> p h t", h=2)
            XQ = GB[:, 1024:1536]
            XQ2 = GB[:, 1536:2048]
            T1 = GB[:, 2048:2304]
            T2 = GB[:, 2304:2560]
            GQ = GB[:, 2560:2688]
            GK = GB[:, 2688:2816]
            COS = GB[:, 2816:3328].rearrange("p (t c) -> p t c", t=NT)
            SIN = GB[:, 3328:3840].rearrange("p (t c) -> p t c", t=NT)
            SS8 = GB[:, 3840:3848]
            SS8b = GB[:, 3848:3856]
            XR = XN[:, 0:512]
            sc.dma('sp', lambda h: h.dma_start(out=GQ, in_=qn_g[l:l + 1, :].partition_broadcast(128)),
                   writes=[('GQ',)])
            sc.dma('sp', lambda h: h.dma_start(out=GK, in_=kn_g[l:l + 1, :].partition_broadcast(128)),
                   writes=[('GK',)])
            sc.dma('sp', lambda h: h.dma_start(out=COS,
                                               in_=cos_in.ap().rearrange("(t p) c -> p t c", p=128)),
                   writes=[('COS',)])
            sc.dma('sp', lambda h: h.dma_start(out=SIN,
                                               in_=sin_in.ap().rearrange("(t p) c -> p t c", p=128)),
                   writes=[('SIN',)])

            acc_i = [0]

            def next_acc():
                i = acc_i[0] % 6
                acc_i[0] += 1
                return i

            def fm_chunk(wi, wv, c, dst_tensor, dst_row):
                for half in range(2):
                    pi = next_acc()
                    def mm(h, pi=pi, half=half):
                        inst = None
                        for kc in range(KC):
                            inst = h.matmul(PS[pi][:, :], lhsT=wv[:, kc, c * 128:(c + 1) * 128],
                                            rhs=hT[:, kc, half * 512:(half + 1) * 512],
                                            start=(kc == 0), stop=(kc == KC - 1))
                        return inst
                    sc.op('pe', mm, reads=[('W', wi)] + allhT, writes=[('PS', pi)])
                    si = next_stg()
                    if half == 0:
                        sc.op('act', lambda h, pi=pi, si=si: h.activation(out=STGB[si][:, :], in_=PS[pi][:, :],
                                                                          func=ACTF.Copy),
                              reads=[('PS', pi)], writes=[('STGB', si)])
                    else:
                        sc.op('dve', lambda h, pi=pi, si=si: h.tensor_copy(out=STGB[si][:, :], in_=PS[pi][:, :]),
                              reads=[('PS', pi)], writes=[('STGB', si)])
                    sc.dma('sp', lambda h, si=si, half=half: h.dma_start(
                        out=dst_tensor[dst_row:dst_row + 128, half * 512:(half + 1) * 512], in_=STGB[si][:, :]),
                        reads=[('STGB', si)], writes=[('fm', id(dst_tensor), dst_row, half)])

            def tm_block(wi, wv, c0, ncols, consume):
                for tt in range(NT):
                    pi = next_acc()
                    def mm(h, pi=pi, tt=tt):
                        inst = None
                        for kc in range(KC):
                            inst = h.matmul(PS[pi][:, 0:ncols], lhsT=hT[:, kc, tt * 128:(tt + 1) * 128],
                                            rhs=wv[:, kc, c0:c0 + ncols],
                                            start=(kc == 0), stop=(kc == KC - 1))
                        return inst
                    sc.op('pe', mm, reads=[('W', wi), ('BIG', tt)], writes=[('PS', pi)])
                    consume(tt, pi)

            def v_consume(vt, col0, ncols):
                def f(tt, pi):
                    si = next_stg()
                    sc.op('act', lambda h: h.activation(out=STGB[si][:, 0:ncols], in_=PS[pi][:, 0:ncols],
                                                        func=ACTF.Copy),
                          reads=[('PS', pi)], writes=[('STGB', si)])
                    sc.dma('sp', lambda h: h.dma_start(out=vt[tt * 128:(tt + 1) * 128, col0:col0 + ncols],
                                                       in_=STGB[si][:, 0:ncols]),
                           reads=[('STGB', si)], writes=[('vout', id(vt), tt, col0)])
                return f

            def rope_consume(nh, gtile, gkey, dst3, dkey, head0):
                W_ = nh * 128
                def f(tt, pi):
                    x3 = XQ[:, 0:W_].rearrange("p (h d) -> p h d", h=nh)
                    sc.op('act', lambda h: h.activation(out=XQ[:, 0:W_], in_=PS[pi][:, 0:W_], func=ACTF.Copy),
                          reads=[('PS', pi)], writes=[('XQ',)])
                    sc.op('dve', lambda h: h.tensor_tensor(out=XQ2[:, 0:W_], in0=XQ[:, 0:W_], in1=XQ[:, 0:W_],
                                                           op=ALU.mult),
                          reads=[('XQ',)], writes=[('XQ2',)])
                    sc.op('dve', lambda h: h.reduce_sum(out=SS8[:, 0:nh],
                                                        in_=XQ2[:, 0:W_].rearrange("p (h d) -> p h d", h=nh),
                                                        axis=AX.X),
                          reads=[('XQ2',)], writes=[('SS8',)])
                    sc.op('act', lambda h: h.activation(out=SS8b[:, 0:nh], in_=SS8[:, 0:nh], func=ACTF.Sqrt,
                                                        bias=EPS, scale=1.0 / 128),
                          reads=[('SS8',)], writes=[('SS8b',)])
                    sc.op('dve', lambda h: h.reciprocal(out=SS8[:, 0:nh], in_=SS8b[:, 0:nh]),
                          reads=[('SS8b',)], writes=[('SS8',)])
                    sc.op('dve', lambda h: h.tensor_tensor(
                        out=x3, in0=x3, in1=SS8[:, 0:nh, None].to_broadcast([128, nh, 128]), op=ALU.mult),
                        reads=[('XQ',), ('SS8',)], writes=[('XQ',)])
                    sc.op('pool', lambda h: h.tensor_tensor(
                        out=x3, in0=x3, in1=gtile[:, None, :].to_broadcast([128, nh, 128]), op=ALU.mult),
                        reads=[('XQ',), gkey], writes=[('XQ',)])
                    x4 = XQ[:, 0:W_].rearrange("p (h i two) -> p h i two", h=nh, two=2)
                    x1 = x4[:, :, :, 0]
                    x2 = x4[:, :, :, 1]
                    cb = COS[:, tt:tt + 1, :].to_broadcast([128, nh, 64])
                    sb_ = SIN[:, tt:tt + 1, :].to_broadcast([128, nh, 64])
                    t1 = T1[:, 0:nh * 64].rearrange("p (h i) -> p h i", h=nh)
                    t2 = T2[:, 0:nh * 64].rearrange("p (h i) -> p h i", h=nh)
                    r4 = XR[:, 0:W_].rearrange("p (h i two) -> p h i two", h=nh, two=2)
                    sc.op('dve', lambda h: h.tensor_tensor(out=t1, in0=x1, in1=cb, op=ALU.mult),
                          reads=[('XQ',), ('COS',)], writes=[('T1',)])
                    sc.op('pool', lambda h: h.tensor_tensor(out=t2, in0=x2, in1=sb_, op=ALU.mult),
                          reads=[('XQ',), ('SIN',)], writes=[('T2',)])
                    sc.op('dve', lambda h: h.tensor_tensor(out=r4[:, :, :, 0], in0=t1, in1=t2, op=ALU.subtract),
                          reads=[('T1',), ('T2',)], writes=[('XR', 0)])
                    sc.op('dve', lambda h: h.tensor_tensor(out=t1, in0=x1, in1=sb_, op=ALU.mult),
                          reads=[('XQ',), ('SIN',), ('XR', 0)], writes=[('T1',)])
                    sc.op('pool', lambda h: h.tensor_tensor(out=t2, in0=x2, in1=cb, op=ALU.mult),
                          reads=[('XQ',), ('COS',), ('XR', 0)], writes=[('T2',)])
                    sc.op('dve', lambda h: h.tensor_tensor(out=r4[:, :, :, 1], in0=t1, in1=t2, op=ALU.add),
                          reads=[('T1',), ('T2',)], writes=[('XR', 1)])
                    pv = PS[6 + (tt % 2)][:, :].bitcast(BF16)
                    def tr(h):
                        inst = None
                        for j in range(nh):
                            inst = h.transpose(out=pv[:, j * 128:(j + 1) * 128], in_=XR[:, j * 128:(j + 1) * 128],
                                               identity=ident[:, :])
                        return inst
                    sc.op('pe', tr, reads=[('XR', 0), ('XR', 1), ('ident',)], writes=[('PS', 6 + (tt % 2))])
                    sc.op('act', lambda h: h.activation(
                        out=dst3[:, head0:head0 + nh, tt * 128:(tt + 1) * 128],
                        in_=pv[:, 0:W_].rearrange("p (j t) -> p j t", j=nh), func=ACTF.Copy),
                        reads=[('PS', 6 + (tt % 2))], writes=[(dkey, head0, tt)])
                return f

            for nb in range(18):
                src = w_in[fz(l)].rearrange("(kc p) n -> p kc n", p=128)[:, :, fz(nb) * 512:(fz(nb) + 1) * 512]
                wi, wv = load_w(src)
                if nb in (0, 1):
                    for c in range(4):
                        fm_chunk(wi, wv, c, qbuf, (4 * nb + c) * 128)
                elif nb in (2, 3):
                    for c in range(4):
                        fm_chunk(wi, wv, c, kbuf[nb - 2], c * 128)
                elif nb in (4, 5):
                    tm_block(wi, wv, 0, 512, v_consume(vbuf[nb - 4], 0, 512))
                elif nb in (6, 7):
                    tm_block(wi, wv, 0, 512, rope_consume(4, GQ, ('GQ',), BQ8, 'BQ8', 4 * (nb - 6)))
                elif nb == 8:
                    tm_block(wi, wv, 0, 256, rope_consume(2, GK, ('GK',), BK2, 'BK2', 0))
                    tm_block(wi, wv, 256, 256, v_consume(vbuf[2], 0, 256))
                elif nb in (9, 10):
                    for c in range(4):
                        fm_chunk(wi, wv, c, qbuf, (16 + 4 * (nb - 9) + c) * 128)
                elif nb == 11:
                    for c in range(2):
                        fm_chunk(wi, wv, c, kbuf[2], 256 + c * 128)
                    tm_block(wi, wv, 256, 256, v_consume(vbuf[2], 256, 256))
                elif nb in (12, 13):
                    for c in range(4):
                        fm_chunk(wi, wv, c, qbuf, (24 + 4 * (nb - 12) + c) * 128)
                elif nb in (14, 15):
                    for c in range(4):
                        fm_chunk(wi, wv, c, kbuf[3 + nb - 14], c * 128)
                else:
                    tm_block(wi, wv, 0, 512, v_consume(vbuf[3 + nb - 16], 0, 512))
            for hq in range(8):
                sc.dma('sp', lambda h, hq=hq: h.dma_start(out=qbuf[(8 + hq) * 128:(9 + hq) * 128, :],
                                                          in_=BQ8[:, hq, :]),
                       reads=[('BQ8', 4 * (hq // 4), tt) for tt in range(NT)], writes=[('qbuf_b', hq)])
            for kv in range(2):
                sc.dma('sp', lambda h, kv=kv: h.dma_start(out=kbuf[2][kv * 128:(kv + 1) * 128, :],
                                                          in_=BK2[:, kv, :]),
                       reads=[('BK2', 0, tt) for tt in range(NT)], writes=[('kbuf_b', kv)])
            sc.barrier()
            for j in range(5):
                sc.coll(lambda h, j=j: h.collective_compute("AllGather", ALU.bypass, replica_groups=PAIRS,
                                                            ins=[kbuf[j].ap().opt()], outs=[kgat[j].ap().opt()]))
                sc.coll(lambda h, j=j: h.collective_compute("AllGather", ALU.bypass, replica_groups=PAIRS,
                                                            ins=[vbuf[j].ap().opt()], outs=[vgat[j].ap().opt()]))
            sc.barrier()
            if debug and l == 0:
                for j in range(5):
                    sc.dma('sp', lambda h, j=j: h.dma_start(out=dbg_k[j], in_=kgat[j][:, :]))
                    sc.dma('sp', lambda h, j=j: h.dma_start(out=dbg_v[j], in_=vgat[j][:, :]))
                sc.barrier()
            sc.emit()
        if stop_after == 'proj':
            break

        mixT = BIG
        with ExitStack() as pes:
            def psb(name, shape, dt):
                return pes.enter_context(nc.sbuf_tensor("%s_l%d" % (name, l), list(shape), dt))
            VT = [W[0][:, 0:8192].rearrange("p (k c) -> p k c", c=512),
                  W[0][:, 8192:16384].rearrange("p (k c) -> p k c", c=512)]
            w1f = W[1][:, :].bitcast(F32)
            BIAS = [w1f[:, 0:2048], w1f[:, 2048:4096]]
            S_SB = [w1f[:, 4096:6144], w1f[:, 6144:8192]]
            xtb = XTF[:, :].bitcast(BF16)
            KTt = [xtb[:, 0:2048], xtb[:, 2048:4096]]
            Pt = [xtb[:, 4096:6144], xtb[:, 6144:8192]]
            gbb = GB[:, :].bitcast(BF16)
            PTt = [gbb[:, 0:2048], gbb[:, 2048:4096]]
            QTt = [gbb[:, 4096:5120], gbb[:, 5120:6144]]
            h2f = H2[:, :, :].rearrange("p k t -> p (k t)").bitcast(F32)
            O1N = h2f[:, 0:256]
            OD = h2f[:, 256:512]
            LAMT = h2f[:, 512:1024]
            LAMP = h2f[:, 1024:1536]
            SUBG = h2f[:, 1536:1792]
            ODB = psb("ODB", [128, 256], BF16)
            ONB = [psb("ONB0", [128, 128], BF16), psb("ONB1", [128, 128], BF16)]
            SINK = psb("SINK", [128, 8], F32)
            LAMS = psb("LAMS", [128, 8], F32)
            lambda_init = 0.8 - 0.6 * math.exp(-0.3 * l)

            sc.dma('sp', lambda h: h.dma_start(out=SUBG, in_=subln_in[l:l + 1, :].partition_broadcast(128)),
                   writes=[('SUBG',)])
            sc.dma('sp', lambda h: h.dma_start(out=SINK[:, :], in_=sink_in[l:l + 1, :].partition_broadcast(128)),
                   writes=[('SINK',)])
            sc.dma('sp', lambda h: h.dma_start(out=LAMT, in_=lam_in[l:l + 1, :].partition_broadcast(128)),
                   writes=[('LAMT',)])
            lt = LAMT.rearrange("p (a d) -> p a d", a=4)
            lp = LAMP[:, 0:256].rearrange("p (a d) -> p a d", a=2)
            sc.op('dve', lambda h: h.tensor_tensor(out=lp[:, 0, :], in0=lt[:, 0, :], in1=lt[:, 1, :], op=ALU.mult),
                  reads=[('LAMT',)], writes=[('LAMP', 0)])
            sc.op('dve', lambda h: h.tensor_tensor(out=lp[:, 1, :], in0=lt[:, 2, :], in1=lt[:, 3, :], op=ALU.mult),
                  reads=[('LAMT',)], writes=[('LAMP', 1)])
            sc.op('dve', lambda h: h.reduce_sum(out=LAMS[:, 0:2], in_=lp, axis=AX.X),
                  reads=[('LAMP', 0), ('LAMP', 1)], writes=[('LAMS', 0)])
            sc.op('act', lambda h: h.activation(out=LAMS[:, 2:4], in_=LAMS[:, 0:2], func=ACTF.Exp),
                  reads=[('LAMS', 0)], writes=[('LAMS', 1)])
            sc.op('dve', lambda h: h.scalar_tensor_tensor(out=LAMS[:, 4:5], in0=LAMS[:, 3:4], scalar=-lambda_init,
                                                          in1=LAMS[:, 2:3], op0=ALU.add, op1=ALU.subtract),
                  reads=[('LAMS', 1)], writes=[('LAMS', 2)])
            neglam = LAMS[:, 4:5]

            S_PS_keys = [('PS', i) for i in range(4)]
            def s_ps(j):
                return PS[j][:, :]
            PT_PS = [PS[4][:, :].bitcast(BF16), PS[5][:, :].bitcast(BF16)]
            O_PS = PS[6]
            OT_PS = PS[7][:, :].bitcast(BF16)

            units = []
            for hh in range(8):
                units.append(dict(q=hh, kj=hh // 4, kr=(hh % 4) * 128, vj=hh // 4, vc=(hh % 4) * 128, vw=128,
                                  bias=('A', hh), sink=None, e=hh, kind='n'))
            for hq in range(8):
                kv = hq // 4
                units.append(dict(q=8 + hq, kj=2, kr=kv * 128, vj=2, vc=kv * 128, vw=128,
                                  bias=None, sink=None, e=8 + hq, kind='n'))
            for hq in range(8):
                kv = hq // 4
                units.append(dict(q=16 + hq, kj=2, kr=256 + kv * 128, vj=2, vc=256 + kv * 128, vw=128,
                                  bias=('C', hq), sink=hq, e=16 + hq, kind='n'))
            for hh in range(4):
                units.append(dict(q=24 + hh, kj=3, kr=hh * 128, vj=3 + hh // 2, vc=(hh % 2) * 256, vw=256,
                                  bias=('D', hh), sink=None, e=24 + 2 * hh, kind='d1'))
                units.append(dict(q=28 + hh, kj=4, kr=hh * 128, vj=3 + hh // 2, vc=(hh % 2) * 256, vw=256,
                                  bias=('D', hh), sink=None, e=24 + 2 * hh, kind='d2'))

            cur_v = [None, None]
            v_rr = [0]
            it = [0]
            for ui, u in enumerate(units):
                ub = ui % 2
                kt = KTt[ub]
                for r in range(2):
                    sc.dma('sp', lambda h, r=r, kt=kt, u=u: h.dma_start(
                        out=kt[:, r * 1024:(r + 1) * 1024],
                        in_=kgat[u['kj']][r * 512 + u['kr']:r * 512 + u['kr'] + 128, :]),
                        writes=[('KT', ub, r)])
                qt = QTt[ub]
                sc.dma('sp', lambda h, qt=qt, u=u: h.dma_start(out=qt, in_=qbuf[u['q'] * 128:(u['q'] + 1) * 128, :]),
                       writes=[('QT', ub)])
                if u['vj'] in cur_v:
                    vb = cur_v.index(u['vj'])
                else:
                    vb = v_rr[0] % 2
                    v_rr[0] += 1
                    cur_v[vb] = u['vj']
                    sc.dma('sp', lambda h, vb=vb, u=u: h.dma_start(
                        out=VT[vb], in_=vgat[u['vj']].ap().rearrange("(k p) c -> p k c", p=128)),
                        writes=[('VT', vb)])
                vt = VT[vb]
                vw = u['vw']
                for qt_i in range(NT):
                    b2 = it[0] % 2
                    it[0] += 1
                    qs = slice(qt_i * 128, (qt_i + 1) * 128)
                    if u['bias'] is not None:
                        kind, hh = u['bias']
                        if kind == 'A':
                            bsrc = biasA[fz(l), fz(hh), qs, :]
                        elif kind == 'C':
                            bsrc = biasC[fz(hh), qs, :]
                        else:
                            bsrc = biasD[fz(hh), qs, :]
                        sc.dma('sp', lambda h, bsrc=bsrc, b2=b2: h.dma_start(out=BIAS[b2], in_=bsrc),
                               writes=[('BIAS', b2)])
                    def qk(h, qt=qt, kt=kt, qs=qs):
                        inst = None
                        for j in range(4):
                            inst = h.matmul(s_ps(j), lhsT=qt[:, qs], rhs=kt[:, j * 512:(j + 1) * 512],
                                            start=True, stop=True)
                        return inst
                    sc.op('pe', qk, reads=[('QT', ub), ('KT', ub, 0), ('KT', ub, 1)], writes=S_PS_keys)
                    m, mk = sm()
                    nm, nmk = sm()
                    rs, rsk = sm()
                    pt_ = Pt[b2]
                    if u['bias'] is not None:
                        ssb = S_SB[b2]
                        for j in range(4):
                            sc.op('dve', lambda h, j=j, ssb=ssb, b2=b2: h.scalar_tensor_tensor(
                                out=ssb[:, j * 512:(j + 1) * 512], in0=s_ps(j), scalar=SCALE,
                                in1=BIAS[b2][:, j * 512:(j + 1) * 512], op0=ALU.mult, op1=ALU.add),
                                reads=[('PS', j), ('BIAS', b2)], writes=[('S_SB', b2, j)])
                        sc.op('dve', lambda h, ssb=ssb, m=m: h.reduce_max(out=m, in_=ssb[:, 0:S], axis=AX.X),
                              reads=[('S_SB', b2, j) for j in range(4)], writes=[mk])
                        if u['sink'] is not None:
                            m2, m2k = sm()
                            sk = SINK[:, u['sink']:u['sink'] + 1]
                            sc.op('dve', lambda h, m=m, m2=m2, sk=sk: h.tensor_tensor(out=m2, in0=m, in1=sk,
                                                                                     op=ALU.max),
                                  reads=[mk, ('SINK',)], writes=[m2k])
                            m, mk = m2, m2k
                        sc.op('dve', lambda h, m=m, nm=nm: h.tensor_scalar(out=nm, in0=m, scalar1=-1.0, scalar2=None,
                                                                         op0=ALU.mult),
                              reads=[mk], writes=[nmk])
                        sc.op('act', lambda h, ssb=ssb, pt_=pt_, nm=nm, rs=rs: h.activation(
                            out=pt_, in_=ssb[:, 0:S], func=ACTF.Exp, bias=nm, scale=1.0, accum_out=rs),
                            reads=[('S_SB', b2, j) for j in range(4)] + [nmk], writes=[('P', b2), rsk])
                    else:
                        mcols = [sm() for _ in range(4)]
                        for j in range(4):
                            sc.op('dve', lambda h, j=j, mc=mcols[j][0]: h.reduce_max(out=mc, in_=s_ps(j), axis=AX.X),
                                  reads=[('PS', j)], writes=[mcols[j][1]])
                        ma, mak = sm()
                        mb, mbk = sm()
                        sc.op('dve', lambda h, ma=ma, a0=mcols[0][0], a1=mcols[1][0]: h.tensor_tensor(out=ma, in0=a0, in1=a1,
                                                                      op=ALU.max),
                              reads=[mcols[0][1], mcols[1][1]], writes=[mak])
                        sc.op('dve', lambda h, mb=mb, a0=mcols[2][0], a1=mcols[3][0]: h.tensor_tensor(out=mb, in0=a0, in1=a1,
                                                                      op=ALU.max),
                              reads=[mcols[2][1], mcols[3][1]], writes=[mbk])
                        sc.op('dve', lambda h, m=m, ma=ma, mb=mb: h.tensor_tensor(out=m, in0=ma, in1=mb, op=ALU.max),
                              reads=[mak, mbk], writes=[mk])
                        sc.op('dve', lambda h, m=m, nm=nm: h.tensor_scalar(out=nm, in0=m, scalar1=-SCALE,
                                                                         scalar2=None, op0=ALU.mult),
                              reads=[mk], writes=[nmk])
                        rparts = [sm() for _ in range(4)]
                        for j in range(4):
                            sc.op('act', lambda h, j=j, pt_=pt_, nm=nm, rp=rparts[j][0]: h.activation(
                                out=pt_[:, j * 512:(j + 1) * 512], in_=s_ps(j), func=ACTF.Exp, bias=nm, scale=SCALE,
                                accum_out=rp),
                                reads=[('PS', j), nmk], writes=[('P', b2, j), rparts[j][1]])
                        ra, rak = sm()
                        rb, rbk = sm()
                        sc.op('dve', lambda h, ra=ra, a0=rparts[0][0], a1=rparts[1][0]: h.tensor_tensor(out=ra, in0=a0, in1=a1,
                                                                      op=ALU.add),
                              reads=[rparts[0][1], rparts[1][1]], writes=[rak])
                        sc.op('dve', lambda h, rb=rb, a0=rparts[2][0], a1=rparts[3][0]: h.tensor_tensor(out=rb, in0=a0, in1=a1,
                                                                      op=ALU.add),
                              reads=[rparts[2][1], rparts[3][1]], writes=[rbk])
                        sc.op('dve', lambda h, rs=rs, ra=ra, rb=rb: h.tensor_tensor(out=rs, in0=ra, in1=rb,
                                                                                    op=ALU.add),
                              reads=[rak, rbk], writes=[rsk])
                    pkeys = [('P', b2)] + [('P', b2, j) for j in range(4)]
                    if u['sink'] is not None:
                        e1, e1k = sm()
                        sk = SINK[:, u['sink']:u['sink'] + 1]
                        sc.op('act', lambda h, e1=e1, sk=sk, nm=nm: h.activation(out=e1, in_=sk, func=ACTF.Exp,
                                                                              bias=nm, scale=1.0),
                              reads=[('SINK',), nmk], writes=[e1k])
                        rs2, rs2k = sm()
                        sc.op('dve', lambda h, rs2=rs2, rs=rs, e1=e1: h.tensor_tensor(out=rs2, in0=rs, in1=e1,
                                                                                      op=ALU.add),
                              reads=[rsk, e1k], writes=[rs2k])
                        rs, rsk = rs2, rs2k
                    ri, rik = sm()
                    sc.op('dve', lambda h, ri=ri, rs=rs: h.reciprocal(out=ri, in_=rs), reads=[rsk], writes=[rik])
                    ptt = PTt[b2]
                    for g in range(2):
                        def tr(h, g=g, pt_=pt_):
                            inst = None
                            for j in range(8):
                                k_ = g * 8 + j
                                inst = h.transpose(out=PT_PS[g][:, j * 128:(j + 1) * 128],
                                                   in_=pt_[:, k_ * 128:(k_ + 1) * 128], identity=ident[:, :])
                            return inst
                        sc.op('pe', tr, reads=pkeys + [('ident',)], writes=[('PS', 4 + g)])
                        if g == 0:
                            sc.op('act', lambda h, g=g, ptt=ptt: h.activation(
                                out=ptt[:, g * 1024:(g + 1) * 1024], in_=PT_PS[g], func=ACTF.Copy),
                                reads=[('PS', 4 + g)], writes=[('PT', b2, g)])
                        else:
                            sc.op('dve', lambda h, g=g, ptt=ptt: h.tensor_copy(
                                out=ptt[:, g * 1024:(g + 1) * 1024], in_=PT_PS[g]),
                                reads=[('PS', 4 + g)], writes=[('PT', b2, g)])
                    def pv(h, ptt=ptt, vt=vt, u=u, vw=vw):
                        inst = None
                        for k_ in range(KT):
                            inst = h.matmul(O_PS[:, 0:vw], lhsT=ptt[:, k_ * 128:(k_ + 1) * 128],
                                            rhs=vt[:, k_, u['vc']:u['vc'] + vw],
                                            start=(k_ == 0), stop=(k_ == KT - 1))
                        return inst
                    sc.op('pe', pv, reads=[('PT', b2, 0), ('PT', b2, 1), ('VT', vb)], writes=[('PS', 6)])
                    if u['kind'] == 'n':
                        onb = ONB[b2]
                        sc.op('dve', lambda h, onb=onb, ri=ri: h.tensor_scalar(out=onb[:, :], in0=O_PS[:, 0:128],
                                                                              scalar1=ri, scalar2=None, op0=ALU.mult),
                              reads=[('PS', 6), rik], writes=[('ONB', b2)])
                        sc.op('pe', lambda h, onb=onb: h.transpose(out=OT_PS[:, 0:128], in_=onb[:, :],
                                                                   identity=ident[:, :]),
                              reads=[('ONB', b2), ('ident',)], writes=[('PS', 7)])
                        sc.op('dve', lambda h, u=u, qs=qs: h.tensor_copy(out=mixT[:, u['e'], qs], in_=OT_PS[:, 0:128]),
                              reads=[('PS', 7)], writes=[('mixT', u['e'], qt_i)])
                    elif u['kind'] == 'd1':
                        sc.op('dve', lambda h, ri=ri: h.tensor_scalar(out=O1N, in0=O_PS[:, 0:256], scalar1=ri,
                                                                      scalar2=None, op0=ALU.mult),
                              reads=[('PS', 6), rik], writes=[('O1N',)])
                        sc.dma('sp', lambda h, qs=qs: h.dma_start(out=ybuf[qs, 0:256], in_=O1N),
                               reads=[('O1N',)], writes=[('yb1', qt_i)])
                    else:
                        sc.dma('sp', lambda h, qs=qs: h.dma_start(out=O1N, in_=ybuf[qs, 0:256]),
                               reads=[('yb1', qt_i)], writes=[('O1N',)])
                        c2, c2k = sm()
                        sc.op('dve', lambda h, c2=c2, ri=ri: h.tensor_tensor(out=c2, in0=ri, in1=neglam, op=ALU.mult),
                              reads=[rik, ('LAMS', 2)], writes=[c2k])
                        sc.op('dve', lambda h, c2=c2: h.scalar_tensor_tensor(out=OD, in0=O_PS[:, 0:256],
                                                                             scalar=c2, in1=O1N,
                                                                             op0=ALU.mult, op1=ALU.add),
                              reads=[('PS', 6), c2k, ('O1N',)], writes=[('OD',)])
                        ss, ssk = sm()
                        sc.op('act', lambda h, ss=ss: h.activation(out=ODB[:, :], in_=OD, func=ACTF.Square,
                                                                   accum_out=ss),
                              reads=[('OD',)], writes=[('ODB',), ssk])
                        a_, ak_ = sm()
                        sc.op('act', lambda h, a_=a_, ss=ss: h.activation(out=a_, in_=ss, func=ACTF.Sqrt, bias=EPS,
                                                                         scale=1.0 / 256),
                              reads=[ssk], writes=[ak_])
                        r_, rk_ = sm()
                        sc.op('dve', lambda h, r_=r_, a_=a_: h.reciprocal(out=r_, in_=a_), reads=[ak_], writes=[rk_])
                        r3, r3k = sm()
                        sc.op('dve', lambda h, r3=r3, r_=r_: h.tensor_scalar(out=r3, in0=r_,
                                                                            scalar1=(1.0 - lambda_init),
                                                                            scalar2=None, op0=ALU.mult),
                              reads=[rk_], writes=[r3k])
                        sc.op('dve', lambda h, r3=r3: h.scalar_tensor_tensor(out=ODB[:, :], in0=OD, scalar=r3,
                                                                             in1=SUBG, op0=ALU.mult,
                                                                             op1=ALU.mult),
                              reads=[('OD',), r3k, ('SUBG',), ('ODB',)], writes=[('ODB',)])
                        def tr2(h):
                            h.transpose(out=OT_PS[:, 0:128], in_=ODB[:, 0:128], identity=ident[:, :])
                            return h.transpose(out=OT_PS[:, 128:256], in_=ODB[:, 128:256], identity=ident[:, :])
                        sc.op('pe', tr2, reads=[('ODB',), ('ident',)], writes=[('PS', 7)])
                        sc.op('dve', lambda h, u=u, qs=qs: h.tensor_copy(
                            out=mixT[:, u['e']:u['e'] + 2, qs],
                            in_=OT_PS[:, 0:256].rearrange("p (j t) -> p j t", j=2)),
                            reads=[('PS', 7)], writes=[('mixT', u['e'], qt_i)])
            sc.barrier()
            if debug and l == 0:
                sc.dma('sp', lambda h: h.dma_start(out=dbg_mix[:, :, :], in_=BIG[:, :, :]))
                sc.barrier()
            sc.emit()
        if stop_after == 'attn':
            break

        for db in range(8):
            src = w_out[fz(l)].rearrange("(kc p) n -> p kc n", p=128)[:, :, fz(db) * 512:(fz(db) + 1) * 512]
            wi, wv = load_w(src)
            for tt in range(NT):
                pi = (db * NT + tt) % 8
                def mm(h, pi=pi, tt=tt, wv=wv):
                    inst = None
                    for kc in range(KC):
                        inst = h.matmul(PS[pi][:, :], lhsT=mixT[:, kc, tt * 128:(tt + 1) * 128], rhs=wv[:, kc, :],
                                        start=(kc == 0), stop=(kc == KC - 1))
                    return inst
                sc.op('pe', mm, reads=[('W', wi), ('BIGALL',)], writes=[('PS', pi)])
                evac_y(pi, tt, db)
        load_gain(g_attn_post, l)
        for tt in range(NT):
            if os.environ.get("K_SKIP_POSTNORM"):
                break
            postnorm_tile(slice(tt * 128, (tt + 1) * 128), xsrc, xa, 0, 1, tt)
        sc.barrier()
        sc.emit()
        if stop_after == 'wout':
            break

        uT = BIG[:, :, :].rearrange("p k t -> p (k t)").rearrange("p (f t) -> p f t", t=256)
        w0f = W[0][:, :].bitcast(F32)
        SG = [w0f[:, 0:4096], w0f[:, 4096:8192]]
        WU = [W[1][:, k * 4096:(k + 1) * 4096] for k in range(4)]
        WU += [W[0][:, k * 4096:(k + 1) * 4096] for k in range(4)]
        ucnt = [0]

        def load_unit(src3, a, b, uid, tb):
            n = ucnt[0]
            ucnt[0] += 1
            si_, k = n % 2, (n % 4 if tb == 0 else n % 8)
            if tb == 0:
                sc.dma('sp', lambda h: h.dma_start(out=SG[si_].rearrange("p (a b) -> p a b", a=a), in_=src3),
                       writes=[('SG', si_)])
                if n % 2 == 0:
                    sc.op('act', lambda h: h.activation(out=WU[k], in_=SG[si_], func=ACTF.Copy),
                          reads=[('SG', si_)], writes=[('WU', k)])
                else:
                    sc.op('dve', lambda h: h.tensor_copy(out=WU[k], in_=SG[si_]),
                          reads=[('SG', si_)], writes=[('WU', k)])
                sc.dma('pool', lambda h: h.dma_start(out=wsc[uid], in_=WU[k]),
                       reads=[('WU', k)], writes=[('wsc', uid)])
            else:
                sc.dma('sp', lambda h: h.dma_start(out=WU[k], in_=wsc[uid]), reads=[('wsc', uid)],
                       writes=[('WU', k)] + ([('SG', (k - 4) // 2)] if k >= 4 else []))
            return k, WU[k].rearrange("p (a b) -> p a b", a=a)

        auxq[0] = 'pool'
        for tb in range(4):
            load_gain(g_mlp_pre, l)
            for i in range(2):
                tt = tb * 2 + i
                prenorm_tile(xa[tt * 128:(tt + 1) * 128, :], i,
                             lambda kc0, i=i: H2[:, kc0:kc0 + 8, i * 128:(i + 1) * 128], ('H2', i), psbase=i * 2)
            h2keys = [('H2', 0), ('H2', 1)]
            w1v = w1[fz(l)].rearrange("(kc p) n -> p kc n", p=128)
            for fb in range(32):
                pbase = 4 if fb % 2 == 0 else 0
                for q in range(4):
                    src = w1v[:, q * 8:(q + 1) * 8, fz(fb) * 512:(fz(fb) + 1) * 512]
                    k, wu = load_unit(src, 8, 512, fb * 4 + q, tb)
                    for c in range(4):
                        pi = pbase + c
                        def mm(h, pi=pi, c=c, wu=wu, q=q):
                            inst = None
                            for kk in range(8):
                                inst = h.matmul(PS[pi][:, 0:256], lhsT=wu[:, kk, c * 128:(c + 1) * 128],
                                                rhs=H2[:, q * 8 + kk, :],
                                                start=(q == 0 and kk == 0), stop=(q == 3 and kk == 7))
                            return inst
                        sc.op('pe', mm, reads=[('WU', k)] + h2keys, writes=[('PS', pi)])
                for c in range(4):
                    pi = pbase + c
                    fc = fb * 4 + c
                    si = next_stg()
                    sc.op('act', lambda h, pi=pi, si=si: h.activation(out=STG[si][:, 0:256], in_=PS[pi][:, 0:256],
                                                                      func=ACTF.Relu),
                          reads=[('PS', pi)], writes=[('STG', si)])
                    sc.op('pool', lambda h, si=si, fc=fc: h.tensor_tensor(out=uT[:, fc, :], in0=STG[si][:, 0:256],
                                                                          in1=STG[si][:, 0:256], op=ALU.mult),
                          reads=[('STG', si)], writes=[('uT', fc)])
            w2v = w2[fz(l)].rearrange("(fc p) n -> p fc n", p=128)
            for db in range(8):
                for j8 in range(16):
                    src = w2v[:, fz(j8) * 8:(fz(j8) + 1) * 8, fz(db) * 512:(fz(db) + 1) * 512]
                    k, wu = load_unit(src, 8, 512, 128 + db * 16 + j8, tb)
                    for i in range(2):
                        pi = (db % 2) * 2 + i
                        def mm(h, pi=pi, i=i, j8=j8, wu=wu):
                            inst = None
                            for k_ in range(8):
                                fc = j8 * 8 + k_
                                inst = h.matmul(PS[pi][:, :], lhsT=uT[:, fc, i * 128:(i + 1) * 128], rhs=wu[:, k_, :],
                                                start=(fc == 0), stop=(fc == 127))
                            return inst
                        sc.op('pe', mm, reads=[('WU', k)] + [('uT', fc) for fc in range(j8 * 8, j8 * 8 + 8)],
                              writes=[('PS', pi)])
                for i in range(2):
                    evac_y((db % 2) * 2 + i, tb * 2 + i, db)
            load_gain(g_mlp_post, l)
            for i in range(2):
                tt = tb * 2 + i
                postnorm_tile(slice(tt * 128, (tt + 1) * 128), xa, xfinal, 0, 1, tt)
        sc.barrier()
        sc.emit()
        auxq[0] = 'sp'
    sc.barrier()
    sc.emit()


def _t5_bucket_np(rel):
    import jax
    import jax.numpy as jnp
    cpu = jax.devices("cpu")[0]
    with jax.default_device(cpu):
        rel = jnp.asarray(rel, dtype=jnp.int32)
        nb = 16
        max_exact = 8
        base = jnp.where(rel > 0, nb, 0)
        n = jnp.abs(rel)
        n_f = jnp.maximum(n, 1).astype(jnp.float32)
        large = max_exact + (jnp.log(n_f / max_exact) / math.log(128 / max_exact) * (nb - max_exact)).astype(jnp.int32)
        large = jnp.minimum(large, nb - 1)
        return np.asarray(base + jnp.where(n < max_exact, n, large))


_CONST = {}


def _consts():
    if _CONST:
        return _CONST
    rel1d = np.arange(-(S - 1), S, dtype=np.int32)
    bucket1d = _t5_bucket_np(rel1d)
    per_half = []
    for hf in range(2):
        qpos = hf * NTOK + np.arange(NTOK)
        kpos = np.arange(S)
        rel = kpos[None, :] - qpos[:, None]
        bidx = bucket1d[rel + S - 1]
        cmask = np.abs(rel) <= 128
        r = qpos // 64
        c = qpos % 64
        rs = np.clip(r - 4, 0, 32 - 8)
        cs = np.clip(c - 8, 0, 64 - 16)
        kr = kpos // 64
        kc = kpos % 64
        amask = (kr[None, :] >= rs[:, None]) & (kr[None, :] < rs[:, None] + 8) & \
                (kc[None, :] >= cs[:, None]) & (kc[None, :] < cs[:, None] + 16)
        dr = np.clip(kr[None, :] - r[:, None] + 7, 0, 14)
        dc = np.clip(kc[None, :] - c[:, None] + 15, 0, 30)
        row = (qpos // 64).astype(np.float32)
        col = (qpos % 64).astype(np.float32)
        inv = (10000.0 ** (-np.arange(32, dtype=np.float32) / 32)).astype(np.float32)
        ang = np.concatenate([row[:, None] * inv, col[:, None] * inv], axis=-1).astype(np.float32)
        per_half.append(dict(bidx=bidx, cmask=cmask, amask=amask, dr=dr, dc=dc,
                             cos=np.cos(ang).astype(np.float32), sin=np.sin(ang).astype(np.float32)))
    _CONST['h'] = per_half
    _CONST['ident'] = np.eye(128, dtype=np.float32).astype(ml_dtypes.bfloat16)
    return _CONST


def make_in_maps(inputs):
    cst = _consts()
    f = lambda a: np.ascontiguousarray(np.asarray(a, dtype=np.float32))
    x = f(inputs["x"])
    shared = {k: f(inputs[k]) for k in ["ln_attn_pre", "ln_attn_post", "ln_mlp_pre", "ln_mlp_post", "w_in", "w_out",
                                        "w_mlp_in", "w_mlp_out", "ax_q_norm", "ax_k_norm", "sw_sink", "df_subln"]}
    shared["df_lambda"] = f(inputs["df_lambda"]).reshape(2, 512)
    shared["ident"] = cst['ident']
    rpb = f(inputs["na_rpb"])
    t5 = f(inputs["t5_table"])
    halves = []
    for hf in range(2):
        c = cst['h'][hf]
        bA = np.empty((2, 8, NTOK, S), np.float32)
        for l in range(2):
            for h in range(8):
                bA[l, h] = np.where(c['amask'], rpb[l, h][c['dr'], c['dc']], np.float32(NEG))
        bC = np.empty((8, NTOK, S), np.float32)
        for h in range(8):
            bC[h] = np.where(c['cmask'], t5[:, h][c['bidx']], np.float32(NEG))
        bD = np.empty((4, NTOK, S), np.float32)
        for h in range(4):
            bD[h] = t5[:, 8 + h][c['bidx']]
        halves.append(dict(biasA=bA, biasC=bC, biasD=bD, rope_cos=c['cos'], rope_sin=c['sin']))
    in_maps = []
    for core in range(8):
        b, hf = core // 2, core % 2
        m = dict(shared)
        m.update(halves[hf])
        m["x"] = np.ascontiguousarray(x[b, hf * NTOK:(hf + 1) * NTOK, :])
        in_maps.append(m)
    return in_maps


_NC = {}


def kernel(**inputs):
    if 'nc' not in _NC:
        _NC['nc'] = build_program()
    nc = _NC['nc']
    in_maps = make_in_maps(inputs)
    res = run_bass_kernel_spmd(nc, in_maps, core_ids=list(range(8)))
    outp = np.empty((4, S, D), np.float32)
    for core in range(8):
        b, hf = core // 2, core % 2
        outp[b, hf * NTOK:(hf + 1) * NTOK, :] = res.results[core]["out"]
    return outp
```
